# Optimizing a Trainium2 kernel written in Bass

```python
import jax, jax.numpy as jnp
from jax import lax
import numpy as np

D_MODEL = 1024
BATCH = 8
SEQ = 4096
DEPTH = 2

GRID_W = 64
CTX_LEN = 256
HEAD_DIM = 64
NA_WIDTH = D_MODEL // 2
NA_HEADS = NA_WIDTH // HEAD_DIM
NA_ROWS_MAX = 8
NA_COLS = 16
NA_QB = 16
NA_KB = NA_QB + NA_COLS
RET_V_WIDTH = D_MODEL // 2
RET_HEADS = 4
RET_DV = RET_V_WIDTH // RET_HEADS
RET_DK = RET_DV // 2
RET_QK_WIDTH = RET_HEADS * RET_DK
RET_CHUNK = 128
RET_LOG2_DECAY_MIN = -5.0
RET_LOG2_DECAY_MAX = -12.0
POOL_WIDTH = D_MODEL // 2
POOL_WINDOWS = (2, 4, 8, 16)
POOL_GROUPS = 4
POOL_GROUP_WIDTH = POOL_WIDTH // POOL_GROUPS
FFN_HIDDEN = -(-8 * D_MODEL // (3 * 256)) * 256
N_MOD = 6
ROPE_BASE = 10000.0
NORM_EPS = 1e-6
GN_EPS = 1e-5
NEG_INF = -1e30
IN_WIDTHS = (NA_WIDTH, NA_WIDTH, NA_WIDTH, RET_QK_WIDTH, RET_QK_WIDTH, RET_V_WIDTH, RET_V_WIDTH,
             POOL_WIDTH, D_MODEL, D_MODEL, D_MODEL)
IN_WIDTH = 3 * NA_WIDTH + 2 * RET_QK_WIDTH + 2 * RET_V_WIDTH + POOL_WIDTH + 3 * D_MODEL

kernel_name = "hybrid_na_retention_pool_dit"


def _rms_norm(x, g):
    xf = x.astype(jnp.float32)
    y = xf * lax.rsqrt(jnp.mean(xf * xf, axis=-1, keepdims=True) + NORM_EPS)
    return (y * g.astype(jnp.float32)).astype(x.dtype)


def _modulate(h, shift, scale):
    return h * (1.0 + scale) + shift


def _split_cols(u):
    out, start = [], 0
    for w in IN_WIDTHS:
        out.append(u[..., start:start + w])
        start += w
    return out


def _heads(t, n_heads, d):
    b, n, _ = t.shape
    return t.reshape(b, n, n_heads, d).transpose(0, 2, 1, 3)


def _merge_heads(t):
    b, h, n, d = t.shape
    return t.transpose(0, 2, 1, 3).reshape(b, n, h * d)


def _rope_1d(x, pos):
    nf = x.shape[-1] // 2
    inv = ROPE_BASE ** (-jnp.arange(nf, dtype=jnp.float32) / nf)
    ang = pos[:, None] * inv[None, :]
    cos, sin = jnp.cos(ang), jnp.sin(ang)
    x1, x2 = x[..., :nf], x[..., nf:]
    return jnp.concatenate([x1 * cos - x2 * sin, x1 * sin + x2 * cos], axis=-1)


def _axial_rope(x, row, col):
    half = x.shape[-1] // 2
    return jnp.concatenate([_rope_1d(x[..., :half], row), _rope_1d(x[..., half:], col)], axis=-1)


def _dense_attn(q, k, v):
    s = jnp.einsum('bhqd,bhkd->bhqk', q * HEAD_DIM ** -0.5, k).astype(jnp.float32)
    p = jax.nn.softmax(s, axis=-1).astype(v.dtype)
    return jnp.einsum('bhqk,bhkd->bhqd', p, v)


def _na_latent(q, k, v, k_ctx, v_ctx, rpb):
    b, n, _ = q.shape
    rows = n // GRID_W
    kr = min(NA_ROWS_MAX, rows)
    ncb = GRID_W // NA_QB

    def grid(t):
        return t.reshape(b, rows, GRID_W, NA_HEADS, HEAD_DIM).transpose(1, 0, 3, 2, 4)

    qg, kg, vg = grid(q * HEAD_DIM ** -0.5), grid(k), grid(v)
    qcol = np.arange(GRID_W).reshape(ncb, NA_QB)
    kstart = np.clip(qcol[:, 0] - NA_COLS // 2, 0, GRID_W - NA_KB)
    kcol = kstart[:, None] + np.arange(NA_KB)
    wstart = np.clip(qcol - NA_COLS // 2, 0, GRID_W - NA_COLS)
    col_ok = (kcol[:, None, :] >= wstart[..., None]) & (kcol[:, None, :] < wstart[..., None] + NA_COLS)
    dcol_idx = np.clip(kcol[:, None, :] - qcol[:, :, None] + NA_COLS - 1, 0, 2 * NA_COLS - 2)
    rpb_f = rpb.astype(jnp.float32)
    nw = kr * NA_KB

    def row_block(args):
        r, q_r = args
        r0 = jnp.clip(r - kr // 2, 0, rows - kr)
        k_r = lax.dynamic_slice_in_dim(kg, r0, kr, axis=0)
        v_r = lax.dynamic_slice_in_dim(vg, r0, kr, axis=0)
        kb = k_r[:, :, :, kcol].transpose(1, 2, 3, 0, 4, 5).reshape(b, NA_HEADS, ncb, nw, HEAD_DIM)
        vb = v_r[:, :, :, kcol].transpose(1, 2, 3, 0, 4, 5).reshape(b, NA_HEADS, ncb, nw, HEAD_DIM)
        qb = q_r.reshape(b, NA_HEADS, ncb, NA_QB, HEAD_DIM)
        s_win = jnp.einsum('bhnqd,bhnkd->bhnqk', qb, kb).astype(jnp.float32)
        s_win = s_win.reshape(b, NA_HEADS, ncb, NA_QB, kr, NA_KB)
        dr_idx = r0 + jnp.arange(kr) - r + NA_ROWS_MAX - 1
        bias = rpb_f[:, dr_idx][:, :, dcol_idx].transpose(0, 2, 3, 1, 4)
        s_win = jnp.where(col_ok[:, :, None, :], s_win + bias, NEG_INF).reshape(b, NA_HEADS, ncb, NA_QB, nw)
        s_ctx = jnp.einsum('bhnqd,bhkd->bhnqk', qb, k_ctx).astype(jnp.float32)
        p = jax.nn.softmax(jnp.concatenate([s_win, s_ctx], axis=-1), axis=-1).astype(v.dtype)
        o = (jnp.einsum('bhnqk,bhnkd->bhnqd', p[..., :nw], vb)
             + jnp.einsum('bhnqk,bhkd->bhnqd', p[..., nw:], v_ctx))
        return o.reshape(b, NA_HEADS, GRID_W, HEAD_DIM)

    out = lax.map(row_block, (jnp.arange(rows), qg))
    return out.transpose(1, 0, 3, 2, 4).reshape(b, n, NA_WIDTH)


def _retention_scan(q, k, v, log_g, state0, include_diag):
    b, h, n, dk = q.shape
    dv = v.shape[-1]
    nc = n // RET_CHUNK
    qc = q.reshape(b, h, nc, RET_CHUNK, dk)
    kc = k.reshape(b, h, nc, RET_CHUNK, dk)
    vc = v.reshape(b, h, nc, RET_CHUNK, dv)
    pos = jnp.arange(RET_CHUNK, dtype=jnp.float32)
    lg = log_g[:, None, None]
    diff = pos[:, None] - pos[None, :]
    mask = (diff >= 0) if include_diag else (diff > 0)
    decay = jnp.where(mask, jnp.exp(lg * jnp.where(mask, diff, 0.0)), 0.0)
    inner = jnp.einsum('bhnqd,bhnkd->bhnqk', qc, kc) * decay[:, None]
    o_in = jnp.einsum('bhnqk,bhnkv->bhnqv', inner, vc)
    zeta = jnp.exp(lg * (RET_CHUNK - 1.0 - pos))[..., None]
    contrib = jnp.einsum('bhnkd,bhnkv->nbhdv', kc * zeta, vc)
    chunk_decay = jnp.exp(log_g * RET_CHUNK)[:, None, None]

    def step(state, ctb):
        return state * chunk_decay + ctb, state

    final, prev = lax.scan(step, state0, contrib)
    xi = jnp.exp(lg * (pos + 1.0))[..., None]
    o_cross = jnp.einsum('bhnqd,nbhdv->bhnqv', qc, prev) * xi
    return (o_in + o_cross).reshape(b, h, n, dv), final


def _bi_retention(q, k, v, lg_f, lg_b, s0_f, s0_b):
    o_f, s_f = _retention_scan(q, k, v, lg_f, s0_f, True)
    flip = lambda t: t[:, :, ::-1]
    o_b, s_b = _retention_scan(flip(q), flip(k), flip(v), lg_b, s0_b, False)
    return o_f + flip(o_b), s_f, s_b


def _ret_out(o, g, gn_g):
    mu = jnp.mean(o, axis=-1, keepdims=True)
    var = jnp.mean(jnp.square(o - mu), axis=-1, keepdims=True)
    y = _merge_heads((o - mu) * lax.rsqrt(var + GN_EPS)) * gn_g.astype(jnp.float32)
    return (jax.nn.silu(g.astype(jnp.float32)) * y).astype(g.dtype)


def _pool_mixer(u, pool_w, pool_scale):
    b, n, _ = u.shape
    uf = u.astype(jnp.float32)
    cs = jnp.concatenate([jnp.zeros((b, 1, POOL_WIDTH), jnp.float32), jnp.cumsum(uf, axis=1)], axis=1)
    t = np.arange(n)
    groups = []
    for gi, w in enumerate(POOL_WINDOWS):
        lo = np.clip(t - w // 2, 0, n)
        hi = np.clip(t - w // 2 + w, 0, n)
        sl = slice(gi * POOL_GROUP_WIDTH, (gi + 1) * POOL_GROUP_WIDTH)
        csg = cs[..., sl]
        mean = (csg[:, hi] - csg[:, lo]) / jnp.asarray(hi - lo, jnp.float32)[None, :, None]
        groups.append(mean - uf[..., sl])
    pooled = jnp.stack(groups, axis=2).astype(u.dtype)
    y = jnp.einsum('bngc,gcd->bngd', pooled, pool_w).reshape(b, n, POOL_WIDTH)
    return y * pool_scale


def _merge(a, r, p, ga, gb, gc, wpa, wpb, wpc, wo):
    m = (jax.nn.sigmoid(ga) * (a @ wpa) + jax.nn.sigmoid(gb) * (r @ wpb)
         + jax.nn.sigmoid(gc) * (p @ wpc))
    return m @ wo


def _swiglu(h, wg, wu, wd):
    return (jax.nn.silu(h @ wg) * (h @ wu)) @ wd


def _token_mixers(hx, hc, w_in, rpb, logit_f, logit_b, gn_g, pool_w, pool_scale,
                  wpa, wpb, wpc, wo, with_ctx):
    b, n, _ = hx.shape
    ux = _split_cols(hx @ w_in)
    uc = _split_cols(hc @ w_in)
    f32 = jnp.float32
    k_na_c = _heads(uc[1], NA_HEADS, HEAD_DIM)
    v_na_c = _heads(uc[2], NA_HEADS, HEAD_DIM)
    a_x = _na_latent(ux[0], ux[1], ux[2], k_na_c, v_na_c, rpb)
    lg_f = jax.nn.log_sigmoid(logit_f.astype(f32))
    lg_b = jax.nn.log_sigmoid(logit_b.astype(f32))
    ksc = RET_DK ** -0.5
    q_rc = _heads(uc[3], RET_HEADS, RET_DK).astype(f32)
    k_rc = _heads(uc[4], RET_HEADS, RET_DK).astype(f32) * ksc
    v_rc = _heads(uc[5], RET_HEADS, RET_DV).astype(f32)
    zero = jnp.zeros((b, RET_HEADS, RET_DK, RET_DV), f32)
    o_rc, s_f, s_b = _bi_retention(q_rc, k_rc, v_rc, lg_f, lg_b, zero, zero)
    tpos = jnp.arange(n)
    row = (tpos // GRID_W).astype(f32)
    col = (tpos % GRID_W).astype(f32)
    q_rx = _axial_rope(_heads(ux[3], RET_HEADS, RET_DK).astype(f32), row, col)
    k_rx = _axial_rope(_heads(ux[4], RET_HEADS, RET_DK).astype(f32), row, col) * ksc
    v_rx = _heads(ux[5], RET_HEADS, RET_DV).astype(f32)
    o_rx, _, _ = _bi_retention(q_rx, k_rx, v_rx, lg_f, lg_b, s_f, s_b)
    r_x = _ret_out(o_rx, ux[6], gn_g)
    p_x = _pool_mixer(ux[7], pool_w, pool_scale)
    mix_x = _merge(a_x, r_x, p_x, ux[8], ux[9], ux[10], wpa, wpb, wpc, wo)
    if not with_ctx:
        return mix_x, None
    a_c = _merge_heads(_dense_attn(_heads(uc[0], NA_HEADS, HEAD_DIM), k_na_c, v_na_c))
    r_c = _ret_out(o_rc, uc[6], gn_g)
    p_c = _pool_mixer(uc[7], pool_w, pool_scale)
    mix_c = _merge(a_c, r_c, p_c, uc[8], uc[9], uc[10], wpa, wpb, wpc, wo)
    return mix_x, mix_c


def setup_inputs(seed: int = 0) -> dict:
    key = jax.random.key(seed)
    ks = jax.random.split(key, 24)
    f32 = jnp.float32
    nrm = lambda k, shape, s: jax.random.normal(k, shape, f32) * s
    g0 = 1.0 - 2.0 ** np.linspace(RET_LOG2_DECAY_MIN, RET_LOG2_DECAY_MAX, RET_HEADS)
    logit0 = jnp.asarray(np.log(g0) - np.log1p(-g0), f32)
    L = DEPTH
    return {
        "x": nrm(ks[0], (BATCH, SEQ, D_MODEL), 1.0),
        "c": nrm(ks[1], (BATCH, D_MODEL), 1.0),
        "ctx": nrm(ks[2], (BATCH, CTX_LEN, D_MODEL), 1.0),
        "c_ctx": nrm(ks[3], (D_MODEL,), 1.0),
        "norm1_g": 1.0 + nrm(ks[4], (L, D_MODEL), 0.02),
        "norm2_g": 1.0 + nrm(ks[5], (L, D_MODEL), 0.02),
        "w_ada": nrm(ks[6], (L, D_MODEL, N_MOD * D_MODEL), 0.5 * D_MODEL ** -0.5),
        "b_ada": nrm(ks[7], (L, N_MOD * D_MODEL), 0.02),
        "w_in": nrm(ks[8], (L, D_MODEL, IN_WIDTH), D_MODEL ** -0.5),
        "na_rpb": nrm(ks[9], (L, NA_HEADS, 2 * NA_ROWS_MAX - 1, 2 * NA_COLS - 1), 0.05),
        "ret_logit_f": logit0 + nrm(ks[10], (L, RET_HEADS), 0.1),
        "ret_logit_b": logit0 + nrm(ks[11], (L, RET_HEADS), 0.1),
        "ret_gn_g": 1.0 + nrm(ks[12], (L, RET_V_WIDTH), 0.02),
        "pool_w": nrm(ks[13], (L, POOL_GROUPS, POOL_GROUP_WIDTH, POOL_GROUP_WIDTH), POOL_GROUP_WIDTH ** -0.5),
        "pool_scale": 1.0 + nrm(ks[14], (L, POOL_WIDTH), 0.02),
        "w_branch_a": nrm(ks[15], (L, NA_WIDTH, D_MODEL), NA_WIDTH ** -0.5),
        "w_branch_b": nrm(ks[16], (L, RET_V_WIDTH, D_MODEL), RET_V_WIDTH ** -0.5),
        "w_branch_c": nrm(ks[17], (L, POOL_WIDTH, D_MODEL), POOL_WIDTH ** -0.5),
        "w_out": nrm(ks[18], (L, D_MODEL, D_MODEL), D_MODEL ** -0.5),
        "w_ffn_gate": nrm(ks[19], (L, D_MODEL, FFN_HIDDEN), D_MODEL ** -0.5),
        "w_ffn_up": nrm(ks[20], (L, D_MODEL, FFN_HIDDEN), D_MODEL ** -0.5),
        "w_ffn_down": nrm(ks[21], (L, FFN_HIDDEN, D_MODEL), FFN_HIDDEN ** -0.5),
        "final_norm_g": 1.0 + nrm(ks[22], (D_MODEL,), 0.02),
    }


def reference(x, c, ctx, c_ctx, norm1_g, norm2_g, w_ada, b_ada, w_in, na_rpb, ret_logit_f,
              ret_logit_b, ret_gn_g, pool_w, pool_scale, w_branch_a, w_branch_b, w_branch_c,
              w_out, w_ffn_gate, w_ffn_up, w_ffn_down, final_norm_g):
    b = x.shape[0]
    h, hc = x, ctx
    silu_c = jax.nn.silu(c)
    silu_cc = jax.nn.silu(c_ctx)
    for l in range(DEPTH):
        with_ctx = l < DEPTH - 1
        mx = (silu_c @ w_ada[l] + b_ada[l]).reshape(b, N_MOD, 1, D_MODEL)
        mc = (silu_cc @ w_ada[l] + b_ada[l]).reshape(N_MOD, 1, 1, D_MODEL)
        hx_n = _modulate(_rms_norm(h, norm1_g[l]), mx[:, 0], mx[:, 1])
        hc_n = _modulate(_rms_norm(hc, norm1_g[l]), mc[0], mc[1])
        mix_x, mix_c = _token_mixers(hx_n, hc_n, w_in[l], na_rpb[l], ret_logit_f[l], ret_logit_b[l],
                                     ret_gn_g[l], pool_w[l], pool_scale[l], w_branch_a[l],
                                     w_branch_b[l], w_branch_c[l], w_out[l], with_ctx)
        h = h + mx[:, 2] * mix_x
        f_x = _modulate(_rms_norm(h, norm2_g[l]), mx[:, 3], mx[:, 4])
        h = h + mx[:, 5] * _swiglu(f_x, w_ffn_gate[l], w_ffn_up[l], w_ffn_down[l])
        if with_ctx:
            hc = hc + mc[2] * mix_c
            f_c = _modulate(_rms_norm(hc, norm2_g[l]), mc[3], mc[4])
            hc = hc + mc[5] * _swiglu(f_c, w_ffn_gate[l], w_ffn_up[l], w_ffn_down[l])
    return _rms_norm(h, final_norm_g)
```

```python
import contextlib
import numpy as np
import concourse.bass as bass
import concourse.mybir as mybir
from concourse.bass_utils import run_bass_kernel_spmd

F32 = mybir.dt.float32
BF16 = mybir.dt.bfloat16
AF = mybir.ActivationFunctionType
ALU = mybir.AluOpType

ENGS = ("pe", "act", "dve", "pool", "sp")
ENGN = {"pe": "tensor", "act": "scalar", "dve": "vector", "pool": "gpsimd", "sp": "sync"}
SEM_CAP = 30000
DMA_ROT = 8
NSEM = 96

D = 1024
SEQ = 4096
CTX = 256
TOK = SEQ + CTX
DEPTH = 2
HID = 2816
NHC = HID // 128
NORM_EPS = 1e-6
GN_EPS = 1e-5
NEGM = -30000.0
TILES = [(i * 512, 512, 0) for i in range(8)] + [(SEQ, CTX, 1)]
DEBUG_OUTS = ()


class Res:
    __slots__ = ("w", "r")

    def __init__(self):
        self.w = None
        self.r = {}


class Prog:
    def __init__(self, nc, sems):
        self.nc = nc
        self.sems = sems
        self.semmap = {}
        self.q = {e: [] for e in ENGS}
        self.stream = {}
        self.waited = {e: {} for e in ENGS}
        self.dma_rot = {e: 0 for e in ENGS}
        self.out_events = []

    def _sem(self, key):
        if key not in self.semmap:
            self.semmap[key] = self.sems[len(self.semmap)]
        return self.semmap[key]

    def _next_event(self, stream, inc):
        st = self.stream.setdefault(stream, [0, 0])
        if st[1] + inc > SEM_CAP:
            st[0] += 1
            st[1] = 0
        st[1] += inc
        return ((stream, st[0]), st[1])

    def _peek_prev(self, stream):
        st = self.stream.get(stream)
        if st is None or st[1] == 0:
            return None
        return ((stream, st[0]), st[1])

    def _waits(self, eng, reads, writes, extra=()):
        need = {}

        def add(ev):
            if ev is None:
                return
            k, v = ev
            if need.get(k, 0) < v:
                need[k] = v
        for r in reads:
            add(r.w)
        for w in writes:
            add(w.w)
            for k, v in w.r.items():
                add((k, v))
        for ev in extra:
            add(ev)
        wd = self.waited[eng]
        for k, v in need.items():
            if eng == "pe" and k[0] == "c:pe":
                continue
            if wd.get(k, 0) >= v:
                continue
            wd[k] = v
            self.q[eng].append(("wait", k, v))

    def _mark(self, ev, reads, writes):
        k, v = ev
        for r in reads:
            if r.r.get(k, 0) < v:
                r.r[k] = v
        for w in writes:
            w.w = ev
            w.r = {}

    def op(self, eng, fn, reads=(), writes=()):
        self._waits(eng, reads, writes)
        ev = self._next_event("c:" + eng, 1)
        self.q[eng].append(("op", fn, ev[0]))
        self._mark(ev, reads, writes)

    def group(self, eng, fns, reads=(), writes=()):
        self._waits(eng, reads, writes)
        ev = self._next_event("c:" + eng, 1)
        for f in fns[:-1]:
            self.q[eng].append(("op", f, None))
        self.q[eng].append(("op", fns[-1], ev[0]))
        self._mark(ev, reads, writes)

    def dma(self, eng, out, in_, reads=(), writes=(), is_output=False, **kw):
        j = self.dma_rot[eng]
        self.dma_rot[eng] = (j + 1) % DMA_ROT
        stream = "d:%s:%d" % (eng, j)
        prev = self._peek_prev(stream)
        self._waits(eng, reads, writes, extra=(prev,) if prev else ())
        ev = self._next_event(stream, 16)
        self.q[eng].append(("dma", out, in_, ev[0], kw))
        self._mark(ev, reads, writes)
        if is_output:
            self.out_events.append(ev)

    def barrier(self):
        for e in ENGS:
            for stream, st in self.stream.items():
                if st[1] == 0 or (e == "pe" and stream == "c:pe"):
                    continue
                key, v = (stream, st[0]), st[1]
                if self.waited[e].get(key, 0) < v:
                    self.waited[e][key] = v
                    self.q[e].append(("wait", key, v))

    def flush(self, final=False):
        self.barrier()
        if final:
            for k, v in self.out_events:
                if self.waited["sp"].get(k, 0) < v:
                    self.waited["sp"][k] = v
                    self.q["sp"].append(("wait", k, v))
        nc = self.nc
        with nc.Block() as block:
            for e in ENGS:
                ql = self.q[e]

                def run(engine, ql=ql):
                    for it in ql:
                        if it[0] == "wait":
                            engine.wait_ge(self._sem(it[1]), it[2])
                        elif it[0] == "op":
                            ins = it[1](engine)
                            if it[2] is not None:
                                ins.then_inc(self._sem(it[2]), 1)
                        else:
                            _, out, in_, k, kw = it
                            engine.dma_start(out=out, in_=in_, **kw).then_inc(self._sem(k), 16)
                getattr(block, ENGN[e])(run)
        self.q = {e: [] for e in ENGS}


class T:
    def __init__(self, h):
        self.h = h
        self.r = Res()

    def __getitem__(self, k):
        return self.h[k]


def build_program(depth=DEPTH, debug_outs=(), stop_after=None):
    nc = bass.Bass("TRN2", target_bir_lowering=False)

    def din(name, shape, dt=F32):
        return nc.dram_tensor(name, list(shape), dt, kind="ExternalInput").ap()

    dres = {}

    def dscr(name, shape, dt):
        kind = "ExternalOutput" if name in debug_outs else "Internal"
        ap = nc.dram_tensor(name, list(shape), dt, kind=kind).ap()
        dres[name] = Res()
        return ap

    x_in = din("x", [SEQ, D])
    ctx_in = din("ctx", [CTX, D])
    cvec_in = din("cvec", [128, 8, 2])
    w_ada = din("w_ada", [DEPTH, D, 6 * D])
    b_ada = din("b_ada", [DEPTH, 1, 6 * D])
    vecs_in = din("vecs", [128, DEPTH, 32])
    logit_in = din("logits", [128, DEPTH, 8])
    w_in = din("w_in", [DEPTH, D, 6656])
    pool_w = din("pool_w", [DEPTH, 4, 128, 128])
    w_ba = din("w_branch_a", [DEPTH, 512, D])
    w_bb = din("w_branch_b", [DEPTH, 512, D])
    w_bc = din("w_branch_c", [DEPTH, 512, D])
    w_out = din("w_out", [DEPTH, D, D])
    w_fg = din("w_ffn_gate", [DEPTH, D, HID])
    w_fu = din("w_ffn_up", [DEPTH, D, HID])
    w_fd = din("w_ffn_down", [DEPTH, HID, D])
    tl_in = din("tl", [DEPTH, 128, 8 * 22 * 64])
    ident_in = din("ident", [128, 128])
    amask_in = din("amask", [16, 1024])
    bmask_in = din("bmask", [16, 8, 512])
    ropeC_in = din("ropeC", [128, SEQ])
    ropeS_in = din("ropeS", [128, SEQ])
    rc_in = din("rconst", [128, 768])
    prc_in = din("prc", [4, SEQ])
    prcc_in = din("prcc", [4, CTX])
    out_ap = nc.dram_tensor("out", [SEQ, D], F32, kind="ExternalOutput").ap()
    r_out = Res()

    hT = dscr("hT", [8, 128, TOK], F32)
    QT = dscr("QT", [512, TOK], BF16)
    KT = dscr("KT", [512, TOK], BF16)
    VV = dscr("VV", [TOK, 512], BF16)
    RQT = dscr("RQT", [256, TOK], BF16)
    RKT = dscr("RKT", [256, TOK], BF16)
    KZ = dscr("KZ", [TOK, 512], BF16)
    RV = dscr("RV", [TOK, 512], BF16)
    GT = dscr("GT", [512, TOK], BF16)
    SG = dscr("SG", [3, D, TOK], BF16)
    PT = dscr("PT", [512, TOK], BF16)
    AT = dscr("AT", [512, TOK], BF16)
    RT = dscr("RT", [512, TOK], BF16)
    ACTT = dscr("ACTT", [NHC, 128, TOK], BF16)
    XNd = dscr("XNd", [8, 128, TOK], BF16)

    with contextlib.ExitStack() as top:
        sems = [top.enter_context(nc.semaphore("s%d" % i)) for i in range(NSEM)]
        P = Prog(nc, sems)

        uniq = [0]

        def sb(es, name, shape, dt):
            uniq[0] += 1
            return T(es.enter_context(nc.sbuf_tensor("%s_%d" % (name, uniq[0]), list(shape), dt)))

        def psum(es, name, shape, dt=F32):
            return T(es.enter_context(nc.psum_tensor(name, list(shape), dt)))

        ps = [psum(top, "ps%d" % i, [128, 512]) for i in range(7)]
        psb = psum(top, "psb", [128, 1024], BF16)
        ident = sb(top, "ident", [128, 128], F32)
        identb = sb(top, "identb", [128, 128], BF16)
        onesb = sb(top, "onesb", [128, 128], BF16)
        ones1 = sb(top, "ones1", [128, 64], F32)
        mods = sb(top, "mods", [128, DEPTH, 48, 2], F32)
        vecs = sb(top, "vecs", [128, DEPTH, 32], F32)
        gs = sb(top, "gs", [128, DEPTH, 2, 8, 2], F32)
        gfin = sb(top, "gfin", [128, 8], F32)
        XNh = [None]
        rcn = sb(top, "rcn", [128, 768], F32)
        DT = sb(top, "DT", [128, 4, 128], F32)
        ZT = sb(top, "ZT", [128, 4, 2, 64], F32)
        XI = sb(top, "XI", [128, 4, 128], F32)
        DEC = sb(top, "DEC", [128, 4, 128], F32)
        lg = sb(top, "lg", [128, 8], F32)

        psi = [0]

        def nps():
            psi[0] = (psi[0] + 1) % 7
            return ps[psi[0]]

        P.dma("sp", ident[:], ident_in, writes=[ident.r])
        P.dma("pool", identb[:], ident_in, writes=[identb.r])
        P.dma("sp", vecs[:], vecs_in, writes=[vecs.r])
        P.dma("sp", rcn[:], rc_in, writes=[rcn.r])
        P.op("dve", lambda e: e.memset(onesb[:], 1.0), writes=[onesb.r])
        P.op("dve", lambda e: e.memset(ones1[:], 1.0), writes=[ones1.r])
        epsD = sb(top, "epsD", [128, 2], F32)
        P.op("dve", lambda e: e.memset(epsD[:, 0:1], float(D * NORM_EPS)), writes=[epsD.r])
        P.op("dve", lambda e: e.memset(epsD[:, 1:2], float(GN_EPS)), writes=[epsD.r])

        def mod(l, j, kc, s):
            return mods[:, l, j * 8 + kc, s:s + 1]

        with contextlib.ExitStack() as es:
            cv = sb(es, "cv", [128, 8, 2], F32)
            sc = sb(es, "sc", [128, 8, 2], F32)
            ba = sb(es, "ba", [2, DEPTH, 6 * D], F32)
            mrow = sb(es, "mrow", [2, DEPTH, 6 * D], F32)
            wa = [sb(es, "wa%d" % i, [128, 8, 256], F32) for i in range(3)]
            xt = [sb(es, "xt%d" % i, [128, D], F32) for i in range(2)]
            hst = [sb(es, "hst%d" % i, [128, 8, 512], F32) for i in range(2)]
            tmpm = sb(es, "tmpm", [128, 8, 2], F32)
            P.dma("sp", cv[:], cvec_in, writes=[cv.r])
            P.dma("sp", ba[:], b_ada.rearrange("l o n -> o l n").broadcast_to([2, DEPTH, 6 * D]), writes=[ba.r])
            P.op("act", lambda e: e.activation(sc[:], cv[:], AF.Silu), reads=[cv.r], writes=[sc.r])
            pm = ps[0]

            NAG = 24

            def ada_group(l, cg):
                wav = w_ada[l].rearrange("(p k) n -> p k n", k=8)
                wt = wa[cg % 3]
                c0 = cg * 256
                hf = (cg % 2) * 256
                P.dma("sp", wt[:], wav[:, :, c0:c0 + 256], writes=[wt.r])
                P.group("pe", [lambda e, k=k: e.matmul(
                    pm[0:2, hf:hf + 256], sc[:, k, :], wt[:, k, :], start=(k == 0), stop=(k == 7)) for k in range(8)],
                    reads=[wt.r, sc.r], writes=[pm.r])
                P.op("dve", lambda e: e.tensor_copy(mrow[0:2, l, c0:c0 + 256], pm[0:2, hf:hf + 256]),
                     reads=[pm.r], writes=[mrow.r])

            for cg in range(NAG):
                ada_group(0, cg)
            pending = [(l, cg) for l in range(1, depth) for cg in range(NAG)]
            xb = [0]

            def xps():
                xb[0] = xb[0] % 6 + 1
                return ps[xb[0]]
            cnt = 0
            for ti, (t0, n, s) in enumerate(TILES):
                hs = hst[ti % 2]
                for sub in range(n // 128):
                    xx = xt[cnt % 2]
                    cnt += 1
                    src = x_in[t0 + sub * 128:t0 + (sub + 1) * 128, :] if s == 0 else \
                        ctx_in[sub * 128:(sub + 1) * 128, :]
                    P.dma("sp", xx[:], src, writes=[xx.r])
                    for half in range(2):
                        pt = xps()
                        for kk in range(4):
                            kc = half * 4 + kk
                            P.op("pe", lambda e, pt=pt, kk=kk, kc=kc, xx=xx: e.transpose(
                                pt[:, kk * 128:(kk + 1) * 128], xx[:, kc * 128:(kc + 1) * 128], ident[:]),
                                reads=[xx.r, ident.r], writes=[pt.r])
                        if half == 0:
                            P.op("dve", lambda e, pt=pt, hs=hs, half=half, sub=sub: e.tensor_copy(
                                hs[:, half * 4:half * 4 + 4, sub * 128:(sub + 1) * 128],
                                pt[:].rearrange("p (k t) -> p k t", k=4)), reads=[pt.r], writes=[hs.r])
                        else:
                            P.op("act", lambda e, pt=pt, hs=hs, half=half, sub=sub: e.activation(
                                hs[:, half * 4:half * 4 + 4, sub * 128:(sub + 1) * 128],
                                pt[:].rearrange("p (k t) -> p k t", k=4), AF.Identity), reads=[pt.r], writes=[hs.r])
                    if pending and cnt % 2 == 0:
                        ada_group(*pending.pop(0))
                P.dma("pool", hT[:, :, t0:t0 + n].rearrange("k p t -> p k t"), hs[:, :, 0:n],
                      reads=[hs.r], writes=[dres["hT"]])
            while pending:
                ada_group(*pending.pop(0))
            P.op("dve", lambda e: e.tensor_tensor(mrow[:], mrow[:], ba[:], op=ALU.add),
                 reads=[mrow.r, ba.r], writes=[mrow.r])
            fns = []
            for l in range(depth):
                for m in range(48):
                    col = (l * 48 + m) * 2
                    fns.append(lambda e, l=l, m=m, col=col: e.matmul(
                        pm[:, col:col + 2], mrow[0:2, l, m * 128:(m + 1) * 128], ident[0:2, 0:2],
                        start=True, stop=True))
            P.group("pe", fns, reads=[mrow.r, ident.r], writes=[pm.r])
            P.op("dve", lambda e: e.tensor_copy(
                mods[:, 0:depth].rearrange("p l m s -> p (l m s)"), pm[:, 0:depth * 96]),
                reads=[pm.r], writes=[mods.r])
            for l in range(depth):
                for w in range(2):
                    sc0 = 8 + 24 * w
                    P.op("dve", lambda e, l=l, sc0=sc0: e.tensor_scalar(
                        tmpm[:], mods[:, l, sc0:sc0 + 8, :], 1.0, 32.0, op0=ALU.add, op1=ALU.mult),
                        reads=[mods.r], writes=[tmpm.r])
                    for s in range(2):
                        P.op("dve", lambda e, l=l, w=w, s=s: e.tensor_tensor(
                            gs[:, l, w, :, s], tmpm[:, :, s], vecs[:, l, 8 * w:8 * w + 8], op=ALU.mult),
                            reads=[tmpm.r, vecs.r], writes=[gs.r])
            P.op("dve", lambda e: e.tensor_scalar(gfin[:], vecs[:, 0, 24:32], 32.0, None, op0=ALU.mult),
                 reads=[vecs.r], writes=[gfin.r])
            P.flush()

        def norm_phase(l, w, tiles, dump=False):
            with contextlib.ExitStack() as es:
                hb = [sb(es, "nh%d" % i, [128, 8, 512], F32) for i in range(2)]
                sqs = [sb(es, "nsq%d" % i, [128, 8, 512], BF16) for i in range(2)]
                rss = [sb(es, "nrs%d" % i, [128, 512], F32) for i in range(2)]
                tmp = [sb(es, "ntm%d" % i, [128, 512], F32) for i in range(4)]
                def stage_a(ti):
                    t0, n, s = tiles[ti]
                    ht, sq, rs = hb[ti % 2], sqs[ti % 2], rss[ti % 2]
                    P.dma("sp", ht[:, :, 0:n], hT[:, :, t0:t0 + n].rearrange("k p t -> p k t"),
                          reads=[dres["hT"]], writes=[ht.r])
                    P.op("act", lambda e: e.activation(sq[:, :, 0:n], ht[:, :, 0:n], AF.Square),
                         reads=[ht.r], writes=[sq.r])
                    pp = nps()
                    P.group("pe", [lambda e, k=k: e.matmul(
                        pp[:, 0:n], onesb[:], sq[:, k, 0:n], start=(k == 0), stop=(k == 7)) for k in range(8)],
                        reads=[onesb.r, sq.r], writes=[pp.r])
                    P.op("act", lambda e: e.activation(
                        rs[:, 0:n], pp[:, 0:n], AF.Sqrt, bias=epsD[:, 0:1]), reads=[pp.r, epsD.r], writes=[rs.r])
                    P.op("dve", lambda e: e.reciprocal(rs[:, 0:n], rs[:, 0:n]), reads=[rs.r], writes=[rs.r])

                def stage_b(ti):
                    t0, n, s = tiles[ti]
                    ht, rs = hb[ti % 2], rss[ti % 2]
                    for kc in range(8):
                        tm = tmp[kc % 4]
                        P.op("dve", lambda e, tm=tm, kc=kc: e.scalar_tensor_tensor(
                            tm[:, 0:n], ht[:, kc, 0:n], gs[:, l, w, kc, s:s + 1], rs[:, 0:n],
                            op0=ALU.mult, op1=ALU.mult), reads=[ht.r, gs.r, rs.r], writes=[tm.r])
                        P.op("act", lambda e, tm=tm, kc=kc: e.activation(
                            XNh[0][:, kc, t0:t0 + n], tm[:, 0:n], AF.Identity, bias=mod(l, 3 * w, kc, s)),
                            reads=[tm.r, mods.r], writes=[XNh[0].r])
                    if dump:
                        P.dma("act", XNd[:, :, t0:t0 + n].rearrange("k p t -> p k t"), XNh[0][:, :, t0:t0 + n],
                              reads=[XNh[0].r], writes=[dres["XNd"]])

                stage_a(0)
                for ti in range(len(tiles)):
                    if ti + 1 < len(tiles):
                        stage_a(ti + 1)
                    stage_b(ti)
                P.flush()

        def retention_tables(l):
            with contextlib.ExitStack() as es:
                lgt = sb(es, "lgt", [128, 8], F32)
                t1 = sb(es, "rt1", [128, 128], F32)
                t2 = sb(es, "rt2", [128, 128], F32)
                c128 = sb(es, "c128", [128, 128], F32)
                P.dma("sp", lgt[:], logit_in[:, l, :], writes=[lgt.r])
                P.op("dve", lambda e: e.memset(c128[:], 128.0), writes=[c128.r])
                P.op("act", lambda e: e.activation(lgt[:], lgt[:], AF.Exp, scale=-1.0), reads=[lgt.r], writes=[lgt.r])
                P.op("act", lambda e: e.activation(lgt[:], lgt[:], AF.Ln, bias=1.0), reads=[lgt.r], writes=[lgt.r])
                P.op("dve", lambda e: e.tensor_scalar(lg[:], lgt[:], -1.0, None, op0=ALU.mult),
                     reads=[lgt.r], writes=[lg.r])
                EF, MF, EB, MB = (rcn[:, 0:128], rcn[:, 128:256], rcn[:, 256:384], rcn[:, 384:512])
                ZEf, ZEb, XE = rcn[:, 512:576], rcn[:, 576:640], rcn[:, 640:768]
                for h in range(4):
                    lf = lg[:, h:h + 1]
                    lb = lg[:, 4 + h:5 + h]
                    P.op("act", lambda e, lf=lf: e.activation(t1[:], EF, AF.Exp, scale=lf),
                         reads=[lg.r, rcn.r], writes=[t1.r])
                    P.op("dve", lambda e: e.tensor_tensor(t1[:], t1[:], MF, op=ALU.mult),
                         reads=[t1.r, rcn.r], writes=[t1.r])
                    P.op("act", lambda e, lb=lb: e.activation(t2[:], EB, AF.Exp, scale=lb),
                         reads=[lg.r, rcn.r], writes=[t2.r])
                    P.op("dve", lambda e: e.tensor_tensor(t2[:], t2[:], MB, op=ALU.mult),
                         reads=[t2.r, rcn.r], writes=[t2.r])
                    P.op("dve", lambda e, h=h: e.tensor_tensor(DT[:, h, :], t1[:], t2[:], op=ALU.add),
                         reads=[t1.r, t2.r], writes=[DT.r])
                    P.op("act", lambda e, h=h, lf=lf: e.activation(ZT[:, h, 0, :], ZEf, AF.Exp, scale=lf),
                         reads=[lg.r, rcn.r], writes=[ZT.r])
                    P.op("act", lambda e, h=h, lb=lb: e.activation(ZT[:, h, 1, :], ZEb, AF.Exp, scale=lb),
                         reads=[lg.r, rcn.r], writes=[ZT.r])
                    P.op("act", lambda e, h=h: e.activation(XI[0:64, h, :], rcn[0:64, 640:768], AF.Exp,
                                                            scale=lg[0:64, h:h + 1]),
                         reads=[lg.r, rcn.r], writes=[XI.r])
                    P.op("act", lambda e, h=h: e.activation(XI[64:128, h, :], rcn[64:128, 640:768], AF.Exp,
                                                            scale=lg[64:128, 4 + h:5 + h]),
                         reads=[lg.r, rcn.r], writes=[XI.r])
                    P.op("act", lambda e, h=h: e.activation(DEC[0:64, h, :], c128[0:64, :], AF.Exp,
                                                            scale=lg[0:64, h:h + 1]),
                         reads=[lg.r, c128.r], writes=[DEC.r])
                    P.op("act", lambda e, h=h: e.activation(DEC[64:128, h, :], c128[64:128, :], AF.Exp,
                                                            scale=lg[64:128, 4 + h:5 + h]),
                         reads=[lg.r, c128.r], writes=[DEC.r])
                P.flush()

        def proj_phase(l):
            with contextlib.ExitStack() as es:
                wb = [sb(es, "pw%d" % i, [128, 8, 512], BF16) for i in range(2)]
                wrot = sb(es, "pwrot", [128, 8, 512], BF16)
                stg = [sb(es, "pst%d" % i, [128, 4, 512], BF16) for i in range(2)]
                rC = sb(es, "ropeC", [128, 512], F32)
                rS = sb(es, "ropeS", [128, 512], F32)
                r1 = [sb(es, "pr1%d" % i, [128, 512], F32) for i in range(2)]
                r2 = [sb(es, "pr2%d" % i, [128, 512], F32) for i in range(2)]
                kzs = sb(es, "kzs", [128, 4, 512], BF16)
                wv = w_in[l].rearrange("(k p) n -> p k n", p=128)
                wcnt = [0]

                def loadw(c0, width=512):
                    wt = wb[wcnt[0] % 2]
                    wcnt[0] += 1
                    P.dma("pool", wt[:, :, 0:width], wv[:, :, c0:c0 + width], writes=[wt.r])
                    return wt

                scnt = [0]

                def nstg():
                    scnt[0] += 1
                    return stg[scnt[0] % 2]

                def fm_mm(pp, wt, c, t0, n):
                    P.group("pe", [lambda e, k=k: e.matmul(
                        pp[:, 0:n], wt[:, k, c * 128:(c + 1) * 128], XNh[0][:, k, t0:t0 + n],
                        start=(k == 0), stop=(k == 7)) for k in range(8)],
                        reads=[wt.r, XNh[0].r], writes=[pp.r])

                def fm_group(c0, dst, func, scale=1.0, nchunks=4, tiles=TILES):
                    wt = loadw(c0, nchunks * 128)
                    for (t0, n, s) in tiles:
                        st = nstg()
                        for c in range(nchunks):
                            pp = nps()
                            fm_mm(pp, wt, c, t0, n)
                            P.op("act", lambda e, st=st, c=c, pp=pp, n=n: e.activation(
                                st[:, c, 0:n], pp[:, 0:n], func, scale=scale), reads=[pp.r], writes=[st.r])
                        P.dma("act", dst[:, t0:t0 + n].rearrange("(c p) t -> p c t", p=128), st[:, 0:nchunks, 0:n],
                              reads=[st.r], writes=[dres_of[id(dst)]])

                def tm_group(c0, dst, name):
                    wt = loadw(c0)
                    for (t0, n, s) in TILES:
                        st = nstg()
                        for sub in range(n // 128):
                            pp = nps()
                            P.group("pe", [lambda e, k=k, pp=pp, sub=sub, t0=t0: e.matmul(
                                pp[:, :], XNh[0][:, k, t0 + sub * 128:t0 + (sub + 1) * 128], wt[:, k, :],
                                start=(k == 0), stop=(k == 7)) for k in range(8)],
                                reads=[wt.r, XNh[0].r], writes=[pp.r])
                            P.op("act", lambda e, st=st, sub=sub, pp=pp: e.activation(
                                st[:, sub, :], pp[:, :], AF.Identity), reads=[pp.r], writes=[st.r])
                        P.dma("act", dst[t0:t0 + n, :].rearrange("(s p) c -> p s c", p=128), st[:, 0:n // 128, :],
                              reads=[st.r], writes=[dres[name]])

                dres_of = {id(QT): dres["QT"], id(KT): dres["KT"], id(GT): dres["GT"]}
                fm_group(0, QT, AF.Identity, scale=0.125)
                fm_group(512, KT, AF.Identity)
                tm_group(1024, VV, "VV")
                tm_group(2048, RV, "RV")
                fm_group(2560, GT, AF.Silu)
                for b in range(3):
                    for hf in range(2):
                        dst = SG[b, hf * 512:(hf + 1) * 512, :]
                        dres_of[id(dst)] = dres["SG"]
                        fm_group(3584 + b * 1024 + hf * 512, dst, AF.Sigmoid)

                wt = loadw(1536)
                wvw = wt[:].rearrange("p k (b two i) -> p (k b) two i", two=2, i=16)
                wvr = wrot[:].rearrange("p k (b two i) -> p (k b) two i", two=2, i=16)
                P.op("dve", lambda e: e.tensor_copy(wvr[:, :, 0, :], wvw[:, :, 1, :]), reads=[wt.r], writes=[wrot.r])
                P.op("dve", lambda e: e.tensor_copy(wvr[:, :, 1, :], wvw[:, :, 0, :]), reads=[wt.r], writes=[wrot.r])
                for (t0, n, s) in TILES:
                    if s == 0:
                        P.dma("sp", rC[:], ropeC_in[:, t0:t0 + n], writes=[rC.r])
                        P.dma("sp", rS[:], ropeS_in[:, t0:t0 + n], writes=[rS.r])
                    st = nstg()
                    for c in range(4):
                        ksc = 1.0 if c < 2 else 0.125
                        pp = nps()
                        fm_mm(pp, wt, c, t0, n)
                        if s == 0:
                            pr = nps()
                            fm_mm(pr, wrot, c, t0, n)
                            a1 = r1[c % 2]
                            a2 = r2[c % 2]
                            P.op("dve", lambda e, a1=a1, pp=pp, ksc=ksc: e.scalar_tensor_tensor(
                                a1[:], pp[:], ksc, rC[:], op0=ALU.mult, op1=ALU.mult),
                                reads=[pp.r, rC.r], writes=[a1.r])
                            P.op("dve", lambda e, a2=a2, pr=pr, ksc=ksc: e.scalar_tensor_tensor(
                                a2[:], pr[:], ksc, rS[:], op0=ALU.mult, op1=ALU.mult),
                                reads=[pr.r, rS.r], writes=[a2.r])
                            P.op("dve", lambda e, st=st, c=c, a1=a1, a2=a2: e.tensor_tensor(
                                st[:, c, :], a1[:], a2[:], op=ALU.add), reads=[a1.r, a2.r], writes=[st.r])
                        else:
                            P.op("act", lambda e, st=st, c=c, pp=pp, n=n, ksc=ksc: e.activation(
                                st[:, c, 0:n], pp[:, 0:n], AF.Identity, scale=ksc), reads=[pp.r], writes=[st.r])
                    P.dma("act", RQT[:, t0:t0 + n].rearrange("(c p) t -> p c t", p=128), st[:, 0:2, 0:n],
                          reads=[st.r], writes=[dres["RQT"]])
                    P.dma("act", RKT[:, t0:t0 + n].rearrange("(c p) t -> p c t", p=128), st[:, 2:4, 0:n],
                          reads=[st.r], writes=[dres["RKT"]])
                    nsub = n // 128
                    for sub in range(nsub):
                        for c in range(2):
                            P.op("pe", lambda e, st=st, sub=sub, c=c: e.transpose(
                                psb[:, (sub * 2 + c) * 128:(sub * 2 + c + 1) * 128],
                                st[:, 2 + c, sub * 128:(sub + 1) * 128], identb[:]),
                                reads=[st.r, identb.r], writes=[psb.r])
                    for sub in range(nsub):
                        for d in range(2):
                            P.op("dve", lambda e, sub=sub, d=d: e.tensor_tensor(
                                kzs[:, sub, :].rearrange("p (h d k) -> p h d k", h=4, d=2)[:, :, d, :],
                                psb[:, sub * 256:(sub + 1) * 256].rearrange("p (h k) -> p h k", h=4),
                                ZT[:, :, d, :], op=ALU.mult), reads=[psb.r, ZT.r], writes=[kzs.r])
                    P.dma("act", KZ[t0:t0 + n, :].rearrange("(s p) c -> p s c", p=128), kzs[:, 0:nsub, :],
                          reads=[kzs.r], writes=[dres["KZ"]])
                P.flush()

        def pool_phase(l):
            with contextlib.ExitStack() as es:
                wt = sb(es, "plw", [128, 8, 512], BF16)
                pw = sb(es, "plpw", [128, 4, 128], BF16)
                U = sb(es, "plU", [128, TOK + 64], F32)
                A = sb(es, "plA", [128, TOK + 64], F32)
                B = sb(es, "plB", [128, TOK + 64], F32)
                rcp = sb(es, "plrc", [128, TOK], F32)
                pl = sb(es, "plpl", [128, TOK], BF16)
                st = [sb(es, "plst%d" % i, [128, 512], BF16) for i in range(2)]
                P.dma("pool", wt[:], w_in[l].rearrange("(k p) n -> p k n", p=128)[:, :, 3072:3584], writes=[wt.r])
                P.dma("pool", pw[:], pool_w[l].rearrange("g c d -> c g d"), writes=[pw.r])
                segs = [(16, SEQ, 0)] + ([(16 + SEQ + 32, CTX, SEQ)] if l == 0 else [])
                tiles = TILES if l == 0 else TILES[:8]
                for g in range(4):
                    w = (2, 4, 8, 16)[g]
                    P.dma("sp", rcp[:, 0:SEQ], prc_in[g:g + 1, :].broadcast_to([128, SEQ]), writes=[rcp.r])
                    if l == 0:
                        P.dma("sp", rcp[:, SEQ:TOK], prcc_in[g:g + 1, :].broadcast_to([128, CTX]), writes=[rcp.r])
                    if g == 0:
                        for buf in (U, A, B):
                            P.op("dve", lambda e, buf=buf: e.memset(buf[:], 0.0), writes=[buf.r])
                    for (t0, n, s) in tiles:
                        pp = nps()
                        P.group("pe", [lambda e, k=k, pp=pp, t0=t0, n=n, g=g: e.matmul(
                            pp[:, 0:n], wt[:, k, g * 128:(g + 1) * 128], XNh[0][:, k, t0:t0 + n],
                            start=(k == 0), stop=(k == 7)) for k in range(8)], reads=[wt.r, XNh[0].r], writes=[pp.r])
                        off = 16 + t0 if s == 0 else 16 + SEQ + 32
                        P.op("act", lambda e, pp=pp, off=off, n=n: e.activation(U[:, off:off + n], pp[:, 0:n], AF.Identity),
                             reads=[pp.r], writes=[U.r])
                    for (o, n, tb) in segs:
                        lo, hi = o - 12, o + n + 12
                        P.op("dve", lambda e, lo=lo, hi=hi: e.tensor_tensor(
                            A[:, lo:hi], U[:, lo - 1:hi - 1], U[:, lo:hi], op=ALU.add), reads=[U.r], writes=[A.r])
                        cur, oth = A, B
                        if w >= 4:
                            lo, hi = o - 10, o + n + 10
                            P.op("dve", lambda e, lo=lo, hi=hi: e.tensor_tensor(
                                B[:, lo:hi], A[:, lo - 1:hi - 1], A[:, lo + 1:hi + 1], op=ALU.add),
                                reads=[A.r], writes=[B.r])
                            cur, oth = B, A
                        if w >= 8:
                            lo, hi = o - 6, o + n + 6
                            P.op("dve", lambda e, lo=lo, hi=hi: e.tensor_tensor(
                                A[:, lo:hi], B[:, lo - 2:hi - 2], B[:, lo + 2:hi + 2], op=ALU.add),
                                reads=[B.r], writes=[A.r])
                            cur, oth = A, B
                        if w >= 16:
                            lo, hi = o, o + n
                            P.op("dve", lambda e, lo=lo, hi=hi: e.tensor_tensor(
                                B[:, lo:hi], A[:, lo - 4:hi - 4], A[:, lo + 4:hi + 4], op=ALU.add),
                                reads=[A.r], writes=[B.r])
                            cur, oth = B, A
                        P.op("dve", lambda e, cur=cur, o=o, n=n, tb=tb, w=w: e.scalar_tensor_tensor(
                            pl[:, tb:tb + n], cur[:, o:o + n], 1.0 / w, U[:, o:o + n], op0=ALU.mult, op1=ALU.subtract),
                            reads=[cur.r, U.r], writes=[pl.r])
                        for (eo, et) in ((o, tb), (o + n - 8, tb + n - 8)):
                            P.op("dve", lambda e, cur=cur, oth=oth, eo=eo, et=et: e.tensor_tensor(
                                oth[:, eo:eo + 8], cur[:, eo:eo + 8], rcp[:, et:et + 8], op=ALU.mult),
                                reads=[cur.r, rcp.r], writes=[oth.r])
                            P.op("dve", lambda e, oth=oth, eo=eo, et=et: e.tensor_tensor(
                                pl[:, et:et + 8], oth[:, eo:eo + 8], U[:, eo:eo + 8], op=ALU.subtract),
                                reads=[oth.r, U.r], writes=[pl.r])
                    for ti, (t0, n, s) in enumerate(tiles):
                        pp = nps()
                        P.group("pe", [lambda e, pp=pp, t0=t0, n=n, g=g: e.matmul(
                            pp[:, 0:n], pw[:, g, :], pl[:, t0:t0 + n], start=True, stop=True)],
                            reads=[pw.r, pl.r], writes=[pp.r])
                        so = st[ti % 2]
                        P.op("act", lambda e, so=so, pp=pp, n=n, g=g: e.activation(
                            so[:, 0:n], pp[:, 0:n], AF.Identity, scale=vecs[:, l, 20 + g:21 + g]),
                            reads=[pp.r, vecs.r], writes=[so.r])
                        P.dma("act", PT[g * 128:(g + 1) * 128, t0:t0 + n], so[:, 0:n], reads=[so.r], writes=[dres["PT"]])
                P.flush()

        def retention_phase(l):
            with contextlib.ExitStack() as es:
                ST = sb(es, "rST", [128, 34, 4, 128], BF16)
                Scur = sb(es, "rScur", [128, 4, 128], F32)
                stmp = sb(es, "rstmp", [128, 4, 128], F32)
                kzt = [sb(es, "rkz%d" % i, [128, 4, 512], BF16) for i in range(2)]
                rvt = [sb(es, "rrv%d" % i, [128, 4, 512], BF16) for i in range(2)]
                chunks_f = [32, 33] + list(range(32))
                chunks_b = [33, 32] + list(range(31, -1, -1))

                def tok_of(c):
                    return SEQ + (c - 32) * 128 if c >= 32 else c * 128

                lcnt = [0]

                def run_pass(chunks, lo, hi, Scur, stmp, kzt, rvt):
                    P.op("dve", lambda e: e.memset(Scur[lo:hi], 0.0), writes=[Scur.r])
                    loaded = {}
                    lc = 0
                    for idx, c in enumerate(chunks):
                        grp = c // 4 if c < 32 else 8
                        if grp not in loaded:
                            kz = kzt[lc % 2]
                            rv = rvt[lc % 2]
                            lc += 1
                            g0 = grp * 512 if grp < 8 else SEQ
                            ng = 4 if grp < 8 else 2
                            P.dma("sp", kz[:, 0:ng, :], KZ[g0:g0 + ng * 128, :].rearrange("(s p) c -> p s c", p=128),
                                  reads=[dres["KZ"]], writes=[kz.r])
                            P.dma("sp", rv[:, 0:ng, :], RV[g0:g0 + ng * 128, :].rearrange("(s p) c -> p s c", p=128),
                                  reads=[dres["RV"]], writes=[rv.r])
                            loaded = {grp: (kz, rv)}
                        kz, rv = loaded[grp]
                        ci = c % 4 if c < 32 else c - 32
                        pp = nps()
                        P.group("pe", [lambda e, h=h, pp=pp, kz=kz, rv=rv, ci=ci: e.matmul(
                            pp[:, h * 128:(h + 1) * 128], kz[:, ci, h * 128:(h + 1) * 128],
                            rv[:, ci, h * 128:(h + 1) * 128], start=True, stop=True) for h in range(4)],
                            reads=[kz.r, rv.r], writes=[pp.r])
                        P.op("act", lambda e, c=c: e.activation(ST[lo:hi, c], Scur[lo:hi], AF.Identity),
                             reads=[Scur.r], writes=[ST.r])
                        if c >= 32 and idx == 0:
                            P.op("dve", lambda e, pp=pp: e.tensor_copy(
                                Scur[lo:hi], pp[lo:hi, :].rearrange("p (h v) -> p h v", h=4)),
                                reads=[pp.r], writes=[Scur.r])
                        else:
                            P.op("dve", lambda e: e.tensor_tensor(stmp[lo:hi], Scur[lo:hi], DEC[lo:hi], op=ALU.mult),
                                 reads=[Scur.r, DEC.r], writes=[stmp.r])
                            P.op("dve", lambda e, pp=pp: e.tensor_tensor(
                                Scur[lo:hi], stmp[lo:hi], pp[lo:hi, :].rearrange("p (h v) -> p h v", h=4), op=ALU.add),
                                reads=[stmp.r, pp.r], writes=[Scur.r])
                        yield

                ScurB = sb(es, "rScurB", [128, 4, 128], F32)
                stmpB = sb(es, "rstmpB", [128, 4, 128], F32)
                kztB = [sb(es, "rkzB%d" % i, [128, 4, 512], BF16) for i in range(2)]
                rvtB = [sb(es, "rrvB%d" % i, [128, 4, 512], BF16) for i in range(2)]
                gf_ = run_pass(chunks_f, 0, 64, Scur, stmp, kzt, rvt)
                gb_ = run_pass(chunks_b, 64, 128, ScurB, stmpB, kztB, rvtB)
                for _ in range(len(chunks_f)):
                    next(gf_)
                    next(gb_)

                def two(name, shape, dt):
                    return [sb(es, "%s%d" % (name, i), shape, dt) for i in range(2)]
                QD = two("rQD", [128, 4, 512], BF16)
                KD = two("rKD", [64, 4, 512], BF16)
                RVt = two("rRVt", [128, 4, 512], BF16)
                GTt = two("rGT", [128, 4, 512], BF16)
                GG = two("rGG", [128, 4, 512], F32)
                rts = two("rrts", [128, 4, 512], BF16)
                ATt = two("rAT", [128, 4, 128], BF16)
                QX = two("rQX", [128, 4, 128], BF16)
                ob = two("rob", [128, 512], BF16)
                o2b = two("ro2b", [128, 512], BF16)
                mean = two("rmean", [128, 512], F32)
                msq = two("rmsq", [128, 512], F32)
                rstd = two("rrstd", [128, 512], F32)
                dd = two("rdd", [128, 512], F32)
                tiles = TILES if l == 0 else TILES[:8]
                clist = []
                for ti, (t0, n, s) in enumerate(tiles):
                    for ci in range(n // 128):
                        clist.append((ti, t0, n, s, ci))

                def load_tile(ti, t0, n, s):
                    b = ti % 2
                    nch = n // 128
                    for half in range(2):
                        P.dma("sp", QD[b][half * 64:(half + 1) * 64, :, 0:n],
                              RQT[:, t0:t0 + n].rearrange("(h d) t -> d h t", d=64),
                              reads=[dres["RQT"]], writes=[QD[b].r])
                    P.dma("sp", KD[b][:, :, 0:n], RKT[:, t0:t0 + n].rearrange("(h d) t -> d h t", d=64),
                          reads=[dres["RKT"]], writes=[KD[b].r])
                    P.dma("sp", RVt[b][:, 0:nch, :], RV[t0:t0 + n, :].rearrange("(s p) c -> p s c", p=128),
                          reads=[dres["RV"]], writes=[RVt[b].r])
                    P.dma("sp", GTt[b][:, :, 0:n], GT[:, t0:t0 + n].rearrange("(c p) t -> p c t", p=128),
                          reads=[dres["GT"]], writes=[GTt[b].r])
                    for h in range(4):
                        P.op("pool", lambda e, h=h, n=n, b=b: e.tensor_scalar(
                            GG[b][:, h, 0:n], GTt[b][:, h, 0:n], vecs[:, l, 16 + h:17 + h], None, op0=ALU.mult),
                            reads=[GTt[b].r, vecs.r], writes=[GG[b].r])

                def stage_a(idx):
                    ti, t0, n, s, ci = clist[idx]
                    if ci == 0:
                        load_tile(ti, t0, n, s)
                    b = ti % 2
                    q = idx % 2
                    o = ci * 128
                    c = (t0 // 128 + ci) if s == 0 else 32 + ci
                    pI = ps[0]
                    pO = ps[1 + q]
                    pM = ps[3 + q]
                    pQ = ps[5 + q]
                    P.group("pe", [lambda e, h=h, o=o, b=b: e.matmul(
                        pI[:, h * 128:(h + 1) * 128], KD[b][0:64, h, o:o + 128], QD[b][0:64, h, o:o + 128],
                        start=True, stop=True) for h in range(4)], reads=[KD[b].r, QD[b].r], writes=[pI.r])
                    P.op("dve", lambda e, q=q: e.tensor_tensor(
                        ATt[q][:], pI[:].rearrange("p (h i) -> p h i", h=4), DT[:], op=ALU.mult),
                        reads=[pI.r, DT.r], writes=[ATt[q].r])
                    P.op("dve", lambda e, o=o, b=b, q=q: e.tensor_tensor(QX[q][:], QD[b][:, :, o:o + 128], XI[:], op=ALU.mult),
                         reads=[QD[b].r, XI.r], writes=[QX[q].r])
                    fns = []
                    for h in range(4):
                        fns.append(lambda e, h=h, pO=pO, ci=ci, b=b, q=q: e.matmul(
                            pO[:, h * 128:(h + 1) * 128], RVt[b][:, ci, h * 128:(h + 1) * 128], ATt[q][:, h, :],
                            start=True, stop=False))
                        fns.append(lambda e, h=h, pO=pO, c=c, q=q: e.matmul(
                            pO[:, h * 128:(h + 1) * 128], ST[:, c, h, :], QX[q][:, h, :], start=False, stop=True))
                    P.group("pe", fns, reads=[RVt[b].r, ATt[q].r, ST.r, QX[q].r], writes=[pO.r])
                    P.op("act", lambda e, pO=pO, q=q: e.activation(ob[q][:], pO[:], AF.Identity), reads=[pO.r], writes=[ob[q].r])
                    P.op("act", lambda e, pO=pO, q=q: e.activation(o2b[q][:], pO[:], AF.Square), reads=[pO.r], writes=[o2b[q].r])
                    P.group("pe", [lambda e, pM=pM, q=q: e.matmul(pM[:], onesb[:], ob[q][:], start=True, stop=True)],
                            reads=[onesb.r, ob[q].r], writes=[pM.r])
                    P.group("pe", [lambda e, pQ=pQ, q=q: e.matmul(pQ[:], onesb[:], o2b[q][:], start=True, stop=True)],
                            reads=[onesb.r, o2b[q].r], writes=[pQ.r])

                def stage_b(idx):
                    ti, t0, n, s, ci = clist[idx]
                    b = ti % 2
                    q = idx % 2
                    o = ci * 128
                    pO = ps[1 + q]
                    pM = ps[3 + q]
                    pQ = ps[5 + q]
                    P.op("act", lambda e, q=q, pM=pM: e.activation(mean[q][:], pM[:], AF.Identity, scale=1.0 / 128),
                         reads=[pM.r], writes=[mean[q].r])
                    P.op("act", lambda e, q=q, pM=pM: e.activation(msq[q][:], pM[:], AF.Square, scale=1.0 / 128),
                         reads=[pM.r], writes=[msq[q].r])
                    P.op("dve", lambda e, q=q, pQ=pQ: e.scalar_tensor_tensor(
                        rstd[q][:], pQ[:], 1.0 / 128, msq[q][:], op0=ALU.mult, op1=ALU.subtract),
                        reads=[pQ.r, msq[q].r], writes=[rstd[q].r])
                    P.op("act", lambda e, q=q: e.activation(rstd[q][:], rstd[q][:], AF.Sqrt, bias=epsD[:, 1:2]),
                         reads=[rstd[q].r, epsD.r], writes=[rstd[q].r])
                    P.op("dve", lambda e, q=q: e.reciprocal(rstd[q][:], rstd[q][:]), reads=[rstd[q].r], writes=[rstd[q].r])
                    P.op("dve", lambda e, q=q, pO=pO: e.tensor_tensor(dd[q][:], pO[:], mean[q][:], op=ALU.subtract),
                         reads=[pO.r, mean[q].r], writes=[dd[q].r])
                    P.op("dve", lambda e, q=q: e.tensor_tensor(dd[q][:], dd[q][:], rstd[q][:], op=ALU.mult),
                         reads=[dd[q].r, rstd[q].r], writes=[dd[q].r])
                    P.op("dve", lambda e, o=o, b=b, q=q: e.tensor_tensor(
                        rts[b][:, :, o:o + 128], dd[q][:].rearrange("p (h i) -> p h i", h=4), GG[b][:, :, o:o + 128],
                        op=ALU.mult), reads=[dd[q].r, GG[b].r], writes=[rts[b].r])
                    if ci == n // 128 - 1:
                        P.dma("act", RT[:, t0:t0 + n].rearrange("(c p) t -> p c t", p=128), rts[b][:, :, 0:n],
                              reads=[rts[b].r], writes=[dres["RT"]])

                stage_a(0)
                for idx in range(len(clist)):
                    if idx + 1 < len(clist):
                        stage_a(idx + 1)
                    stage_b(idx)
                P.flush()

        def na_phase(l, pre=None):
            with contextlib.ExitStack() as es:
                TL = sb(es, "aTL", [128, 8, 22 * 64], BF16)
                BM = sb(es, "aBM", [80, 8, 512], BF16)
                QAs = [sb(es, "aQA%d" % i, [80, 8, 512], BF16) for i in range(2)]
                KAs = [sb(es, "aKA%d" % i, [80, 8, 1024], BF16) for i in range(2)]
                VAs = [sb(es, "aVA%d" % i, [128, 8, 8, 65], BF16) for i in range(2)]
                KC = sb(es, "aKC", [64, 8, CTX], BF16)
                VC = sb(es, "aVC", [128, 2, 8, 65], BF16)
                PTt = [sb(es, "aPT%d" % i, [128, 512], BF16) for i in range(4)]
                rden = [sb(es, "arden%d" % i, [65, 512], F32) for i in range(2)]
                bcs = [sb(es, "abcs%d" % i, [64, 512], F32) for i in range(2)]
                ast = [sb(es, "aast%d" % i, [64, 512], BF16) for i in range(2)]
                P.dma("pool", TL[:].rearrange("p h x -> p (h x)"), tl_in[l], writes=[TL.r])
                P.dma("pool", BM[64:80], bmask_in, writes=[BM.r])
                for KA in KAs:
                    for hh in range(8):
                        P.dma("pool", KA[64:80, hh, :], amask_in, writes=[KA.r])
                for VA in VAs:
                    P.op("pool", lambda e, VA=VA: e.memset(VA[:, :, :, 64:65], 1.0), writes=[VA.r])
                if pre is not None:
                    pre()
                P.op("pool", lambda e: e.memset(VC[:, :, :, 64:65], 1.0), writes=[VC.r])
                P.dma("sp", KC[:], KT[:, SEQ:TOK].rearrange("(h d) t -> d h t", d=64), reads=[dres["KT"]], writes=[KC.r])
                for cs in range(2):
                    P.dma("sp", VC[:, cs, :, 0:64],
                          VV[SEQ + cs * 128:SEQ + (cs + 1) * 128, :].rearrange("p (h d) -> p h d", d=64),
                          reads=[dres["VV"]], writes=[VC.r])
                tiles = TILES if l == 0 else TILES[:8]
                pcnt = 0
                for qt, (t0, n, s) in enumerate(tiles):
                    QA, KA, VA = QAs[qt % 2], KAs[qt % 2], VAs[qt % 2]
                    P.dma("sp", QA[0:64, :, 0:n], QT[:, t0:t0 + n].rearrange("(h d) t -> d h t", d=64),
                          reads=[dres["QT"]], writes=[QA.r])
                    slots = []
                    if s == 0:
                        R = qt * 8
                        P.op("pool", lambda e, qt=qt, QA=QA: e.tensor_copy(
                            QA[64:80, :, :], BM[64:80, qt:qt + 1, :].broadcast_to([16, 8, 512])),
                            reads=[BM.r], writes=[QA.r])
                        slots = [sp_ for sp_ in range(8) if 0 <= R - 4 + 2 * sp_ <= 62]
                        s_lo, s_hi = slots[0], slots[-1] + 1
                        k0 = (R - 4 + 2 * s_lo) * 64
                        nk = (s_hi - s_lo) * 128
                        P.dma("sp", KA[0:64, :, s_lo * 128:s_hi * 128],
                              KT[:, k0:k0 + nk].rearrange("(h d) t -> d h t", d=64), reads=[dres["KT"]], writes=[KA.r])
                        for sp_ in range(s_lo, s_hi):
                            kk0 = k0 + (sp_ - s_lo) * 128
                            P.dma("sp", VA[:, sp_, :, 0:64],
                                  VV[kk0:kk0 + 128, :].rearrange("p (h d) -> p h d", d=64),
                                  reads=[dres["VV"]], writes=[VA.r])
                    items = []
                    for h in range(8):
                        its = [("w", sp_) for sp_ in slots] + [("c", 0), ("c", 1)]
                        for ii, (kind, sp_) in enumerate(its):
                            items.append((h, kind, sp_, ii, len(its) - 1))

                    def emit_s(j):
                        h, kind, sp_, ii, last = items[j]
                        pS = ps[2 + j % 4]
                        if kind == "w":
                            w0 = 14 - 2 * sp_
                            P.group("pe", [
                                lambda e, pS=pS, h=h, sp_=sp_, n=n, KA=KA, QA=QA: e.matmul(
                                    pS[:, 0:n], KA[0:80, h, sp_ * 128:(sp_ + 1) * 128], QA[0:80, h, 0:n],
                                    start=True, stop=False),
                                lambda e, pS=pS, h=h, w0=w0, n=n: e.matmul(
                                    pS[:, 0:n], identb[:], TL[:, h, w0 * 64:w0 * 64 + n], start=False, stop=True)],
                                reads=[KA.r, QA.r, identb.r, TL.r], writes=[pS.r])
                        else:
                            P.group("pe", [lambda e, pS=pS, h=h, sp_=sp_, n=n, QA=QA: e.matmul(
                                pS[:, 0:n], KC[0:64, h, sp_ * 128:(sp_ + 1) * 128], QA[0:64, h, 0:n],
                                start=True, stop=True)], reads=[KC.r, QA.r], writes=[pS.r])

                    def emit_rest(j):
                        h, kind, sp_, ii, last = items[j]
                        pS = ps[2 + j % 4]
                        pO = ps[h % 2]
                        pt_ = PTt[j % 4]
                        P.op("act", lambda e, pt_=pt_, pS=pS, n=n: e.activation(pt_[:, 0:n], pS[:, 0:n], AF.Exp),
                             reads=[pS.r], writes=[pt_.r])
                        lhs = (lambda sp_=sp_, h=h, VA=VA: VA[:, sp_, h, :]) if kind == "w" else \
                            (lambda sp_=sp_, h=h: VC[:, sp_, h, :])
                        vres = VA.r if kind == "w" else VC.r
                        P.group("pe", [lambda e, pO=pO, lhs=lhs, pt_=pt_, n=n, ii=ii, last=last: e.matmul(
                            pO[0:65, 0:n], lhs(), pt_[:, 0:n], start=(ii == 0), stop=(ii == last))],
                            reads=[vres, pt_.r], writes=[pO.r])
                        if ii == last:
                            fin_q.append((j + 2, lambda h=h, pO=pO: finalize(h, pO)))

                    def finalize(h, pO):
                        if True:
                            rd = rden[h % 2]
                            bc = bcs[h % 2]
                            P.op("dve", lambda e, pO=pO, n=n, rd=rd: e.reciprocal(rd[64:65, 0:n], pO[64:65, 0:n]),
                                 reads=[pO.r], writes=[rd.r])
                            pB = ps[6]
                            P.group("pe", [lambda e, pB=pB, n=n, rd=rd: e.matmul(
                                pB[0:64, 0:n], ones1[64:65, 0:64], rd[64:65, 0:n], start=True, stop=True)],
                                reads=[ones1.r, rd.r], writes=[pB.r])
                            P.op("act", lambda e, pB=pB, n=n, bc=bc: e.activation(bc[:, 0:n], pB[0:64, 0:n], AF.Identity),
                                 reads=[pB.r], writes=[bc.r])
                            a_ = ast[h % 2]
                            P.op("dve", lambda e, a_=a_, pO=pO, n=n, bc=bc: e.tensor_tensor(
                                a_[:, 0:n], pO[0:64, 0:n], bc[:, 0:n], op=ALU.mult), reads=[pO.r, bc.r], writes=[a_.r])
                            P.dma("act", AT[h * 64:(h + 1) * 64, t0:t0 + n], a_[:, 0:n], reads=[a_.r], writes=[dres["AT"]])

                    LOOK = 3
                    fin_q = []
                    for j in range(min(LOOK, len(items))):
                        emit_s(j)
                    for j in range(len(items)):
                        if j + LOOK < len(items):
                            emit_s(j + LOOK)
                        emit_rest(j)
                        while fin_q and fin_q[0][0] <= j:
                            fin_q.pop(0)[1]()
                    while fin_q:
                        fin_q.pop(0)[1]()
                P.flush()

        def merge_weights(es, l):
            WB = [sb(es, "mWB%d" % i, [128, 4, D], BF16) for i in range(3)]
            WO = sb(es, "mWO", [128, 8, D], BF16)

            def load():
                for i, wsrc in enumerate((w_ba, w_bb, w_bc)):
                    P.dma("pool", WB[i][:], wsrc[l].rearrange("(k p) n -> p k n", p=128), writes=[WB[i].r])
                P.dma("pool", WO[:], w_out[l].rearrange("(k p) n -> p k n", p=128), writes=[WO.r])
            return WB, WO, load

        def merge_phase(l, WB, WO):
            with contextlib.ExitStack() as es:
                bt = [sb(es, "mbt%d" % i, [128, 4, 512], BF16) for i in range(3)]
                sg = [sb(es, "msg%d" % i, [128, 8, 512], BF16) for i in range(3)]
                ht = sb(es, "mht", [128, 8, 512], F32)
                ho = sb(es, "mho", [128, 8, 512], F32)
                mT = sb(es, "mmT", [128, 8, 512], BF16)
                tt = [sb(es, "mtt%d" % i, [128, 512], F32) for i in range(3)]
                tiles = TILES if l == 0 else TILES[:8]
                srcs = (AT, RT, PT)
                names = ("AT", "RT", "PT")
                for (t0, n, s) in tiles:
                    for i in range(3):
                        P.dma("sp", bt[i][:, :, 0:n], srcs[i][:, t0:t0 + n].rearrange("(c p) t -> p c t", p=128),
                              reads=[dres[names[i]]], writes=[bt[i].r])
                        P.dma("sp", sg[i][:, :, 0:n], SG[i, :, t0:t0 + n].rearrange("(c p) t -> p c t", p=128),
                              reads=[dres["SG"]], writes=[sg[i].r])
                    P.dma("sp", ht[:, :, 0:n], hT[:, :, t0:t0 + n].rearrange("k p t -> p k t"),
                          reads=[dres["hT"]], writes=[ht.r])
                    for mo in range(8):
                        for i in range(3):
                            pp = nps()
                            P.group("pe", [lambda e, k=k, i=i, pp=pp, mo=mo, n=n: e.matmul(
                                pp[:, 0:n], WB[i][:, k, mo * 128:(mo + 1) * 128], bt[i][:, k, 0:n],
                                start=(k == 0), stop=(k == 3)) for k in range(4)],
                                reads=[WB[i].r, bt[i].r], writes=[pp.r])
                            P.op("dve", lambda e, i=i, pp=pp, mo=mo, n=n: e.tensor_tensor(
                                tt[i][:, 0:n], pp[:, 0:n], sg[i][:, mo, 0:n], op=ALU.mult),
                                reads=[pp.r, sg[i].r], writes=[tt[i].r])
                        P.op("dve", lambda e, n=n: e.tensor_tensor(tt[0][:, 0:n], tt[0][:, 0:n], tt[1][:, 0:n], op=ALU.add),
                             reads=[tt[0].r, tt[1].r], writes=[tt[0].r])
                        P.op("dve", lambda e, mo=mo, n=n: e.tensor_tensor(
                            mT[:, mo, 0:n], tt[0][:, 0:n], tt[2][:, 0:n], op=ALU.add),
                            reads=[tt[0].r, tt[2].r], writes=[mT.r])
                    for mo in range(8):
                        pp = nps()
                        P.group("pe", [lambda e, k=k, pp=pp, mo=mo, n=n: e.matmul(
                            pp[:, 0:n], WO[:, k, mo * 128:(mo + 1) * 128], mT[:, k, 0:n],
                            start=(k == 0), stop=(k == 7)) for k in range(8)], reads=[WO.r, mT.r], writes=[pp.r])
                        P.op("dve", lambda e, pp=pp, mo=mo, n=n, s=s: e.scalar_tensor_tensor(
                            ho[:, mo, 0:n], pp[:, 0:n], mod(l, 2, mo, s), ht[:, mo, 0:n], op0=ALU.mult, op1=ALU.add),
                            reads=[pp.r, mods.r, ht.r], writes=[ho.r])
                    P.dma("act", hT[:, :, t0:t0 + n].rearrange("k p t -> p k t"), ho[:, :, 0:n],
                          reads=[ho.r], writes=[dres["hT"]])
                P.flush()

        def ffn1_phase(l, pre=None):
            with contextlib.ExitStack() as es:
                wg = [sb(es, "fwg%d" % i, [128, 8, 256], BF16) for i in range(2)]
                wu = [sb(es, "fwu%d" % i, [128, 8, 256], BF16) for i in range(2)]
                sgt = [sb(es, "fsg%d" % i, [128, 512], F32) for i in range(2)]
                act = [sb(es, "fac%d" % i, [128, 512], BF16) for i in range(3)]
                tiles = TILES if l == 0 else TILES[:8]
                gv = w_fg[l].rearrange("(k p) n -> p k n", p=128)
                uv = w_fu[l].rearrange("(k p) n -> p k n", p=128)
                cnt = 0
                for jg in range(NHC // 2):
                    g_ = wg[jg % 2]
                    u_ = wu[jg % 2]
                    P.dma("pool", g_[:], gv[:, :, jg * 256:(jg + 1) * 256], writes=[g_.r])
                    P.dma("pool", u_[:], uv[:, :, jg * 256:(jg + 1) * 256], writes=[u_.r])
                    if jg == 1 and pre is not None:
                        pre()
                    for jj in range(2):
                        j = jg * 2 + jj
                        for (t0, n, s) in tiles:
                            pg = nps()
                            P.group("pe", [lambda e, k=k, pg=pg, g_=g_, jj=jj, t0=t0, n=n: e.matmul(
                                pg[:, 0:n], g_[:, k, jj * 128:(jj + 1) * 128], XNh[0][:, k, t0:t0 + n],
                                start=(k == 0), stop=(k == 7)) for k in range(8)], reads=[g_.r, XNh[0].r], writes=[pg.r])
                            pu = nps()
                            P.group("pe", [lambda e, k=k, pu=pu, u_=u_, jj=jj, t0=t0, n=n: e.matmul(
                                pu[:, 0:n], u_[:, k, jj * 128:(jj + 1) * 128], XNh[0][:, k, t0:t0 + n],
                                start=(k == 0), stop=(k == 7)) for k in range(8)], reads=[u_.r, XNh[0].r], writes=[pu.r])
                            sg_ = sgt[cnt % 2]
                            ac = act[cnt % 3]
                            cnt += 1
                            P.op("act", lambda e, sg_=sg_, pg=pg, n=n: e.activation(sg_[:, 0:n], pg[:, 0:n], AF.Silu),
                                 reads=[pg.r], writes=[sg_.r])
                            P.op("dve", lambda e, ac=ac, sg_=sg_, pu=pu, n=n: e.tensor_tensor(
                                ac[:, 0:n], pu[:, 0:n], sg_[:, 0:n], op=ALU.mult), reads=[pu.r, sg_.r], writes=[ac.r])
                            P.dma("act", ACTT[j, :, t0:t0 + n], ac[:, 0:n], reads=[ac.r], writes=[dres["ACTT"]])
                P.flush()

        def ffn2_phase(l, WD):
            with contextlib.ExitStack() as es:
                at = [sb(es, "fat%d" % i, [128, NHC, 512], BF16) for i in range(2)]
                ht = [sb(es, "fht%d" % i, [128, 8, 512], F32) for i in range(2)]
                ho = sb(es, "fho", [128, 8, 512], F32)
                tiles = TILES if l == 0 else TILES[:8]
                for ti, (t0, n, s) in enumerate(tiles):
                    a_ = at[ti % 2]
                    h_ = ht[ti % 2]
                    P.dma("sp", a_[:, :, 0:n], ACTT[:, :, t0:t0 + n].rearrange("j p t -> p j t"),
                          reads=[dres["ACTT"]], writes=[a_.r])
                    P.dma("sp", h_[:, :, 0:n], hT[:, :, t0:t0 + n].rearrange("k p t -> p k t"),
                          reads=[dres["hT"]], writes=[h_.r])
                    for mo in range(8):
                        pp = nps()
                        P.group("pe", [lambda e, j=j, pp=pp, mo=mo, a_=a_, n=n: e.matmul(
                            pp[:, 0:n], WD[:, j, mo * 128:(mo + 1) * 128], a_[:, j, 0:n],
                            start=(j == 0), stop=(j == NHC - 1)) for j in range(NHC)], reads=[WD.r, a_.r], writes=[pp.r])
                        P.op("dve", lambda e, pp=pp, mo=mo, n=n, s=s, h_=h_: e.scalar_tensor_tensor(
                            ho[:, mo, 0:n], pp[:, 0:n], mod(l, 5, mo, s), h_[:, mo, 0:n], op0=ALU.mult, op1=ALU.add),
                            reads=[pp.r, mods.r, h_.r], writes=[ho.r])
                    P.dma("act", hT[:, :, t0:t0 + n].rearrange("k p t -> p k t"), ho[:, :, 0:n],
                          reads=[ho.r], writes=[dres["hT"]])
                P.flush()

        def out_phase():
            with contextlib.ExitStack() as es:
                hb = [sb(es, "oh%d" % i, [128, 8, 512], F32) for i in range(2)]
                sqs = [sb(es, "osq%d" % i, [128, 8, 512], BF16) for i in range(2)]
                rss = [sb(es, "ors%d" % i, [128, 512], F32) for i in range(2)]
                yys = [sb(es, "oyy%d" % i, [128, 8, 512], F32) for i in range(2)]
                ot = [sb(es, "oot%d" % i, [128, D], F32) for i in range(2)]
                cnt = 0
                def stage_a(ti):
                    t0, n, s = TILES[ti]
                    ht, sq, rs = hb[ti % 2], sqs[ti % 2], rss[ti % 2]
                    P.dma("sp", ht[:], hT[:, :, t0:t0 + n].rearrange("k p t -> p k t"), reads=[dres["hT"]], writes=[ht.r])
                    P.op("act", lambda e: e.activation(sq[:], ht[:], AF.Square), reads=[ht.r], writes=[sq.r])
                    pp = nps()
                    P.group("pe", [lambda e, k=k: e.matmul(pp[:], onesb[:], sq[:, k, :], start=(k == 0), stop=(k == 7))
                                   for k in range(8)], reads=[onesb.r, sq.r], writes=[pp.r])
                    P.op("act", lambda e: e.activation(rs[:], pp[:], AF.Sqrt, bias=epsD[:, 0:1]),
                         reads=[pp.r, epsD.r], writes=[rs.r])
                    P.op("dve", lambda e: e.reciprocal(rs[:], rs[:]), reads=[rs.r], writes=[rs.r])

                def stage_b(ti):
                    t0, n, s = TILES[ti]
                    ht, rs, yy = hb[ti % 2], rss[ti % 2], yys[ti % 2]
                    for kc in range(8):
                        P.op("dve", lambda e, kc=kc: e.scalar_tensor_tensor(
                            yy[:, kc, :], ht[:, kc, :], gfin[:, kc:kc + 1], rs[:], op0=ALU.mult, op1=ALU.mult),
                            reads=[ht.r, gfin.r, rs.r], writes=[yy.r])
                    for sub in range(4):
                        o_ = ot[(ti * 4 + sub) % 2]
                        for half in range(2):
                            pt = nps()
                            for kk in range(4):
                                kc = half * 4 + kk
                                P.op("pe", lambda e, pt=pt, kk=kk, kc=kc, sub=sub: e.transpose(
                                    pt[:, kk * 128:(kk + 1) * 128], yy[:, kc, sub * 128:(sub + 1) * 128], ident[:]),
                                    reads=[yy.r, ident.r], writes=[pt.r])
                            if half == 0:
                                P.op("dve", lambda e, o_=o_, pt=pt: e.tensor_copy(o_[:, 0:512], pt[:]),
                                     reads=[pt.r], writes=[o_.r])
                            else:
                                P.op("act", lambda e, o_=o_, pt=pt: e.activation(o_[:, 512:1024], pt[:], AF.Identity),
                                     reads=[pt.r], writes=[o_.r])
                        P.dma("pool", out_ap[t0 + sub * 128:t0 + (sub + 1) * 128, :], o_[:],
                              reads=[o_.r], writes=[r_out], is_output=True)

                stage_a(0)
                for ti in range(8):
                    if ti + 1 < 8:
                        stage_a(ti + 1)
                    stage_b(ti)
                P.flush(final=True)

        def with_xn(fns):
            with contextlib.ExitStack() as xs:
                XNh[0] = sb(xs, "XN", [128, 8, TOK], BF16)
                for f in fns:
                    f()
                XNh[0] = None

        for l in range(depth):
            last = (l == DEPTH - 1)
            retention_tables(l)
            with_xn([lambda: norm_phase(l, 0, TILES, dump=("XNd" in debug_outs)),
                     lambda: proj_phase(l), lambda: pool_phase(l)])
            retention_phase(l)
            with contextlib.ExitStack() as ms:
                WB, WO, mload = merge_weights(ms, l)
                na_phase(l, pre=mload)
                merge_phase(l, WB, WO)
            with contextlib.ExitStack() as fs:
                WD = sb(fs, "fWD", [128, NHC, D], BF16)

                def wdload():
                    P.dma("pool", WD[:], w_fd[l].rearrange("(j p) n -> p j n", p=128), writes=[WD.r])
                with_xn([lambda: norm_phase(l, 1, TILES[:8] if last else TILES),
                         lambda: ffn1_phase(l, pre=wdload)])
                ffn2_phase(l, WD)
        out_phase()
    return nc


def _const_tables():
    f32 = np.float32
    ident = np.eye(128, dtype=f32)
    amask = np.zeros((16, 1024), f32)
    for i in range(16):
        amask[i, i * 64:(i + 1) * 64] = 1.0
    bmask = np.full((16, 8, 512), NEGM, f32)
    for qt in range(8):
        R = qt * 8
        for i in range(16):
            kr = R - 4 + i
            for rho in range(8):
                r = R + rho
                r0 = min(max(r - 4, 0), 56)
                if 0 <= kr <= 63 and r0 <= kr < r0 + 8:
                    bmask[i, qt, rho * 64:(rho + 1) * 64] = 0.0
    t = np.arange(SEQ)
    row = (t // 64).astype(f32)
    col = (t % 64).astype(f32)
    inv = (10000.0 ** (-np.arange(16, dtype=f32) / 16)).astype(f32)
    C = np.zeros((64, SEQ), f32)
    S = np.zeros((64, SEQ), f32)
    for d in range(64):
        pos = row if d < 32 else col
        i = d % 16
        ang = (pos * inv[i]).astype(f32)
        C[d] = np.cos(ang)
        sgn = -1.0 if (d % 32) < 16 else 1.0
        S[d] = sgn * np.sin(ang)
    ropeC = np.concatenate([C, C], 0)
    ropeS = np.concatenate([S, S], 0)
    j = np.arange(128)[:, None].astype(f32)
    i = np.arange(128)[None, :].astype(f32)
    EF = np.maximum(i - j, 0)
    MF = (i >= j).astype(f32)
    EB = np.maximum(j - i, 0)
    MB = (j > i).astype(f32)
    ZEf = np.repeat(127.0 - j, 64, 1)
    ZEb = np.repeat(j, 64, 1)
    XE = np.zeros((128, 128), f32)
    XE[0:64] = i + 1.0
    XE[64:128] = 128.0 - i
    rconst = np.concatenate([EF, MF, EB, MB, ZEf, ZEb, XE], 1).astype(f32)

    def rc(n):
        tt = np.arange(n)
        out = np.zeros((4, n), f32)
        for gi, w in enumerate((2, 4, 8, 16)):
            lo = np.clip(tt - w // 2, 0, n)
            hi = np.clip(tt - w // 2 + w, 0, n)
            out[gi] = 1.0 / (hi - lo).astype(f32)
        return out
    return dict(ident=ident, amask=amask, bmask=bmask, ropeC=ropeC, ropeS=ropeS, rconst=rconst,
                prc=rc(SEQ), prcc=rc(CTX))


def _tl_table(rpb):
    L = rpb.shape[0]
    a = np.arange(2)[:, None, None, None]
    kc = np.arange(64)[None, :, None, None]
    u = np.arange(22)[None, None, :, None]
    qc = np.arange(64)[None, None, None, :]
    dr = a + 10 - u + 0 * kc + 0 * qc
    wstart = np.clip(qc - 8, 0, 48)
    ok = (np.abs(dr) <= 7) & (kc >= wstart) & (kc < wstart + 16)
    dri = np.clip(dr + 7, 0, 14)
    dci = np.clip(kc - qc + 15 + 0 * dr, 0, 30)
    out = np.empty((L, 2, 64, 8, 22, 64), np.float32)
    for l in range(L):
        for h in range(8):
            g = rpb[l, h][dri, dci]
            out[l, :, :, h] = np.where(ok, g, np.float32(NEGM))
    return out.reshape(L, 128, 8 * 22 * 64)


_CONSTS = None


def make_in_maps(inp):
    global _CONSTS
    if _CONSTS is None:
        _CONSTS = _const_tables()
    f32 = np.float32
    g = {k: np.asarray(v, dtype=f32) for k, v in inp.items()}
    L = DEPTH

    def col(v):
        return np.ascontiguousarray(v.reshape(L, -1, 128).transpose(2, 0, 1))
    fin = np.broadcast_to(g["final_norm_g"][None], (L, D))
    vecs = np.concatenate([col(g["norm1_g"]), col(g["norm2_g"]), col(g["ret_gn_g"]), col(g["pool_scale"]),
                           col(np.ascontiguousarray(fin))], axis=2)
    logits = np.ascontiguousarray(np.broadcast_to(
        np.concatenate([g["ret_logit_f"], g["ret_logit_b"]], 1)[None], (128, L, 8)))
    shared = dict(w_ada=g["w_ada"], b_ada=g["b_ada"].reshape(L, 1, 6 * D), vecs=np.ascontiguousarray(vecs),
                  logits=logits, w_in=g["w_in"], pool_w=g["pool_w"], w_branch_a=g["w_branch_a"],
                  w_branch_b=g["w_branch_b"], w_branch_c=g["w_branch_c"], w_out=g["w_out"],
                  w_ffn_gate=g["w_ffn_gate"], w_ffn_up=g["w_ffn_up"], w_ffn_down=g["w_ffn_down"],
                  tl=_tl_table(g["na_rpb"]))
    shared.update(_CONSTS)
    maps = []
    cc = g["c_ctx"].reshape(128, 8)
    for b in range(g["x"].shape[0]):
        m = dict(shared)
        m["x"] = np.ascontiguousarray(g["x"][b])
        m["ctx"] = np.ascontiguousarray(g["ctx"][b])
        m["cvec"] = np.ascontiguousarray(np.stack([g["c"][b].reshape(128, 8), cc], axis=2))
        maps.append(m)
    return maps


_NC = None


def kernel(**inputs):
    global _NC
    maps = make_in_maps(inputs)
    if _NC is None:
        _NC = build_program()
    res = run_bass_kernel_spmd(_NC, maps, core_ids=list(range(len(maps))))
    return np.stack([np.asarray(r["out"], dtype=np.float32) for r in res.results], axis=0)
```

```python
import contextlib
import numpy as np
import concourse.bass as bass
import concourse.mybir as mybir
from concourse.bass_utils import run_bass_kernel_spmd

F32 = mybir.dt.float32
BF16 = mybir.dt.bfloat16
AF = mybir.ActivationFunctionType
ALU = mybir.AluOpType

ENGS = ("pe", "act", "dve", "pool", "sp")
ENGN = {"pe": "tensor", "act": "scalar", "dve": "vector", "pool": "gpsimd", "sp": "sync"}
SEM_CAP = 30000
DMA_ROT = 8
NSEM = 96

D = 1024
SEQ = 4096
CTX = 256
TOK = SEQ + CTX
DEPTH = 2
HID = 2816
NHC = HID // 128
NORM_EPS = 1e-6
GN_EPS = 1e-5
NEGM = -30000.0
TILES = [(i * 512, 512, 0) for i in range(8)] + [(SEQ, CTX, 1)]
DEBUG_OUTS = ()


class Res:
    __slots__ = ("w", "r")

    def __init__(self):
        self.w = None
        self.r = {}


class Prog:
    def __init__(self, nc, sems):
        self.nc = nc
        self.sems = sems
        self.semmap = {}
        self.q = {e: [] for e in ENGS}
        self.stream = {}
        self.waited = {e: {} for e in ENGS}
        self.dma_rot = {e: 0 for e in ENGS}
        self.out_events = []

    def _sem(self, key):
        if key not in self.semmap:
            self.semmap[key] = self.sems[len(self.semmap)]
        return self.semmap[key]

    def _next_event(self, stream, inc):
        st = self.stream.setdefault(stream, [0, 0])
        if st[1] + inc > SEM_CAP:
            st[0] += 1
            st[1] = 0
        st[1] += inc
        return ((stream, st[0]), st[1])

    def _peek_prev(self, stream):
        st = self.stream.get(stream)
        if st is None or st[1] == 0:
            return None
        return ((stream, st[0]), st[1])

    def _waits(self, eng, reads, writes, extra=()):
        need = {}

        def add(ev):
            if ev is None:
                return
            k, v = ev
            if need.get(k, 0) < v:
                need[k] = v
        for r in reads:
            add(r.w)
        for w in writes:
            add(w.w)
            for k, v in w.r.items():
                add((k, v))
        for ev in extra:
            add(ev)
        wd = self.waited[eng]
        for k, v in need.items():
            if eng == "pe" and k[0] == "c:pe":
                continue
            if wd.get(k, 0) >= v:
                continue
            wd[k] = v
            self.q[eng].append(("wait", k, v))

    def _mark(self, ev, reads, writes):
        k, v = ev
        for r in reads:
            if r.r.get(k, 0) < v:
                r.r[k] = v
        for w in writes:
            w.w = ev
            w.r = {}

    def op(self, eng, fn, reads=(), writes=()):
        self._waits(eng, reads, writes)
        ev = self._next_event("c:" + eng, 1)
        self.q[eng].append(("op", fn, ev[0]))
        self._mark(ev, reads, writes)

    def group(self, eng, fns, reads=(), writes=()):
        self._waits(eng, reads, writes)
        ev = self._next_event("c:" + eng, 1)
        for f in fns[:-1]:
            self.q[eng].append(("op", f, None))
        self.q[eng].append(("op", fns[-1], ev[0]))
        self._mark(ev, reads, writes)

    def dma(self, eng, out, in_, reads=(), writes=(), is_output=False, **kw):
        j = self.dma_rot[eng]
        self.dma_rot[eng] = (j + 1) % DMA_ROT
        stream = "d:%s:%d" % (eng, j)
        prev = self._peek_prev(stream)
        self._waits(eng, reads, writes, extra=(prev,) if prev else ())
        ev = self._next_event(stream, 16)
        self.q[eng].append(("dma", out, in_, ev[0], kw))
        self._mark(ev, reads, writes)
        if is_output:
            self.out_events.append(ev)

    def barrier(self):
        for e in ENGS:
            for stream, st in self.stream.items():
                if st[1] == 0 or (e == "pe" and stream == "c:pe"):
                    continue
                key, v = (stream, st[0]), st[1]
                if self.waited[e].get(key, 0) < v:
                    self.waited[e][key] = v
                    self.q[e].append(("wait", key, v))

    def flush(self, final=False):
        self.barrier()
        if final:
            for k, v in self.out_events:
                if self.waited["sp"].get(k, 0) < v:
                    self.waited["sp"][k] = v
                    self.q["sp"].append(("wait", k, v))
        nc = self.nc
        with nc.Block() as block:
            for e in ENGS:
                ql = self.q[e]

                def run(engine, ql=ql):
                    pend = []
                    for it in ql:
                        if it[0] == "wait":
                            pend.append(it)
                            continue
                        for w in pend[:-1]:
                            engine.wait_ge(self._sem(w[1]), w[2])
                        if it[0] == "op":
                            ins = it[1](engine)
                            if pend:
                                ins._wait_ge(self._sem(pend[-1][1]), pend[-1][2])
                            if it[2] is not None:
                                ins.then_inc(self._sem(it[2]), 1)
                        else:
                            _, out, in_, k, kw = it
                            ins = engine.dma_start(out=out, in_=in_, **kw)
                            if pend:
                                ins._wait_ge(self._sem(pend[-1][1]), pend[-1][2])
                            ins.then_inc(self._sem(k), 16)
                        pend = []
                    for w in pend:
                        engine.wait_ge(self._sem(w[1]), w[2])
                getattr(block, ENGN[e])(run)
        self.q = {e: [] for e in ENGS}


class T:
    def __init__(self, h):
        self.h = h
        self.r = Res()

    def __getitem__(self, k):
        return self.h[k]


def build_program(depth=DEPTH, debug_outs=(), stop_after=None):
    nc = bass.Bass("TRN2", target_bir_lowering=False)

    def din(name, shape, dt=F32):
        return nc.dram_tensor(name, list(shape), dt, kind="ExternalInput").ap()

    dres = {}

    def dscr(name, shape, dt):
        kind = "ExternalOutput" if name in debug_outs else "Internal"
        ap = nc.dram_tensor(name, list(shape), dt, kind=kind).ap()
        dres[name] = Res()
        return ap

    x_in = din("x", [SEQ, D])
    ctx_in = din("ctx", [CTX, D])
    cvec_in = din("cvec", [128, 8, 2])
    w_ada = din("w_ada", [DEPTH, D, 6 * D])
    b_ada = din("b_ada", [DEPTH, 1, 6 * D])
    vecs_in = din("vecs", [128, DEPTH, 32])
    logit_in = din("logits", [128, DEPTH, 8])
    w_in = din("w_in", [DEPTH, D, 6656])
    pool_w = din("pool_w", [DEPTH, 4, 128, 128])
    w_ba = din("w_branch_a", [DEPTH, 512, D])
    w_bb = din("w_branch_b", [DEPTH, 512, D])
    w_bc = din("w_branch_c", [DEPTH, 512, D])
    w_out = din("w_out", [DEPTH, D, D])
    w_fg = din("w_ffn_gate", [DEPTH, D, HID])
    w_fu = din("w_ffn_up", [DEPTH, D, HID])
    w_fd = din("w_ffn_down", [DEPTH, HID, D])
    tl_in = din("tl", [DEPTH, 128, 8 * 22 * 64])
    ident_in = din("ident", [128, 128])
    amask_in = din("amask", [16, 1024])
    bmask_in = din("bmask", [16, 8, 512])
    ropeC_in = din("ropeC", [128, SEQ])
    ropeS_in = din("ropeS", [128, SEQ])
    rc_in = din("rconst", [128, 768])
    prc_in = din("prc", [4, SEQ])
    prcc_in = din("prcc", [4, CTX])
    out_ap = nc.dram_tensor("out", [SEQ, D], F32, kind="ExternalOutput").ap()
    r_out = Res()

    hT = dscr("hT", [8, 128, TOK], F32)
    QT = dscr("QT", [512, TOK], BF16)
    KT = dscr("KT", [512, TOK], BF16)
    VV = dscr("VV", [TOK, 512], BF16)
    RQT = dscr("RQT", [256, TOK], BF16)
    RKT = dscr("RKT", [256, TOK], BF16)
    KZ = dscr("KZ", [TOK, 512], BF16)
    RV = dscr("RV", [TOK, 512], BF16)
    GT = dscr("GT", [512, TOK], BF16)
    SG = dscr("SG", [3, D, TOK], BF16)
    PT = dscr("PT", [512, TOK], BF16)
    AT = dscr("AT", [512, TOK], BF16)
    RT = dscr("RT", [512, TOK], BF16)
    ACTT = dscr("ACTT", [NHC, 128, TOK], BF16)
    XNd = dscr("XNd", [8, 128, TOK], BF16)

    with contextlib.ExitStack() as top:
        sems = [top.enter_context(nc.semaphore("s%d" % i)) for i in range(NSEM)]
        P = Prog(nc, sems)

        uniq = [0]

        def sb(es, name, shape, dt):
            uniq[0] += 1
            return T(es.enter_context(nc.sbuf_tensor("%s_%d" % (name, uniq[0]), list(shape), dt)))

        def psum(es, name, shape, dt=F32):
            return T(es.enter_context(nc.psum_tensor(name, list(shape), dt)))

        ps = [psum(top, "ps%d" % i, [128, 512]) for i in range(7)]
        psb = psum(top, "psb", [128, 1024], BF16)
        ident = sb(top, "ident", [128, 128], F32)
        identb = sb(top, "identb", [128, 128], BF16)
        onesb = sb(top, "onesb", [128, 128], BF16)
        ones1 = sb(top, "ones1", [128, 64], F32)
        mods = sb(top, "mods", [128, DEPTH, 48, 2], F32)
        vecs = sb(top, "vecs", [128, DEPTH, 32], F32)
        gs = sb(top, "gs", [128, DEPTH, 2, 8, 2], F32)
        gfin = sb(top, "gfin", [128, 8], F32)
        XNh = [None]
        rcn = sb(top, "rcn", [128, 768], F32)
        DT = sb(top, "DT", [128, 4, 128], F32)
        ZT = sb(top, "ZT", [128, 4, 2, 64], F32)
        XI = sb(top, "XI", [128, 4, 128], F32)
        DEC = sb(top, "DEC", [128, 4, 128], F32)
        lg = sb(top, "lg", [128, 8], F32)

        psi = [0]

        def nps():
            psi[0] = (psi[0] + 1) % 7
            return ps[psi[0]]

        P.dma("sp", ident[:], ident_in, writes=[ident.r])
        P.dma("pool", identb[:], ident_in, writes=[identb.r])
        P.dma("sp", vecs[:], vecs_in, writes=[vecs.r])
        P.dma("sp", rcn[:], rc_in, writes=[rcn.r])
        P.op("dve", lambda e: e.memset(onesb[:], 1.0), writes=[onesb.r])
        P.op("dve", lambda e: e.memset(ones1[:], 1.0), writes=[ones1.r])
        epsD = sb(top, "epsD", [128, 2], F32)
        P.op("dve", lambda e: e.memset(epsD[:, 0:1], float(D * NORM_EPS)), writes=[epsD.r])
        P.op("dve", lambda e: e.memset(epsD[:, 1:2], float(GN_EPS)), writes=[epsD.r])

        def mod(l, j, kc, s):
            return mods[:, l, j * 8 + kc, s:s + 1]

        with contextlib.ExitStack() as es:
            cv = sb(es, "cv", [128, 8, 2], F32)
            sc = sb(es, "sc", [128, 8, 2], F32)
            ba = sb(es, "ba", [2, DEPTH, 6 * D], F32)
            mrow = sb(es, "mrow", [2, DEPTH, 6 * D], F32)
            wa = [sb(es, "wa%d" % i, [128, 8, 256], F32) for i in range(3)]
            xt = [sb(es, "xt%d" % i, [128, D], F32) for i in range(2)]
            hst = [sb(es, "hst%d" % i, [128, 8, 512], F32) for i in range(2)]
            tmpm = sb(es, "tmpm", [128, 8, 2], F32)
            P.dma("sp", cv[:], cvec_in, writes=[cv.r])
            P.dma("sp", ba[:], b_ada.rearrange("l o n -> o l n").broadcast_to([2, DEPTH, 6 * D]), writes=[ba.r])
            P.op("act", lambda e: e.activation(sc[:], cv[:], AF.Silu), reads=[cv.r], writes=[sc.r])
            pm = ps[0]

            NAG = 24

            def ada_group(l, cg):
                wav = w_ada[l].rearrange("(p k) n -> p k n", k=8)
                wt = wa[cg % 3]
                c0 = cg * 256
                hf = (cg % 2) * 256
                P.dma("sp", wt[:], wav[:, :, c0:c0 + 256], writes=[wt.r])
                P.group("pe", [lambda e, k=k: e.matmul(
                    pm[0:2, hf:hf + 256], sc[:, k, :], wt[:, k, :], start=(k == 0), stop=(k == 7)) for k in range(8)],
                    reads=[wt.r, sc.r], writes=[pm.r])
                P.op("dve", lambda e: e.tensor_copy(mrow[0:2, l, c0:c0 + 256], pm[0:2, hf:hf + 256]),
                     reads=[pm.r], writes=[mrow.r])

            for cg in range(NAG):
                ada_group(0, cg)
            pending = [(l, cg) for l in range(1, depth) for cg in range(NAG)]
            xb = [0]

            def xps():
                xb[0] = xb[0] % 6 + 1
                return ps[xb[0]]
            cnt = 0
            for ti, (t0, n, s) in enumerate(TILES):
                hs = hst[ti % 2]
                for sub in range(n // 128):
                    xx = xt[cnt % 2]
                    cnt += 1
                    src = x_in[t0 + sub * 128:t0 + (sub + 1) * 128, :] if s == 0 else \
                        ctx_in[sub * 128:(sub + 1) * 128, :]
                    P.dma("sp", xx[:], src, writes=[xx.r])
                    for half in range(2):
                        pt = xps()
                        for kk in range(4):
                            kc = half * 4 + kk
                            P.op("pe", lambda e, pt=pt, kk=kk, kc=kc, xx=xx: e.transpose(
                                pt[:, kk * 128:(kk + 1) * 128], xx[:, kc * 128:(kc + 1) * 128], ident[:]),
                                reads=[xx.r, ident.r], writes=[pt.r])
                        if half == 0:
                            P.op("dve", lambda e, pt=pt, hs=hs, half=half, sub=sub: e.tensor_copy(
                                hs[:, half * 4:half * 4 + 4, sub * 128:(sub + 1) * 128],
                                pt[:].rearrange("p (k t) -> p k t", k=4)), reads=[pt.r], writes=[hs.r])
                        else:
                            P.op("act", lambda e, pt=pt, hs=hs, half=half, sub=sub: e.activation(
                                hs[:, half * 4:half * 4 + 4, sub * 128:(sub + 1) * 128],
                                pt[:].rearrange("p (k t) -> p k t", k=4), AF.Identity), reads=[pt.r], writes=[hs.r])
                    if pending and cnt % 2 == 0:
                        ada_group(*pending.pop(0))
                P.dma("pool", hT[:, :, t0:t0 + n].rearrange("k p t -> p k t"), hs[:, :, 0:n],
                      reads=[hs.r], writes=[dres["hT"]])
            while pending:
                ada_group(*pending.pop(0))
            P.op("dve", lambda e: e.tensor_tensor(mrow[:], mrow[:], ba[:], op=ALU.add),
                 reads=[mrow.r, ba.r], writes=[mrow.r])
            fns = []
            for l in range(depth):
                for m in range(48):
                    col = (l * 48 + m) * 2
                    fns.append(lambda e, l=l, m=m, col=col: e.matmul(
                        pm[:, col:col + 2], mrow[0:2, l, m * 128:(m + 1) * 128], ident[0:2, 0:2],
                        start=True, stop=True))
            P.group("pe", fns, reads=[mrow.r, ident.r], writes=[pm.r])
            P.op("dve", lambda e: e.tensor_copy(
                mods[:, 0:depth].rearrange("p l m s -> p (l m s)"), pm[:, 0:depth * 96]),
                reads=[pm.r], writes=[mods.r])
            for l in range(depth):
                for w in range(2):
                    sc0 = 8 + 24 * w
                    P.op("dve", lambda e, l=l, sc0=sc0: e.tensor_scalar(
                        tmpm[:], mods[:, l, sc0:sc0 + 8, :], 1.0, 32.0, op0=ALU.add, op1=ALU.mult),
                        reads=[mods.r], writes=[tmpm.r])
                    for s in range(2):
                        P.op("dve", lambda e, l=l, w=w, s=s: e.tensor_tensor(
                            gs[:, l, w, :, s], tmpm[:, :, s], vecs[:, l, 8 * w:8 * w + 8], op=ALU.mult),
                            reads=[tmpm.r, vecs.r], writes=[gs.r])
            P.op("dve", lambda e: e.tensor_scalar(gfin[:], vecs[:, 0, 24:32], 32.0, None, op0=ALU.mult),
                 reads=[vecs.r], writes=[gfin.r])
            P.flush()

        def norm_phase(l, w, tiles, dump=False):
            with contextlib.ExitStack() as es:
                hb = [sb(es, "nh%d" % i, [128, 8, 512], F32) for i in range(2)]
                sqs = [sb(es, "nsq%d" % i, [128, 8, 512], BF16) for i in range(2)]
                rss = [sb(es, "nrs%d" % i, [128, 512], F32) for i in range(2)]
                tmp = [sb(es, "ntm%d" % i, [128, 512], F32) for i in range(4)]
                def stage_a(ti):
                    t0, n, s = tiles[ti]
                    ht, sq, rs = hb[ti % 2], sqs[ti % 2], rss[ti % 2]
                    P.dma("sp", ht[:, :, 0:n], hT[:, :, t0:t0 + n].rearrange("k p t -> p k t"),
                          reads=[dres["hT"]], writes=[ht.r])
                    P.op("act", lambda e: e.activation(sq[:, :, 0:n], ht[:, :, 0:n], AF.Square),
                         reads=[ht.r], writes=[sq.r])
                    pp = nps()
                    P.group("pe", [lambda e, k=k: e.matmul(
                        pp[:, 0:n], onesb[:], sq[:, k, 0:n], start=(k == 0), stop=(k == 7)) for k in range(8)],
                        reads=[onesb.r, sq.r], writes=[pp.r])
                    P.op("act", lambda e: e.activation(
                        rs[:, 0:n], pp[:, 0:n], AF.Sqrt, bias=epsD[:, 0:1]), reads=[pp.r, epsD.r], writes=[rs.r])
                    P.op("dve", lambda e: e.reciprocal(rs[:, 0:n], rs[:, 0:n]), reads=[rs.r], writes=[rs.r])

                def stage_b(ti):
                    t0, n, s = tiles[ti]
                    ht, rs = hb[ti % 2], rss[ti % 2]
                    for kc in range(8):
                        tm = tmp[kc % 4]
                        P.op("dve", lambda e, tm=tm, kc=kc: e.scalar_tensor_tensor(
                            tm[:, 0:n], ht[:, kc, 0:n], gs[:, l, w, kc, s:s + 1], rs[:, 0:n],
                            op0=ALU.mult, op1=ALU.mult), reads=[ht.r, gs.r, rs.r], writes=[tm.r])
                        P.op("act", lambda e, tm=tm, kc=kc: e.activation(
                            XNh[0][:, kc, t0:t0 + n], tm[:, 0:n], AF.Identity, bias=mod(l, 3 * w, kc, s)),
                            reads=[tm.r, mods.r], writes=[XNh[0].r])
                    if dump:
                        P.dma("act", XNd[:, :, t0:t0 + n].rearrange("k p t -> p k t"), XNh[0][:, :, t0:t0 + n],
                              reads=[XNh[0].r], writes=[dres["XNd"]])

                stage_a(0)
                for ti in range(len(tiles)):
                    if ti + 1 < len(tiles):
                        stage_a(ti + 1)
                    stage_b(ti)
                P.flush()

        def retention_tables(l):
            with contextlib.ExitStack() as es:
                lgt = sb(es, "lgt", [128, 8], F32)
                t1 = sb(es, "rt1", [128, 128], F32)
                t2 = sb(es, "rt2", [128, 128], F32)
                c128 = sb(es, "c128", [128, 128], F32)
                P.dma("sp", lgt[:], logit_in[:, l, :], writes=[lgt.r])
                P.op("dve", lambda e: e.memset(c128[:], 128.0), writes=[c128.r])
                P.op("act", lambda e: e.activation(lgt[:], lgt[:], AF.Exp, scale=-1.0), reads=[lgt.r], writes=[lgt.r])
                P.op("act", lambda e: e.activation(lgt[:], lgt[:], AF.Ln, bias=1.0), reads=[lgt.r], writes=[lgt.r])
                P.op("dve", lambda e: e.tensor_scalar(lg[:], lgt[:], -1.0, None, op0=ALU.mult),
                     reads=[lgt.r], writes=[lg.r])
                EF, MF, EB, MB = (rcn[:, 0:128], rcn[:, 128:256], rcn[:, 256:384], rcn[:, 384:512])
                ZEf, ZEb, XE = rcn[:, 512:576], rcn[:, 576:640], rcn[:, 640:768]
                for h in range(4):
                    lf = lg[:, h:h + 1]
                    lb = lg[:, 4 + h:5 + h]
                    P.op("act", lambda e, lf=lf: e.activation(t1[:], EF, AF.Exp, scale=lf),
                         reads=[lg.r, rcn.r], writes=[t1.r])
                    P.op("dve", lambda e: e.tensor_tensor(t1[:], t1[:], MF, op=ALU.mult),
                         reads=[t1.r, rcn.r], writes=[t1.r])
                    P.op("act", lambda e, lb=lb: e.activation(t2[:], EB, AF.Exp, scale=lb),
                         reads=[lg.r, rcn.r], writes=[t2.r])
                    P.op("dve", lambda e: e.tensor_tensor(t2[:], t2[:], MB, op=ALU.mult),
                         reads=[t2.r, rcn.r], writes=[t2.r])
                    P.op("dve", lambda e, h=h: e.tensor_tensor(DT[:, h, :], t1[:], t2[:], op=ALU.add),
                         reads=[t1.r, t2.r], writes=[DT.r])
                    P.op("act", lambda e, h=h, lf=lf: e.activation(ZT[:, h, 0, :], ZEf, AF.Exp, scale=lf),
                         reads=[lg.r, rcn.r], writes=[ZT.r])
                    P.op("act", lambda e, h=h, lb=lb: e.activation(ZT[:, h, 1, :], ZEb, AF.Exp, scale=lb),
                         reads=[lg.r, rcn.r], writes=[ZT.r])
                    P.op("act", lambda e, h=h: e.activation(XI[0:64, h, :], rcn[0:64, 640:768], AF.Exp,
                                                            scale=lg[0:64, h:h + 1]),
                         reads=[lg.r, rcn.r], writes=[XI.r])
                    P.op("act", lambda e, h=h: e.activation(XI[64:128, h, :], rcn[64:128, 640:768], AF.Exp,
                                                            scale=lg[64:128, 4 + h:5 + h]),
                         reads=[lg.r, rcn.r], writes=[XI.r])
                    P.op("act", lambda e, h=h: e.activation(DEC[0:64, h, :], c128[0:64, :], AF.Exp,
                                                            scale=lg[0:64, h:h + 1]),
                         reads=[lg.r, c128.r], writes=[DEC.r])
                    P.op("act", lambda e, h=h: e.activation(DEC[64:128, h, :], c128[64:128, :], AF.Exp,
                                                            scale=lg[64:128, 4 + h:5 + h]),
                         reads=[lg.r, c128.r], writes=[DEC.r])
                P.flush()

        def proj_phase(l):
            with contextlib.ExitStack() as es:
                wb = [sb(es, "pw%d" % i, [128, 8, 512], BF16) for i in range(2)]
                wrot = sb(es, "pwrot", [128, 8, 512], BF16)
                stg = [sb(es, "pst%d" % i, [128, 4, 512], BF16) for i in range(2)]
                rC = sb(es, "ropeC", [128, 512], F32)
                rS = sb(es, "ropeS", [128, 512], F32)
                r1 = [sb(es, "pr1%d" % i, [128, 512], F32) for i in range(2)]
                r2 = [sb(es, "pr2%d" % i, [128, 512], F32) for i in range(2)]
                kzs = sb(es, "kzs", [128, 4, 512], BF16)
                wv = w_in[l].rearrange("(k p) n -> p k n", p=128)
                wcnt = [0]

                def loadw(c0, width=512):
                    wt = wb[wcnt[0] % 2]
                    wcnt[0] += 1
                    P.dma("pool", wt[:, :, 0:width], wv[:, :, c0:c0 + width], writes=[wt.r])
                    return wt

                scnt = [0]

                def nstg():
                    scnt[0] += 1
                    return stg[scnt[0] % 2]

                def fm_mm(pp, wt, c, t0, n):
                    P.group("pe", [lambda e, k=k: e.matmul(
                        pp[:, 0:n], wt[:, k, c * 128:(c + 1) * 128], XNh[0][:, k, t0:t0 + n],
                        start=(k == 0), stop=(k == 7)) for k in range(8)],
                        reads=[wt.r, XNh[0].r], writes=[pp.r])

                def fm_group(c0, dst, func, scale=1.0, nchunks=4, tiles=TILES):
                    wt = loadw(c0, nchunks * 128)
                    for (t0, n, s) in tiles:
                        st = nstg()
                        for c in range(nchunks):
                            pp = nps()
                            fm_mm(pp, wt, c, t0, n)
                            P.op("act", lambda e, st=st, c=c, pp=pp, n=n: e.activation(
                                st[:, c, 0:n], pp[:, 0:n], func, scale=scale), reads=[pp.r], writes=[st.r])
                        P.dma("act", dst[:, t0:t0 + n].rearrange("(c p) t -> p c t", p=128), st[:, 0:nchunks, 0:n],
                              reads=[st.r], writes=[dres_of[id(dst)]])

                def tm_group(c0, dst, name):
                    wt = loadw(c0)
                    for (t0, n, s) in TILES:
                        st = nstg()
                        for sub in range(n // 128):
                            pp = nps()
                            P.group("pe", [lambda e, k=k, pp=pp, sub=sub, t0=t0: e.matmul(
                                pp[:, :], XNh[0][:, k, t0 + sub * 128:t0 + (sub + 1) * 128], wt[:, k, :],
                                start=(k == 0), stop=(k == 7)) for k in range(8)],
                                reads=[wt.r, XNh[0].r], writes=[pp.r])
                            P.op("act", lambda e, st=st, sub=sub, pp=pp: e.activation(
                                st[:, sub, :], pp[:, :], AF.Identity), reads=[pp.r], writes=[st.r])
                        P.dma("act", dst[t0:t0 + n, :].rearrange("(s p) c -> p s c", p=128), st[:, 0:n // 128, :],
                              reads=[st.r], writes=[dres[name]])

                dres_of = {id(QT): dres["QT"], id(KT): dres["KT"], id(GT): dres["GT"]}
                fm_group(0, QT, AF.Identity, scale=0.125)
                fm_group(512, KT, AF.Identity)
                tm_group(1024, VV, "VV")
                tm_group(2048, RV, "RV")
                fm_group(2560, GT, AF.Silu)
                for b in range(3):
                    for hf in range(2):
                        dst = SG[b, hf * 512:(hf + 1) * 512, :]
                        dres_of[id(dst)] = dres["SG"]
                        fm_group(3584 + b * 1024 + hf * 512, dst, AF.Sigmoid)

                wt = loadw(1536)
                wvw = wt[:].rearrange("p k (b two i) -> p (k b) two i", two=2, i=16)
                wvr = wrot[:].rearrange("p k (b two i) -> p (k b) two i", two=2, i=16)
                P.op("dve", lambda e: e.tensor_copy(wvr[:, :, 0, :], wvw[:, :, 1, :]), reads=[wt.r], writes=[wrot.r])
                P.op("dve", lambda e: e.tensor_copy(wvr[:, :, 1, :], wvw[:, :, 0, :]), reads=[wt.r], writes=[wrot.r])
                for (t0, n, s) in TILES:
                    if s == 0:
                        P.dma("sp", rC[:], ropeC_in[:, t0:t0 + n], writes=[rC.r])
                        P.dma("sp", rS[:], ropeS_in[:, t0:t0 + n], writes=[rS.r])
                    st = nstg()
                    for c in range(4):
                        ksc = 1.0 if c < 2 else 0.125
                        pp = nps()
                        fm_mm(pp, wt, c, t0, n)
                        if s == 0:
                            pr = nps()
                            fm_mm(pr, wrot, c, t0, n)
                            a1 = r1[c % 2]
                            a2 = r2[c % 2]
                            P.op("dve", lambda e, a1=a1, pp=pp, ksc=ksc: e.scalar_tensor_tensor(
                                a1[:], pp[:], ksc, rC[:], op0=ALU.mult, op1=ALU.mult),
                                reads=[pp.r, rC.r], writes=[a1.r])
                            P.op("dve", lambda e, a2=a2, pr=pr, ksc=ksc: e.scalar_tensor_tensor(
                                a2[:], pr[:], ksc, rS[:], op0=ALU.mult, op1=ALU.mult),
                                reads=[pr.r, rS.r], writes=[a2.r])
                            P.op("dve", lambda e, st=st, c=c, a1=a1, a2=a2: e.tensor_tensor(
                                st[:, c, :], a1[:], a2[:], op=ALU.add), reads=[a1.r, a2.r], writes=[st.r])
                        else:
                            P.op("act", lambda e, st=st, c=c, pp=pp, n=n, ksc=ksc: e.activation(
                                st[:, c, 0:n], pp[:, 0:n], AF.Identity, scale=ksc), reads=[pp.r], writes=[st.r])
                    P.dma("act", RQT[:, t0:t0 + n].rearrange("(c p) t -> p c t", p=128), st[:, 0:2, 0:n],
                          reads=[st.r], writes=[dres["RQT"]])
                    P.dma("act", RKT[:, t0:t0 + n].rearrange("(c p) t -> p c t", p=128), st[:, 2:4, 0:n],
                          reads=[st.r], writes=[dres["RKT"]])
                    nsub = n // 128
                    for sub in range(nsub):
                        for c in range(2):
                            P.op("pe", lambda e, st=st, sub=sub, c=c: e.transpose(
                                psb[:, (sub * 2 + c) * 128:(sub * 2 + c + 1) * 128],
                                st[:, 2 + c, sub * 128:(sub + 1) * 128], identb[:]),
                                reads=[st.r, identb.r], writes=[psb.r])
                    for sub in range(nsub):
                        for d in range(2):
                            P.op("dve", lambda e, sub=sub, d=d: e.tensor_tensor(
                                kzs[:, sub, :].rearrange("p (h d k) -> p h d k", h=4, d=2)[:, :, d, :],
                                psb[:, sub * 256:(sub + 1) * 256].rearrange("p (h k) -> p h k", h=4),
                                ZT[:, :, d, :], op=ALU.mult), reads=[psb.r, ZT.r], writes=[kzs.r])
                    P.dma("act", KZ[t0:t0 + n, :].rearrange("(s p) c -> p s c", p=128), kzs[:, 0:nsub, :],
                          reads=[kzs.r], writes=[dres["KZ"]])
                P.flush()

        def pool_phase(l):
            with contextlib.ExitStack() as es:
                wt = sb(es, "plw", [128, 8, 512], BF16)
                pw = sb(es, "plpw", [128, 4, 128], BF16)
                U = sb(es, "plU", [128, TOK + 64], F32)
                A = sb(es, "plA", [128, TOK + 64], F32)
                B = sb(es, "plB", [128, TOK + 64], F32)
                rcp = sb(es, "plrc", [128, TOK], F32)
                pl = sb(es, "plpl", [128, TOK], BF16)
                st = [sb(es, "plst%d" % i, [128, 512], BF16) for i in range(2)]
                P.dma("pool", wt[:], w_in[l].rearrange("(k p) n -> p k n", p=128)[:, :, 3072:3584], writes=[wt.r])
                P.dma("pool", pw[:], pool_w[l].rearrange("g c d -> c g d"), writes=[pw.r])
                segs = [(16, SEQ, 0)] + ([(16 + SEQ + 32, CTX, SEQ)] if l == 0 else [])
                tiles = TILES if l == 0 else TILES[:8]
                for g in range(4):
                    w = (2, 4, 8, 16)[g]
                    P.dma("sp", rcp[:, 0:SEQ], prc_in[g:g + 1, :].broadcast_to([128, SEQ]), writes=[rcp.r])
                    if l == 0:
                        P.dma("sp", rcp[:, SEQ:TOK], prcc_in[g:g + 1, :].broadcast_to([128, CTX]), writes=[rcp.r])
                    if g == 0:
                        for buf in (U, A, B):
                            P.op("dve", lambda e, buf=buf: e.memset(buf[:], 0.0), writes=[buf.r])
                    for (t0, n, s) in tiles:
                        pp = nps()
                        P.group("pe", [lambda e, k=k, pp=pp, t0=t0, n=n, g=g: e.matmul(
                            pp[:, 0:n], wt[:, k, g * 128:(g + 1) * 128], XNh[0][:, k, t0:t0 + n],
                            start=(k == 0), stop=(k == 7)) for k in range(8)], reads=[wt.r, XNh[0].r], writes=[pp.r])
                        off = 16 + t0 if s == 0 else 16 + SEQ + 32
                        P.op("act", lambda e, pp=pp, off=off, n=n: e.activation(U[:, off:off + n], pp[:, 0:n], AF.Identity),
                             reads=[pp.r], writes=[U.r])
                    for (o, n, tb) in segs:
                        lo, hi = o - 12, o + n + 12
                        P.op("dve", lambda e, lo=lo, hi=hi: e.tensor_tensor(
                            A[:, lo:hi], U[:, lo - 1:hi - 1], U[:, lo:hi], op=ALU.add), reads=[U.r], writes=[A.r])
                        cur, oth = A, B
                        if w >= 4:
                            lo, hi = o - 10, o + n + 10
                            P.op("dve", lambda e, lo=lo, hi=hi: e.tensor_tensor(
                                B[:, lo:hi], A[:, lo - 1:hi - 1], A[:, lo + 1:hi + 1], op=ALU.add),
                                reads=[A.r], writes=[B.r])
                            cur, oth = B, A
                        if w >= 8:
                            lo, hi = o - 6, o + n + 6
                            P.op("dve", lambda e, lo=lo, hi=hi: e.tensor_tensor(
                                A[:, lo:hi], B[:, lo - 2:hi - 2], B[:, lo + 2:hi + 2], op=ALU.add),
                                reads=[B.r], writes=[A.r])
                            cur, oth = A, B
                        if w >= 16:
                            lo, hi = o, o + n
                            P.op("dve", lambda e, lo=lo, hi=hi: e.tensor_tensor(
                                B[:, lo:hi], A[:, lo - 4:hi - 4], A[:, lo + 4:hi + 4], op=ALU.add),
                                reads=[A.r], writes=[B.r])
                            cur, oth = B, A
                        P.op("dve", lambda e, cur=cur, o=o, n=n, tb=tb, w=w: e.scalar_tensor_tensor(
                            pl[:, tb:tb + n], cur[:, o:o + n], 1.0 / w, U[:, o:o + n], op0=ALU.mult, op1=ALU.subtract),
                            reads=[cur.r, U.r], writes=[pl.r])
                        for (eo, et) in ((o, tb), (o + n - 8, tb + n - 8)):
                            P.op("dve", lambda e, cur=cur, oth=oth, eo=eo, et=et: e.tensor_tensor(
                                oth[:, eo:eo + 8], cur[:, eo:eo + 8], rcp[:, et:et + 8], op=ALU.mult),
                                reads=[cur.r, rcp.r], writes=[oth.r])
                            P.op("dve", lambda e, oth=oth, eo=eo, et=et: e.tensor_tensor(
                                pl[:, et:et + 8], oth[:, eo:eo + 8], U[:, eo:eo + 8], op=ALU.subtract),
                                reads=[oth.r, U.r], writes=[pl.r])
                    for ti, (t0, n, s) in enumerate(tiles):
                        pp = nps()
                        P.group("pe", [lambda e, pp=pp, t0=t0, n=n, g=g: e.matmul(
                            pp[:, 0:n], pw[:, g, :], pl[:, t0:t0 + n], start=True, stop=True)],
                            reads=[pw.r, pl.r], writes=[pp.r])
                        so = st[ti % 2]
                        P.op("act", lambda e, so=so, pp=pp, n=n, g=g: e.activation(
                            so[:, 0:n], pp[:, 0:n], AF.Identity, scale=vecs[:, l, 20 + g:21 + g]),
                            reads=[pp.r, vecs.r], writes=[so.r])
                        P.dma("act", PT[g * 128:(g + 1) * 128, t0:t0 + n], so[:, 0:n], reads=[so.r], writes=[dres["PT"]])
                P.flush()

        def retention_phase(l):
            with contextlib.ExitStack() as es:
                ST = sb(es, "rST", [128, 34, 4, 128], BF16)
                Scur = sb(es, "rScur", [128, 4, 128], F32)
                stmp = sb(es, "rstmp", [128, 4, 128], F32)
                kzt = [sb(es, "rkz%d" % i, [128, 4, 512], BF16) for i in range(2)]
                rvt = [sb(es, "rrv%d" % i, [128, 4, 512], BF16) for i in range(2)]
                chunks_f = [32, 33] + list(range(32))
                chunks_b = [33, 32] + list(range(31, -1, -1))

                def tok_of(c):
                    return SEQ + (c - 32) * 128 if c >= 32 else c * 128

                lcnt = [0]

                def run_pass(chunks, lo, hi, Scur, stmp, kzt, rvt):
                    P.op("dve", lambda e: e.memset(Scur[lo:hi], 0.0), writes=[Scur.r])
                    loaded = {}
                    lc = 0
                    for idx, c in enumerate(chunks):
                        grp = c // 4 if c < 32 else 8
                        if grp not in loaded:
                            kz = kzt[lc % 2]
                            rv = rvt[lc % 2]
                            lc += 1
                            g0 = grp * 512 if grp < 8 else SEQ
                            ng = 4 if grp < 8 else 2
                            P.dma("sp", kz[:, 0:ng, :], KZ[g0:g0 + ng * 128, :].rearrange("(s p) c -> p s c", p=128),
                                  reads=[dres["KZ"]], writes=[kz.r])
                            P.dma("sp", rv[:, 0:ng, :], RV[g0:g0 + ng * 128, :].rearrange("(s p) c -> p s c", p=128),
                                  reads=[dres["RV"]], writes=[rv.r])
                            loaded = {grp: (kz, rv)}
                        kz, rv = loaded[grp]
                        ci = c % 4 if c < 32 else c - 32
                        pp = nps()
                        P.group("pe", [lambda e, h=h, pp=pp, kz=kz, rv=rv, ci=ci: e.matmul(
                            pp[:, h * 128:(h + 1) * 128], kz[:, ci, h * 128:(h + 1) * 128],
                            rv[:, ci, h * 128:(h + 1) * 128], start=True, stop=True) for h in range(4)],
                            reads=[kz.r, rv.r], writes=[pp.r])
                        P.op("act", lambda e, c=c: e.activation(ST[lo:hi, c], Scur[lo:hi], AF.Identity),
                             reads=[Scur.r], writes=[ST.r])
                        if c >= 32 and idx == 0:
                            P.op("dve", lambda e, pp=pp: e.tensor_copy(
                                Scur[lo:hi], pp[lo:hi, :].rearrange("p (h v) -> p h v", h=4)),
                                reads=[pp.r], writes=[Scur.r])
                        else:
                            P.op("dve", lambda e: e.tensor_tensor(stmp[lo:hi], Scur[lo:hi], DEC[lo:hi], op=ALU.mult),
                                 reads=[Scur.r, DEC.r], writes=[stmp.r])
                            P.op("dve", lambda e, pp=pp: e.tensor_tensor(
                                Scur[lo:hi], stmp[lo:hi], pp[lo:hi, :].rearrange("p (h v) -> p h v", h=4), op=ALU.add),
                                reads=[stmp.r, pp.r], writes=[Scur.r])
                        yield

                ScurB = sb(es, "rScurB", [128, 4, 128], F32)
                stmpB = sb(es, "rstmpB", [128, 4, 128], F32)
                kztB = [sb(es, "rkzB%d" % i, [128, 4, 512], BF16) for i in range(2)]
                rvtB = [sb(es, "rrvB%d" % i, [128, 4, 512], BF16) for i in range(2)]
                gf_ = run_pass(chunks_f, 0, 64, Scur, stmp, kzt, rvt)
                gb_ = run_pass(chunks_b, 64, 128, ScurB, stmpB, kztB, rvtB)
                for _ in range(len(chunks_f)):
                    next(gf_)
                    next(gb_)

                def two(name, shape, dt):
                    return [sb(es, "%s%d" % (name, i), shape, dt) for i in range(2)]
                QD = two("rQD", [128, 4, 512], BF16)
                KD = two("rKD", [64, 4, 512], BF16)
                RVt = two("rRVt", [128, 4, 512], BF16)
                GTt = two("rGT", [128, 4, 512], BF16)
                GG = two("rGG", [128, 4, 512], F32)
                rts = two("rrts", [128, 4, 512], BF16)
                ATt = two("rAT", [128, 4, 128], BF16)
                QX = two("rQX", [128, 4, 128], BF16)
                ob = two("rob", [128, 512], BF16)
                o2b = two("ro2b", [128, 512], BF16)
                mean = two("rmean", [128, 512], F32)
                msq = two("rmsq", [128, 512], F32)
                rstd = two("rrstd", [128, 512], F32)
                dd = two("rdd", [128, 512], F32)
                tiles = TILES if l == 0 else TILES[:8]
                clist = []
                for ti, (t0, n, s) in enumerate(tiles):
                    for ci in range(n // 128):
                        clist.append((ti, t0, n, s, ci))

                def load_tile(ti, t0, n, s):
                    b = ti % 2
                    nch = n // 128
                    for half in range(2):
                        P.dma("sp", QD[b][half * 64:(half + 1) * 64, :, 0:n],
                              RQT[:, t0:t0 + n].rearrange("(h d) t -> d h t", d=64),
                              reads=[dres["RQT"]], writes=[QD[b].r])
                    P.dma("sp", KD[b][:, :, 0:n], RKT[:, t0:t0 + n].rearrange("(h d) t -> d h t", d=64),
                          reads=[dres["RKT"]], writes=[KD[b].r])
                    P.dma("sp", RVt[b][:, 0:nch, :], RV[t0:t0 + n, :].rearrange("(s p) c -> p s c", p=128),
                          reads=[dres["RV"]], writes=[RVt[b].r])
                    P.dma("sp", GTt[b][:, :, 0:n], GT[:, t0:t0 + n].rearrange("(c p) t -> p c t", p=128),
                          reads=[dres["GT"]], writes=[GTt[b].r])
                    for h in range(4):
                        P.op("pool", lambda e, h=h, n=n, b=b: e.tensor_scalar(
                            GG[b][:, h, 0:n], GTt[b][:, h, 0:n], vecs[:, l, 16 + h:17 + h], None, op0=ALU.mult),
                            reads=[GTt[b].r, vecs.r], writes=[GG[b].r])

                def stage_a(idx):
                    ti, t0, n, s, ci = clist[idx]
                    if ci == 0:
                        load_tile(ti, t0, n, s)
                    b = ti % 2
                    q = idx % 2
                    o = ci * 128
                    c = (t0 // 128 + ci) if s == 0 else 32 + ci
                    pI = ps[0]
                    pO = ps[1 + q]
                    pM = ps[3 + q]
                    pQ = ps[5 + q]
                    P.group("pe", [lambda e, h=h, o=o, b=b: e.matmul(
                        pI[:, h * 128:(h + 1) * 128], KD[b][0:64, h, o:o + 128], QD[b][0:64, h, o:o + 128],
                        start=True, stop=True) for h in range(4)], reads=[KD[b].r, QD[b].r], writes=[pI.r])
                    P.op("dve", lambda e, q=q: e.tensor_tensor(
                        ATt[q][:], pI[:].rearrange("p (h i) -> p h i", h=4), DT[:], op=ALU.mult),
                        reads=[pI.r, DT.r], writes=[ATt[q].r])
                    P.op("dve", lambda e, o=o, b=b, q=q: e.tensor_tensor(QX[q][:], QD[b][:, :, o:o + 128], XI[:], op=ALU.mult),
                         reads=[QD[b].r, XI.r], writes=[QX[q].r])
                    fns = []
                    for h in range(4):
                        fns.append(lambda e, h=h, pO=pO, ci=ci, b=b, q=q: e.matmul(
                            pO[:, h * 128:(h + 1) * 128], RVt[b][:, ci, h * 128:(h + 1) * 128], ATt[q][:, h, :],
                            start=True, stop=False))
                        fns.append(lambda e, h=h, pO=pO, c=c, q=q: e.matmul(
                            pO[:, h * 128:(h + 1) * 128], ST[:, c, h, :], QX[q][:, h, :], start=False, stop=True))
                    P.group("pe", fns, reads=[RVt[b].r, ATt[q].r, ST.r, QX[q].r], writes=[pO.r])
                    P.op("act", lambda e, pO=pO, q=q: e.activation(ob[q][:], pO[:], AF.Identity), reads=[pO.r], writes=[ob[q].r])
                    P.op("act", lambda e, pO=pO, q=q: e.activation(o2b[q][:], pO[:], AF.Square), reads=[pO.r], writes=[o2b[q].r])
                    P.group("pe", [lambda e, pM=pM, q=q: e.matmul(pM[:], onesb[:], ob[q][:], start=True, stop=True)],
                            reads=[onesb.r, ob[q].r], writes=[pM.r])
                    P.group("pe", [lambda e, pQ=pQ, q=q: e.matmul(pQ[:], onesb[:], o2b[q][:], start=True, stop=True)],
                            reads=[onesb.r, o2b[q].r], writes=[pQ.r])

                def stage_b(idx):
                    ti, t0, n, s, ci = clist[idx]
                    b = ti % 2
                    q = idx % 2
                    o = ci * 128
                    pO = ps[1 + q]
                    pM = ps[3 + q]
                    pQ = ps[5 + q]
                    P.op("act", lambda e, q=q, pM=pM: e.activation(mean[q][:], pM[:], AF.Identity, scale=1.0 / 128),
                         reads=[pM.r], writes=[mean[q].r])
                    P.op("act", lambda e, q=q, pM=pM: e.activation(msq[q][:], pM[:], AF.Square, scale=1.0 / 128),
                         reads=[pM.r], writes=[msq[q].r])
                    P.op("dve", lambda e, q=q, pQ=pQ: e.scalar_tensor_tensor(
                        rstd[q][:], pQ[:], 1.0 / 128, msq[q][:], op0=ALU.mult, op1=ALU.subtract),
                        reads=[pQ.r, msq[q].r], writes=[rstd[q].r])
                    P.op("act", lambda e, q=q: e.activation(rstd[q][:], rstd[q][:], AF.Sqrt, bias=epsD[:, 1:2]),
                         reads=[rstd[q].r, epsD.r], writes=[rstd[q].r])
                    P.op("dve", lambda e, q=q: e.reciprocal(rstd[q][:], rstd[q][:]), reads=[rstd[q].r], writes=[rstd[q].r])
                    P.op("dve", lambda e, q=q, pO=pO: e.tensor_tensor(dd[q][:], pO[:], mean[q][:], op=ALU.subtract),
                         reads=[pO.r, mean[q].r], writes=[dd[q].r])
                    P.op("dve", lambda e, q=q: e.tensor_tensor(dd[q][:], dd[q][:], rstd[q][:], op=ALU.mult),
                         reads=[dd[q].r, rstd[q].r], writes=[dd[q].r])
                    P.op("dve", lambda e, o=o, b=b, q=q: e.tensor_tensor(
                        rts[b][:, :, o:o + 128], dd[q][:].rearrange("p (h i) -> p h i", h=4), GG[b][:, :, o:o + 128],
                        op=ALU.mult), reads=[dd[q].r, GG[b].r], writes=[rts[b].r])
                    if ci == n // 128 - 1:
                        P.dma("act", RT[:, t0:t0 + n].rearrange("(c p) t -> p c t", p=128), rts[b][:, :, 0:n],
                              reads=[rts[b].r], writes=[dres["RT"]])

                stage_a(0)
                for idx in range(len(clist)):
                    if idx + 1 < len(clist):
                        stage_a(idx + 1)
                    stage_b(idx)
                P.flush()

        def na_phase(l, pre=None):
            with contextlib.ExitStack() as es:
                TL = sb(es, "aTL", [128, 8, 22 * 64], BF16)
                BM = sb(es, "aBM", [80, 8, 512], BF16)
                QAs = [sb(es, "aQA%d" % i, [80, 8, 512], BF16) for i in range(2)]
                KAs = [sb(es, "aKA%d" % i, [80, 8, 1024], BF16) for i in range(2)]
                VAs = [sb(es, "aVA%d" % i, [128, 8, 8, 65], BF16) for i in range(2)]
                KC = sb(es, "aKC", [64, 8, CTX], BF16)
                VC = sb(es, "aVC", [128, 2, 8, 65], BF16)
                PTt = [sb(es, "aPT%d" % i, [128, 512], BF16) for i in range(4)]
                rden = [sb(es, "arden%d" % i, [65, 512], F32) for i in range(2)]
                bcs = [sb(es, "abcs%d" % i, [64, 512], F32) for i in range(2)]
                ast = [sb(es, "aast%d" % i, [64, 512], BF16) for i in range(2)]
                P.dma("pool", TL[:].rearrange("p h x -> p (h x)"), tl_in[l], writes=[TL.r])
                P.dma("pool", BM[64:80], bmask_in, writes=[BM.r])
                for KA in KAs:
                    for hh in range(8):
                        P.dma("pool", KA[64:80, hh, :], amask_in, writes=[KA.r])
                for VA in VAs:
                    P.op("pool", lambda e, VA=VA: e.memset(VA[:, :, :, 64:65], 1.0), writes=[VA.r])
                if pre is not None:
                    pre()
                P.op("pool", lambda e: e.memset(VC[:, :, :, 64:65], 1.0), writes=[VC.r])
                P.dma("sp", KC[:], KT[:, SEQ:TOK].rearrange("(h d) t -> d h t", d=64), reads=[dres["KT"]], writes=[KC.r])
                for cs in range(2):
                    P.dma("sp", VC[:, cs, :, 0:64],
                          VV[SEQ + cs * 128:SEQ + (cs + 1) * 128, :].rearrange("p (h d) -> p h d", d=64),
                          reads=[dres["VV"]], writes=[VC.r])
                tiles = TILES if l == 0 else TILES[:8]
                pcnt = 0
                for qt, (t0, n, s) in enumerate(tiles):
                    QA, KA, VA = QAs[qt % 2], KAs[qt % 2], VAs[qt % 2]
                    P.dma("sp", QA[0:64, :, 0:n], QT[:, t0:t0 + n].rearrange("(h d) t -> d h t", d=64),
                          reads=[dres["QT"]], writes=[QA.r])
                    slots = []
                    if s == 0:
                        R = qt * 8
                        P.op("pool", lambda e, qt=qt, QA=QA: e.tensor_copy(
                            QA[64:80, :, :], BM[64:80, qt:qt + 1, :].broadcast_to([16, 8, 512])),
                            reads=[BM.r], writes=[QA.r])
                        slots = [sp_ for sp_ in range(8) if 0 <= R - 4 + 2 * sp_ <= 62]
                        s_lo, s_hi = slots[0], slots[-1] + 1
                        k0 = (R - 4 + 2 * s_lo) * 64
                        nk = (s_hi - s_lo) * 128
                        P.dma("sp", KA[0:64, :, s_lo * 128:s_hi * 128],
                              KT[:, k0:k0 + nk].rearrange("(h d) t -> d h t", d=64), reads=[dres["KT"]], writes=[KA.r])
                        for sp_ in range(s_lo, s_hi):
                            kk0 = k0 + (sp_ - s_lo) * 128
                            P.dma("sp", VA[:, sp_, :, 0:64],
                                  VV[kk0:kk0 + 128, :].rearrange("p (h d) -> p h d", d=64),
                                  reads=[dres["VV"]], writes=[VA.r])
                    items = []
                    for h in range(8):
                        its = [("w", sp_) for sp_ in slots] + [("c", 0), ("c", 1)]
                        for ii, (kind, sp_) in enumerate(its):
                            items.append((h, kind, sp_, ii, len(its) - 1))

                    def emit_s(j):
                        h, kind, sp_, ii, last = items[j]
                        pS = ps[2 + j % 4]
                        if kind == "w":
                            w0 = 14 - 2 * sp_
                            P.group("pe", [
                                lambda e, pS=pS, h=h, sp_=sp_, n=n, KA=KA, QA=QA: e.matmul(
                                    pS[:, 0:n], KA[0:80, h, sp_ * 128:(sp_ + 1) * 128], QA[0:80, h, 0:n],
                                    start=True, stop=False),
                                lambda e, pS=pS, h=h, w0=w0, n=n: e.matmul(
                                    pS[:, 0:n], identb[:], TL[:, h, w0 * 64:w0 * 64 + n], start=False, stop=True)],
                                reads=[KA.r, QA.r, identb.r, TL.r], writes=[pS.r])
                        else:
                            P.group("pe", [lambda e, pS=pS, h=h, sp_=sp_, n=n, QA=QA: e.matmul(
                                pS[:, 0:n], KC[0:64, h, sp_ * 128:(sp_ + 1) * 128], QA[0:64, h, 0:n],
                                start=True, stop=True)], reads=[KC.r, QA.r], writes=[pS.r])

                    def emit_rest(j):
                        h, kind, sp_, ii, last = items[j]
                        pS = ps[2 + j % 4]
                        pO = ps[h % 2]
                        pt_ = PTt[j % 4]
                        P.op("act", lambda e, pt_=pt_, pS=pS, n=n: e.activation(pt_[:, 0:n], pS[:, 0:n], AF.Exp),
                             reads=[pS.r], writes=[pt_.r])
                        lhs = (lambda sp_=sp_, h=h, VA=VA: VA[:, sp_, h, :]) if kind == "w" else \
                            (lambda sp_=sp_, h=h: VC[:, sp_, h, :])
                        vres = VA.r if kind == "w" else VC.r
                        P.group("pe", [lambda e, pO=pO, lhs=lhs, pt_=pt_, n=n, ii=ii, last=last: e.matmul(
                            pO[0:65, 0:n], lhs(), pt_[:, 0:n], start=(ii == 0), stop=(ii == last))],
                            reads=[vres, pt_.r], writes=[pO.r])
                        if ii == last:
                            fin_q.append((j + 2, lambda h=h, pO=pO: finalize(h, pO)))

                    def finalize(h, pO):
                        if True:
                            rd = rden[h % 2]
                            bc = bcs[h % 2]
                            P.op("dve", lambda e, pO=pO, n=n, rd=rd: e.reciprocal(rd[64:65, 0:n], pO[64:65, 0:n]),
                                 reads=[pO.r], writes=[rd.r])
                            pB = ps[6]
                            P.group("pe", [lambda e, pB=pB, n=n, rd=rd: e.matmul(
                                pB[0:64, 0:n], ones1[64:65, 0:64], rd[64:65, 0:n], start=True, stop=True)],
                                reads=[ones1.r, rd.r], writes=[pB.r])
                            P.op("act", lambda e, pB=pB, n=n, bc=bc: e.activation(bc[:, 0:n], pB[0:64, 0:n], AF.Identity),
                                 reads=[pB.r], writes=[bc.r])
                            a_ = ast[h % 2]
                            P.op("dve", lambda e, a_=a_, pO=pO, n=n, bc=bc: e.tensor_tensor(
                                a_[:, 0:n], pO[0:64, 0:n], bc[:, 0:n], op=ALU.mult), reads=[pO.r, bc.r], writes=[a_.r])
                            P.dma("act", AT[h * 64:(h + 1) * 64, t0:t0 + n], a_[:, 0:n], reads=[a_.r], writes=[dres["AT"]])

                    LOOK = 3
                    fin_q = []
                    for j in range(min(LOOK, len(items))):
                        emit_s(j)
                    for j in range(len(items)):
                        if j + LOOK < len(items):
                            emit_s(j + LOOK)
                        emit_rest(j)
                        while fin_q and fin_q[0][0] <= j:
                            fin_q.pop(0)[1]()
                    while fin_q:
                        fin_q.pop(0)[1]()
                P.flush()

        def merge_weights(es, l):
            WB = [sb(es, "mWB%d" % i, [128, 4, D], BF16) for i in range(3)]
            WO = sb(es, "mWO", [128, 8, D], BF16)

            def load():
                for i, wsrc in enumerate((w_ba, w_bb, w_bc)):
                    P.dma("pool", WB[i][:], wsrc[l].rearrange("(k p) n -> p k n", p=128), writes=[WB[i].r])
                P.dma("pool", WO[:], w_out[l].rearrange("(k p) n -> p k n", p=128), writes=[WO.r])
            return WB, WO, load

        def merge_phase(l, WB, WO):
            with contextlib.ExitStack() as es:
                bt = [sb(es, "mbt%d" % i, [128, 4, 512], BF16) for i in range(3)]
                sg = [sb(es, "msg%d" % i, [128, 8, 512], BF16) for i in range(3)]
                ht = sb(es, "mht", [128, 8, 512], F32)
                ho = sb(es, "mho", [128, 8, 512], F32)
                mT = sb(es, "mmT", [128, 8, 512], BF16)
                tt = [sb(es, "mtt%d" % i, [128, 512], F32) for i in range(3)]
                tiles = TILES if l == 0 else TILES[:8]
                srcs = (AT, RT, PT)
                names = ("AT", "RT", "PT")
                for (t0, n, s) in tiles:
                    for i in range(3):
                        P.dma("sp", bt[i][:, :, 0:n], srcs[i][:, t0:t0 + n].rearrange("(c p) t -> p c t", p=128),
                              reads=[dres[names[i]]], writes=[bt[i].r])
                        P.dma("sp", sg[i][:, :, 0:n], SG[i, :, t0:t0 + n].rearrange("(c p) t -> p c t", p=128),
                              reads=[dres["SG"]], writes=[sg[i].r])
                    P.dma("sp", ht[:, :, 0:n], hT[:, :, t0:t0 + n].rearrange("k p t -> p k t"),
                          reads=[dres["hT"]], writes=[ht.r])
                    for mo in range(8):
                        for i in range(3):
                            pp = nps()
                            P.group("pe", [lambda e, k=k, i=i, pp=pp, mo=mo, n=n: e.matmul(
                                pp[:, 0:n], WB[i][:, k, mo * 128:(mo + 1) * 128], bt[i][:, k, 0:n],
                                start=(k == 0), stop=(k == 3)) for k in range(4)],
                                reads=[WB[i].r, bt[i].r], writes=[pp.r])
                            P.op("dve", lambda e, i=i, pp=pp, mo=mo, n=n: e.tensor_tensor(
                                tt[i][:, 0:n], pp[:, 0:n], sg[i][:, mo, 0:n], op=ALU.mult),
                                reads=[pp.r, sg[i].r], writes=[tt[i].r])
                        P.op("dve", lambda e, n=n: e.tensor_tensor(tt[0][:, 0:n], tt[0][:, 0:n], tt[1][:, 0:n], op=ALU.add),
                             reads=[tt[0].r, tt[1].r], writes=[tt[0].r])
                        P.op("dve", lambda e, mo=mo, n=n: e.tensor_tensor(
                            mT[:, mo, 0:n], tt[0][:, 0:n], tt[2][:, 0:n], op=ALU.add),
                            reads=[tt[0].r, tt[2].r], writes=[mT.r])
                    for mo in range(8):
                        pp = nps()
                        P.group("pe", [lambda e, k=k, pp=pp, mo=mo, n=n: e.matmul(
                            pp[:, 0:n], WO[:, k, mo * 128:(mo + 1) * 128], mT[:, k, 0:n],
                            start=(k == 0), stop=(k == 7)) for k in range(8)], reads=[WO.r, mT.r], writes=[pp.r])
                        P.op("dve", lambda e, pp=pp, mo=mo, n=n, s=s: e.scalar_tensor_tensor(
                            ho[:, mo, 0:n], pp[:, 0:n], mod(l, 2, mo, s), ht[:, mo, 0:n], op0=ALU.mult, op1=ALU.add),
                            reads=[pp.r, mods.r, ht.r], writes=[ho.r])
                    P.dma("act", hT[:, :, t0:t0 + n].rearrange("k p t -> p k t"), ho[:, :, 0:n],
                          reads=[ho.r], writes=[dres["hT"]])
                P.flush()

        def ffn1_phase(l, pre=None):
            with contextlib.ExitStack() as es:
                wg = [sb(es, "fwg%d" % i, [128, 8, 256], BF16) for i in range(2)]
                wu = [sb(es, "fwu%d" % i, [128, 8, 256], BF16) for i in range(2)]
                sgt = [sb(es, "fsg%d" % i, [128, 512], F32) for i in range(2)]
                act = [sb(es, "fac%d" % i, [128, 512], BF16) for i in range(3)]
                tiles = TILES if l == 0 else TILES[:8]
                gv = w_fg[l].rearrange("(k p) n -> p k n", p=128)
                uv = w_fu[l].rearrange("(k p) n -> p k n", p=128)
                cnt = 0
                for jg in range(NHC // 2):
                    g_ = wg[jg % 2]
                    u_ = wu[jg % 2]
                    P.dma("pool", g_[:], gv[:, :, jg * 256:(jg + 1) * 256], writes=[g_.r])
                    P.dma("pool", u_[:], uv[:, :, jg * 256:(jg + 1) * 256], writes=[u_.r])
                    if jg == 1 and pre is not None:
                        pre()
                    for jj in range(2):
                        j = jg * 2 + jj
                        for (t0, n, s) in tiles:
                            pg = nps()
                            P.group("pe", [lambda e, k=k, pg=pg, g_=g_, jj=jj, t0=t0, n=n: e.matmul(
                                pg[:, 0:n], g_[:, k, jj * 128:(jj + 1) * 128], XNh[0][:, k, t0:t0 + n],
                                start=(k == 0), stop=(k == 7)) for k in range(8)], reads=[g_.r, XNh[0].r], writes=[pg.r])
                            pu = nps()
                            P.group("pe", [lambda e, k=k, pu=pu, u_=u_, jj=jj, t0=t0, n=n: e.matmul(
                                pu[:, 0:n], u_[:, k, jj * 128:(jj + 1) * 128], XNh[0][:, k, t0:t0 + n],
                                start=(k == 0), stop=(k == 7)) for k in range(8)], reads=[u_.r, XNh[0].r], writes=[pu.r])
                            sg_ = sgt[cnt % 2]
                            ac = act[cnt % 3]
                            cnt += 1
                            P.op("act", lambda e, sg_=sg_, pg=pg, n=n: e.activation(sg_[:, 0:n], pg[:, 0:n], AF.Silu),
                                 reads=[pg.r], writes=[sg_.r])
                            P.op("dve", lambda e, ac=ac, sg_=sg_, pu=pu, n=n: e.tensor_tensor(
                                ac[:, 0:n], pu[:, 0:n], sg_[:, 0:n], op=ALU.mult), reads=[pu.r, sg_.r], writes=[ac.r])
                            P.dma("act", ACTT[j, :, t0:t0 + n], ac[:, 0:n], reads=[ac.r], writes=[dres["ACTT"]])
                P.flush()

        def ffn2_phase(l, WD):
            with contextlib.ExitStack() as es:
                at = [sb(es, "fat%d" % i, [128, NHC, 512], BF16) for i in range(2)]
                ht = [sb(es, "fht%d" % i, [128, 8, 512], F32) for i in range(2)]
                ho = sb(es, "fho", [128, 8, 512], F32)
                tiles = TILES if l == 0 else TILES[:8]
                for ti, (t0, n, s) in enumerate(tiles):
                    a_ = at[ti % 2]
                    h_ = ht[ti % 2]
                    P.dma("sp", a_[:, :, 0:n], ACTT[:, :, t0:t0 + n].rearrange("j p t -> p j t"),
                          reads=[dres["ACTT"]], writes=[a_.r])
                    P.dma("sp", h_[:, :, 0:n], hT[:, :, t0:t0 + n].rearrange("k p t -> p k t"),
                          reads=[dres["hT"]], writes=[h_.r])
                    for mo in range(8):
                        pp = nps()
                        P.group("pe", [lambda e, j=j, pp=pp, mo=mo, a_=a_, n=n: e.matmul(
                            pp[:, 0:n], WD[:, j, mo * 128:(mo + 1) * 128], a_[:, j, 0:n],
                            start=(j == 0), stop=(j == NHC - 1)) for j in range(NHC)], reads=[WD.r, a_.r], writes=[pp.r])
                        P.op("dve", lambda e, pp=pp, mo=mo, n=n, s=s, h_=h_: e.scalar_tensor_tensor(
                            ho[:, mo, 0:n], pp[:, 0:n], mod(l, 5, mo, s), h_[:, mo, 0:n], op0=ALU.mult, op1=ALU.add),
                            reads=[pp.r, mods.r, h_.r], writes=[ho.r])
                    P.dma("act", hT[:, :, t0:t0 + n].rearrange("k p t -> p k t"), ho[:, :, 0:n],
                          reads=[ho.r], writes=[dres["hT"]])
                P.flush()

        def out_phase():
            with contextlib.ExitStack() as es:
                hb = [sb(es, "oh%d" % i, [128, 8, 512], F32) for i in range(2)]
                sqs = [sb(es, "osq%d" % i, [128, 8, 512], BF16) for i in range(2)]
                rss = [sb(es, "ors%d" % i, [128, 512], F32) for i in range(2)]
                yys = [sb(es, "oyy%d" % i, [128, 8, 512], F32) for i in range(2)]
                ot = [sb(es, "oot%d" % i, [128, D], F32) for i in range(2)]
                cnt = 0
                def stage_a(ti):
                    t0, n, s = TILES[ti]
                    ht, sq, rs = hb[ti % 2], sqs[ti % 2], rss[ti % 2]
                    P.dma("sp", ht[:], hT[:, :, t0:t0 + n].rearrange("k p t -> p k t"), reads=[dres["hT"]], writes=[ht.r])
                    P.op("act", lambda e: e.activation(sq[:], ht[:], AF.Square), reads=[ht.r], writes=[sq.r])
                    pp = nps()
                    P.group("pe", [lambda e, k=k: e.matmul(pp[:], onesb[:], sq[:, k, :], start=(k == 0), stop=(k == 7))
                                   for k in range(8)], reads=[onesb.r, sq.r], writes=[pp.r])
                    P.op("act", lambda e: e.activation(rs[:], pp[:], AF.Sqrt, bias=epsD[:, 0:1]),
                         reads=[pp.r, epsD.r], writes=[rs.r])
                    P.op("dve", lambda e: e.reciprocal(rs[:], rs[:]), reads=[rs.r], writes=[rs.r])

                def stage_b(ti):
                    t0, n, s = TILES[ti]
                    ht, rs, yy = hb[ti % 2], rss[ti % 2], yys[ti % 2]
                    for kc in range(8):
                        P.op("dve", lambda e, kc=kc: e.scalar_tensor_tensor(
                            yy[:, kc, :], ht[:, kc, :], gfin[:, kc:kc + 1], rs[:], op0=ALU.mult, op1=ALU.mult),
                            reads=[ht.r, gfin.r, rs.r], writes=[yy.r])
                    for sub in range(4):
                        o_ = ot[(ti * 4 + sub) % 2]
                        for half in range(2):
                            pt = nps()
                            for kk in range(4):
                                kc = half * 4 + kk
                                P.op("pe", lambda e, pt=pt, kk=kk, kc=kc, sub=sub: e.transpose(
                                    pt[:, kk * 128:(kk + 1) * 128], yy[:, kc, sub * 128:(sub + 1) * 128], ident[:]),
                                    reads=[yy.r, ident.r], writes=[pt.r])
                            if half == 0:
                                P.op("dve", lambda e, o_=o_, pt=pt: e.tensor_copy(o_[:, 0:512], pt[:]),
                                     reads=[pt.r], writes=[o_.r])
                            else:
                                P.op("act", lambda e, o_=o_, pt=pt: e.activation(o_[:, 512:1024], pt[:], AF.Identity),
                                     reads=[pt.r], writes=[o_.r])
                        P.dma("pool", out_ap[t0 + sub * 128:t0 + (sub + 1) * 128, :], o_[:],
                              reads=[o_.r], writes=[r_out], is_output=True)

                stage_a(0)
                for ti in range(8):
                    if ti + 1 < 8:
                        stage_a(ti + 1)
                    stage_b(ti)
                P.flush(final=True)

        def with_xn(fns):
            with contextlib.ExitStack() as xs:
                XNh[0] = sb(xs, "XN", [128, 8, TOK], BF16)
                for f in fns:
                    f()
                XNh[0] = None

        for l in range(depth):
            last = (l == DEPTH - 1)
            retention_tables(l)
            with_xn([lambda: norm_phase(l, 0, TILES, dump=("XNd" in debug_outs)),
                     lambda: proj_phase(l), lambda: pool_phase(l)])
            retention_phase(l)
            with contextlib.ExitStack() as ms:
                WB, WO, mload = merge_weights(ms, l)
                na_phase(l, pre=mload)
                merge_phase(l, WB, WO)
            with contextlib.ExitStack() as fs:
                WD = sb(fs, "fWD", [128, NHC, D], BF16)

                def wdload():
                    P.dma("pool", WD[:], w_fd[l].rearrange("(j p) n -> p j n", p=128), writes=[WD.r])
                with_xn([lambda: norm_phase(l, 1, TILES[:8] if last else TILES),
                         lambda: ffn1_phase(l, pre=wdload)])
                ffn2_phase(l, WD)
        out_phase()
    return nc


def _const_tables():
    f32 = np.float32
    ident = np.eye(128, dtype=f32)
    amask = np.zeros((16, 1024), f32)
    for i in range(16):
        amask[i, i * 64:(i + 1) * 64] = 1.0
    bmask = np.full((16, 8, 512), NEGM, f32)
    for qt in range(8):
        R = qt * 8
        for i in range(16):
            kr = R - 4 + i
            for rho in range(8):
                r = R + rho
                r0 = min(max(r - 4, 0), 56)
                if 0 <= kr <= 63 and r0 <= kr < r0 + 8:
                    bmask[i, qt, rho * 64:(rho + 1) * 64] = 0.0
    t = np.arange(SEQ)
    row = (t // 64).astype(f32)
    col = (t % 64).astype(f32)
    inv = (10000.0 ** (-np.arange(16, dtype=f32) / 16)).astype(f32)
    C = np.zeros((64, SEQ), f32)
    S = np.zeros((64, SEQ), f32)
    for d in range(64):
        pos = row if d < 32 else col
        i = d % 16
        ang = (pos * inv[i]).astype(f32)
        C[d] = np.cos(ang)
        sgn = -1.0 if (d % 32) < 16 else 1.0
        S[d] = sgn * np.sin(ang)
    ropeC = np.concatenate([C, C], 0)
    ropeS = np.concatenate([S, S], 0)
    j = np.arange(128)[:, None].astype(f32)
    i = np.arange(128)[None, :].astype(f32)
    EF = np.maximum(i - j, 0)
    MF = (i >= j).astype(f32)
    EB = np.maximum(j - i, 0)
    MB = (j > i).astype(f32)
    ZEf = np.repeat(127.0 - j, 64, 1)
    ZEb = np.repeat(j, 64, 1)
    XE = np.zeros((128, 128), f32)
    XE[0:64] = i + 1.0
    XE[64:128] = 128.0 - i
    rconst = np.concatenate([EF, MF, EB, MB, ZEf, ZEb, XE], 1).astype(f32)

    def rc(n):
        tt = np.arange(n)
        out = np.zeros((4, n), f32)
        for gi, w in enumerate((2, 4, 8, 16)):
            lo = np.clip(tt - w // 2, 0, n)
            hi = np.clip(tt - w // 2 + w, 0, n)
            out[gi] = 1.0 / (hi - lo).astype(f32)
        return out
    return dict(ident=ident, amask=amask, bmask=bmask, ropeC=ropeC, ropeS=ropeS, rconst=rconst,
                prc=rc(SEQ), prcc=rc(CTX))


def _tl_table(rpb):
    L = rpb.shape[0]
    a = np.arange(2)[:, None, None, None]
    kc = np.arange(64)[None, :, None, None]
    u = np.arange(22)[None, None, :, None]
    qc = np.arange(64)[None, None, None, :]
    dr = a + 10 - u + 0 * kc + 0 * qc
    wstart = np.clip(qc - 8, 0, 48)
    ok = (np.abs(dr) <= 7) & (kc >= wstart) & (kc < wstart + 16)
    dri = np.clip(dr + 7, 0, 14)
    dci = np.clip(kc - qc + 15 + 0 * dr, 0, 30)
    out = np.empty((L, 2, 64, 8, 22, 64), np.float32)
    for l in range(L):
        for h in range(8):
            g = rpb[l, h][dri, dci]
            out[l, :, :, h] = np.where(ok, g, np.float32(NEGM))
    return out.reshape(L, 128, 8 * 22 * 64)


_CONSTS = None


def make_in_maps(inp):
    global _CONSTS
    if _CONSTS is None:
        _CONSTS = _const_tables()
    f32 = np.float32
    g = {k: np.asarray(v, dtype=f32) for k, v in inp.items()}
    L = DEPTH

    def col(v):
        return np.ascontiguousarray(v.reshape(L, -1, 128).transpose(2, 0, 1))
    fin = np.broadcast_to(g["final_norm_g"][None], (L, D))
    vecs = np.concatenate([col(g["norm1_g"]), col(g["norm2_g"]), col(g["ret_gn_g"]), col(g["pool_scale"]),
                           col(np.ascontiguousarray(fin))], axis=2)
    logits = np.ascontiguousarray(np.broadcast_to(
        np.concatenate([g["ret_logit_f"], g["ret_logit_b"]], 1)[None], (128, L, 8)))
    shared = dict(w_ada=g["w_ada"], b_ada=g["b_ada"].reshape(L, 1, 6 * D), vecs=np.ascontiguousarray(vecs),
                  logits=logits, w_in=g["w_in"], pool_w=g["pool_w"], w_branch_a=g["w_branch_a"],
                  w_branch_b=g["w_branch_b"], w_branch_c=g["w_branch_c"], w_out=g["w_out"],
                  w_ffn_gate=g["w_ffn_gate"], w_ffn_up=g["w_ffn_up"], w_ffn_down=g["w_ffn_down"],
                  tl=_tl_table(g["na_rpb"]))
    shared.update(_CONSTS)
    maps = []
    cc = g["c_ctx"].reshape(128, 8)
    for b in range(g["x"].shape[0]):
        m = dict(shared)
        m["x"] = np.ascontiguousarray(g["x"][b])
        m["ctx"] = np.ascontiguousarray(g["ctx"][b])
        m["cvec"] = np.ascontiguousarray(np.stack([g["c"][b].reshape(128, 8), cc], axis=2))
        maps.append(m)
    return maps


_NC = None


def kernel(**inputs):
    global _NC
    maps = make_in_maps(inputs)
    if _NC is None:
        _NC = build_program()
    res = run_bass_kernel_spmd(_NC, maps, core_ids=list(range(len(maps))))
    return np.stack([np.asarray(r["out"], dtype=np.float32) for r in res.results], axis=0)
```

```python
import contextlib
import numpy as np
import concourse.bass as bass
import concourse.mybir as mybir
from concourse.bass_utils import run_bass_kernel_spmd

F32 = mybir.dt.float32
BF16 = mybir.dt.bfloat16
AF = mybir.ActivationFunctionType
ALU = mybir.AluOpType

ENGS = ("pe", "act", "dve", "pool", "sp")
ENGN = {"pe": "tensor", "act": "scalar", "dve": "vector", "pool": "gpsimd", "sp": "sync"}
SEM_CAP = 30000
DMA_ROT = 8
NSEM = 96

D = 1024
SEQ = 4096
CTX = 256
TOK = SEQ + CTX
DEPTH = 2
HID = 2816
NHC = HID // 128
NORM_EPS = 1e-6
GN_EPS = 1e-5
NEGM = -30000.0
TILES = [(i * 512, 512, 0) for i in range(8)] + [(SEQ, CTX, 1)]
DEBUG_OUTS = ()


class Res:
    __slots__ = ("w", "r")

    def __init__(self):
        self.w = None
        self.r = {}


class Prog:
    def __init__(self, nc, sems):
        self.nc = nc
        self.sems = sems
        self.semmap = {}
        self.q = {e: [] for e in ENGS}
        self.stream = {}
        self.waited = {e: {} for e in ENGS}
        self.dma_rot = {e: 0 for e in ENGS}
        self.out_events = []

    def _sem(self, key):
        if key not in self.semmap:
            self.semmap[key] = self.sems[len(self.semmap)]
        return self.semmap[key]

    def _next_event(self, stream, inc):
        st = self.stream.setdefault(stream, [0, 0])
        if st[1] + inc > SEM_CAP:
            st[0] += 1
            st[1] = 0
        st[1] += inc
        return ((stream, st[0]), st[1])

    def _peek_prev(self, stream):
        st = self.stream.get(stream)
        if st is None or st[1] == 0:
            return None
        return ((stream, st[0]), st[1])

    def _waits(self, eng, reads, writes, extra=()):
        need = {}

        def add(ev):
            if ev is None:
                return
            k, v = ev
            if need.get(k, 0) < v:
                need[k] = v
        for r in reads:
            add(r.w)
        for w in writes:
            add(w.w)
            for k, v in w.r.items():
                add((k, v))
        for ev in extra:
            add(ev)
        wd = self.waited[eng]
        for k, v in need.items():
            if eng == "pe" and k[0] == "c:pe":
                continue
            if wd.get(k, 0) >= v:
                continue
            wd[k] = v
            self.q[eng].append(("wait", k, v))

    def _mark(self, ev, reads, writes):
        k, v = ev
        for r in reads:
            if r.r.get(k, 0) < v:
                r.r[k] = v
        for w in writes:
            w.w = ev
            w.r = {}

    def op(self, eng, fn, reads=(), writes=()):
        self._waits(eng, reads, writes)
        ev = self._next_event("c:" + eng, 1)
        self.q[eng].append(("op", fn, ev[0]))
        self._mark(ev, reads, writes)

    def group(self, eng, fns, reads=(), writes=()):
        self._waits(eng, reads, writes)
        ev = self._next_event("c:" + eng, 1)
        for f in fns[:-1]:
            self.q[eng].append(("op", f, None))
        self.q[eng].append(("op", fns[-1], ev[0]))
        self._mark(ev, reads, writes)

    def dma(self, eng, out, in_, reads=(), writes=(), is_output=False, **kw):
        j = self.dma_rot[eng]
        self.dma_rot[eng] = (j + 1) % DMA_ROT
        stream = "d:%s:%d" % (eng, j)
        prev = self._peek_prev(stream)
        self._waits(eng, reads, writes, extra=(prev,) if prev else ())
        ev = self._next_event(stream, 16)
        self.q[eng].append(("dma", out, in_, ev[0], kw))
        self._mark(ev, reads, writes)
        if is_output:
            self.out_events.append(ev)

    def barrier(self):
        for e in ENGS:
            for stream, st in self.stream.items():
                if st[1] == 0 or (e == "pe" and stream == "c:pe"):
                    continue
                key, v = (stream, st[0]), st[1]
                if self.waited[e].get(key, 0) < v:
                    self.waited[e][key] = v
                    self.q[e].append(("wait", key, v))

    def flush(self, final=False):
        self.barrier()
        if final:
            for k, v in self.out_events:
                if self.waited["sp"].get(k, 0) < v:
                    self.waited["sp"][k] = v
                    self.q["sp"].append(("wait", k, v))
        nc = self.nc
        with nc.Block() as block:
            for e in ENGS:
                ql = self.q[e]

                def run(engine, ql=ql):
                    pend = []
                    for it in ql:
                        if it[0] == "wait":
                            pend.append(it)
                            continue
                        for w in pend[:-1]:
                            engine.wait_ge(self._sem(w[1]), w[2])
                        if it[0] == "op":
                            ins = it[1](engine)
                            if pend:
                                ins._wait_ge(self._sem(pend[-1][1]), pend[-1][2])
                            if it[2] is not None:
                                ins.then_inc(self._sem(it[2]), 1)
                        else:
                            _, out, in_, k, kw = it
                            ins = engine.dma_start(out=out, in_=in_, **kw)
                            if pend:
                                ins._wait_ge(self._sem(pend[-1][1]), pend[-1][2])
                            ins.then_inc(self._sem(k), 16)
                        pend = []
                    for w in pend:
                        engine.wait_ge(self._sem(w[1]), w[2])
                getattr(block, ENGN[e])(run)
        self.q = {e: [] for e in ENGS}


class T:
    def __init__(self, h):
        self.h = h
        self.r = Res()

    def __getitem__(self, k):
        return self.h[k]


def build_program(depth=DEPTH, debug_outs=(), stop_after=None):
    nc = bass.Bass("TRN2", target_bir_lowering=False)

    def din(name, shape, dt=F32):
        return nc.dram_tensor(name, list(shape), dt, kind="ExternalInput").ap()

    dres = {}

    def dscr(name, shape, dt):
        kind = "ExternalOutput" if name in debug_outs else "Internal"
        ap = nc.dram_tensor(name, list(shape), dt, kind=kind).ap()
        dres[name] = Res()
        return ap

    x_in = din("x", [SEQ, D])
    ctx_in = din("ctx", [CTX, D])
    cvec_in = din("cvec", [128, 8, 2])
    w_ada = din("w_ada", [DEPTH, D, 6 * D])
    b_ada = din("b_ada", [DEPTH, 1, 6 * D])
    vecs_in = din("vecs", [128, DEPTH, 32])
    logit_in = din("logits", [128, DEPTH, 8])
    w_in = din("w_in", [DEPTH, D, 6656])
    pool_w = din("pool_w", [DEPTH, 4, 128, 128])
    w_ba = din("w_branch_a", [DEPTH, 512, D])
    w_bb = din("w_branch_b", [DEPTH, 512, D])
    w_bc = din("w_branch_c", [DEPTH, 512, D])
    w_out = din("w_out", [DEPTH, D, D])
    w_fg = din("w_ffn_gate", [DEPTH, D, HID])
    w_fu = din("w_ffn_up", [DEPTH, D, HID])
    w_fd = din("w_ffn_down", [DEPTH, HID, D])
    tl_in = din("tl", [DEPTH, 128, 8 * 22 * 64])
    ident_in = din("ident", [128, 128])
    amask_in = din("amask", [16, 1024])
    bmask_in = din("bmask", [16, 8, 512])
    ropeC_in = din("ropeC", [128, SEQ])
    ropeS_in = din("ropeS", [128, SEQ])
    rc_in = din("rconst", [128, 768])
    prc_in = din("prc", [4, SEQ])
    prcc_in = din("prcc", [4, CTX])
    out_ap = nc.dram_tensor("out", [SEQ, D], F32, kind="ExternalOutput").ap()
    r_out = Res()

    hT = dscr("hT", [8, 128, TOK], F32)
    QT = dscr("QT", [512, TOK], BF16)
    KT = dscr("KT", [512, TOK], BF16)
    VV = dscr("VV", [TOK, 512], BF16)
    RQT = dscr("RQT", [256, TOK], BF16)
    RKT = dscr("RKT", [256, TOK], BF16)
    KZ = dscr("KZ", [TOK, 512], BF16)
    RV = dscr("RV", [TOK, 512], BF16)
    GT = dscr("GT", [512, TOK], BF16)
    SG = dscr("SG", [3, D, TOK], BF16)
    PT = dscr("PT", [512, TOK], BF16)
    AT = dscr("AT", [512, TOK], BF16)
    RT = dscr("RT", [512, TOK], BF16)
    ACTT = dscr("ACTT", [NHC, 128, TOK], BF16)
    XNd = dscr("XNd", [8, 128, TOK], BF16)

    with contextlib.ExitStack() as top:
        sems = [top.enter_context(nc.semaphore("s%d" % i)) for i in range(NSEM)]
        P = Prog(nc, sems)

        uniq = [0]

        def sb(es, name, shape, dt):
            uniq[0] += 1
            return T(es.enter_context(nc.sbuf_tensor("%s_%d" % (name, uniq[0]), list(shape), dt)))

        def psum(es, name, shape, dt=F32):
            return T(es.enter_context(nc.psum_tensor(name, list(shape), dt)))

        ps = [psum(top, "ps%d" % i, [128, 512]) for i in range(7)]
        psb = psum(top, "psb", [128, 1024], BF16)
        ident = sb(top, "ident", [128, 128], F32)
        identb = sb(top, "identb", [128, 128], BF16)
        onesb = sb(top, "onesb", [128, 128], BF16)
        ones1 = sb(top, "ones1", [128, 64], F32)
        mods = sb(top, "mods", [128, DEPTH, 48, 2], F32)
        vecs = sb(top, "vecs", [128, DEPTH, 32], F32)
        gs = sb(top, "gs", [128, DEPTH, 2, 8, 2], F32)
        gfin = sb(top, "gfin", [128, 8], F32)
        XNh = [None]
        rcn = sb(top, "rcn", [128, 768], F32)
        DT = sb(top, "DT", [128, 4, 128], F32)
        ZT = sb(top, "ZT", [128, 4, 2, 64], F32)
        XI = sb(top, "XI", [128, 4, 128], F32)
        DEC = sb(top, "DEC", [128, 4, 128], F32)
        lg = sb(top, "lg", [128, 8], F32)

        psi = [0]

        def nps():
            psi[0] = (psi[0] + 1) % 7
            return ps[psi[0]]

        P.dma("sp", ident[:], ident_in, writes=[ident.r])
        P.dma("pool", identb[:], ident_in, writes=[identb.r])
        P.dma("sp", vecs[:], vecs_in, writes=[vecs.r])
        P.dma("sp", rcn[:], rc_in, writes=[rcn.r])
        P.op("dve", lambda e: e.memset(onesb[:], 1.0), writes=[onesb.r])
        P.op("dve", lambda e: e.memset(ones1[:], 1.0), writes=[ones1.r])
        epsD = sb(top, "epsD", [128, 2], F32)
        P.op("dve", lambda e: e.memset(epsD[:, 0:1], float(D * NORM_EPS)), writes=[epsD.r])
        P.op("dve", lambda e: e.memset(epsD[:, 1:2], float(GN_EPS)), writes=[epsD.r])

        def mod(l, j, kc, s):
            return mods[:, l, j * 8 + kc, s:s + 1]

        with contextlib.ExitStack() as es:
            cv = sb(es, "cv", [128, 8, 2], F32)
            sc = sb(es, "sc", [128, 8, 2], F32)
            ba = sb(es, "ba", [2, DEPTH, 6 * D], F32)
            mrow = sb(es, "mrow", [2, DEPTH, 6 * D], F32)
            wa = [sb(es, "wa%d" % i, [128, 8, 256], F32) for i in range(3)]
            xt = [sb(es, "xt%d" % i, [128, D], F32) for i in range(2)]
            hst = [sb(es, "hst%d" % i, [128, 8, 512], F32) for i in range(2)]
            tmpm = sb(es, "tmpm", [128, 8, 2], F32)
            P.dma("sp", cv[:], cvec_in, writes=[cv.r])
            P.dma("sp", ba[:], b_ada.rearrange("l o n -> o l n").broadcast_to([2, DEPTH, 6 * D]), writes=[ba.r])
            P.op("act", lambda e: e.activation(sc[:], cv[:], AF.Silu), reads=[cv.r], writes=[sc.r])
            pm = ps[0]

            NAG = 24

            def ada_group(l, cg):
                wav = w_ada[l].rearrange("(p k) n -> p k n", k=8)
                wt = wa[cg % 3]
                c0 = cg * 256
                hf = (cg % 2) * 256
                P.dma("sp", wt[:], wav[:, :, c0:c0 + 256], writes=[wt.r])
                P.group("pe", [lambda e, k=k: e.matmul(
                    pm[0:2, hf:hf + 256], sc[:, k, :], wt[:, k, :], start=(k == 0), stop=(k == 7)) for k in range(8)],
                    reads=[wt.r, sc.r], writes=[pm.r])
                P.op("dve", lambda e: e.tensor_copy(mrow[0:2, l, c0:c0 + 256], pm[0:2, hf:hf + 256]),
                     reads=[pm.r], writes=[mrow.r])

            for cg in range(NAG):
                ada_group(0, cg)
            pending = [(l, cg) for l in range(1, depth) for cg in range(NAG)]
            xb = [0]

            def xps():
                xb[0] = xb[0] % 6 + 1
                return ps[xb[0]]
            cnt = 0
            for ti, (t0, n, s) in enumerate(TILES):
                hs = hst[ti % 2]
                for sub in range(n // 128):
                    xx = xt[cnt % 2]
                    cnt += 1
                    src = x_in[t0 + sub * 128:t0 + (sub + 1) * 128, :] if s == 0 else \
                        ctx_in[sub * 128:(sub + 1) * 128, :]
                    P.dma("sp", xx[:], src, writes=[xx.r])
                    for half in range(2):
                        pt = xps()
                        for kk in range(4):
                            kc = half * 4 + kk
                            P.op("pe", lambda e, pt=pt, kk=kk, kc=kc, xx=xx: e.transpose(
                                pt[:, kk * 128:(kk + 1) * 128], xx[:, kc * 128:(kc + 1) * 128], ident[:]),
                                reads=[xx.r, ident.r], writes=[pt.r])
                        if half == 0:
                            P.op("dve", lambda e, pt=pt, hs=hs, half=half, sub=sub: e.tensor_copy(
                                hs[:, half * 4:half * 4 + 4, sub * 128:(sub + 1) * 128],
                                pt[:].rearrange("p (k t) -> p k t", k=4)), reads=[pt.r], writes=[hs.r])
                        else:
                            P.op("act", lambda e, pt=pt, hs=hs, half=half, sub=sub: e.activation(
                                hs[:, half * 4:half * 4 + 4, sub * 128:(sub + 1) * 128],
                                pt[:].rearrange("p (k t) -> p k t", k=4), AF.Identity), reads=[pt.r], writes=[hs.r])
                    if pending and cnt % 2 == 0:
                        ada_group(*pending.pop(0))
                P.dma("pool", hT[:, :, t0:t0 + n].rearrange("k p t -> p k t"), hs[:, :, 0:n],
                      reads=[hs.r], writes=[dres["hT"]])
            while pending:
                ada_group(*pending.pop(0))
            P.op("dve", lambda e: e.tensor_tensor(mrow[:], mrow[:], ba[:], op=ALU.add),
                 reads=[mrow.r, ba.r], writes=[mrow.r])
            fns = []
            for l in range(depth):
                for m in range(48):
                    col = (l * 48 + m) * 2
                    fns.append(lambda e, l=l, m=m, col=col: e.matmul(
                        pm[:, col:col + 2], mrow[0:2, l, m * 128:(m + 1) * 128], ident[0:2, 0:2],
                        start=True, stop=True))
            P.group("pe", fns, reads=[mrow.r, ident.r], writes=[pm.r])
            P.op("dve", lambda e: e.tensor_copy(
                mods[:, 0:depth].rearrange("p l m s -> p (l m s)"), pm[:, 0:depth * 96]),
                reads=[pm.r], writes=[mods.r])
            for l in range(depth):
                for w in range(2):
                    sc0 = 8 + 24 * w
                    P.op("dve", lambda e, l=l, sc0=sc0: e.tensor_scalar(
                        tmpm[:], mods[:, l, sc0:sc0 + 8, :], 1.0, 32.0, op0=ALU.add, op1=ALU.mult),
                        reads=[mods.r], writes=[tmpm.r])
                    for s in range(2):
                        P.op("dve", lambda e, l=l, w=w, s=s: e.tensor_tensor(
                            gs[:, l, w, :, s], tmpm[:, :, s], vecs[:, l, 8 * w:8 * w + 8], op=ALU.mult),
                            reads=[tmpm.r, vecs.r], writes=[gs.r])
            P.op("dve", lambda e: e.tensor_scalar(gfin[:], vecs[:, 0, 24:32], 32.0, None, op0=ALU.mult),
                 reads=[vecs.r], writes=[gfin.r])
            P.flush()

        def norm_phase(l, w, tiles, dump=False):
            with contextlib.ExitStack() as es:
                hb = [sb(es, "nh%d" % i, [128, 8, 512], F32) for i in range(2)]
                sqs = [sb(es, "nsq%d" % i, [128, 8, 512], BF16) for i in range(2)]
                rss = [sb(es, "nrs%d" % i, [128, 512], F32) for i in range(2)]
                tmp = [sb(es, "ntm%d" % i, [128, 512], F32) for i in range(4)]
                def stage_a(ti):
                    t0, n, s = tiles[ti]
                    ht, sq, rs = hb[ti % 2], sqs[ti % 2], rss[ti % 2]
                    P.dma("sp", ht[:, :, 0:n], hT[:, :, t0:t0 + n].rearrange("k p t -> p k t"),
                          reads=[dres["hT"]], writes=[ht.r])
                    P.op("act", lambda e: e.activation(sq[:, :, 0:n], ht[:, :, 0:n], AF.Square),
                         reads=[ht.r], writes=[sq.r])
                    pp = nps()
                    P.group("pe", [lambda e, k=k: e.matmul(
                        pp[:, 0:n], onesb[:], sq[:, k, 0:n], start=(k == 0), stop=(k == 7)) for k in range(8)],
                        reads=[onesb.r, sq.r], writes=[pp.r])
                    P.op("act", lambda e: e.activation(
                        rs[:, 0:n], pp[:, 0:n], AF.Sqrt, bias=epsD[:, 0:1]), reads=[pp.r, epsD.r], writes=[rs.r])
                    P.op("dve", lambda e: e.reciprocal(rs[:, 0:n], rs[:, 0:n]), reads=[rs.r], writes=[rs.r])

                def stage_b(ti):
                    t0, n, s = tiles[ti]
                    ht, rs = hb[ti % 2], rss[ti % 2]
                    for kc in range(8):
                        tm = tmp[kc % 4]
                        P.op("dve", lambda e, tm=tm, kc=kc: e.scalar_tensor_tensor(
                            tm[:, 0:n], ht[:, kc, 0:n], gs[:, l, w, kc, s:s + 1], rs[:, 0:n],
                            op0=ALU.mult, op1=ALU.mult), reads=[ht.r, gs.r, rs.r], writes=[tm.r])
                        P.op("act", lambda e, tm=tm, kc=kc: e.activation(
                            XNh[0][:, kc, t0:t0 + n], tm[:, 0:n], AF.Identity, bias=mod(l, 3 * w, kc, s)),
                            reads=[tm.r, mods.r], writes=[XNh[0].r])
                    if dump:
                        P.dma("act", XNd[:, :, t0:t0 + n].rearrange("k p t -> p k t"), XNh[0][:, :, t0:t0 + n],
                              reads=[XNh[0].r], writes=[dres["XNd"]])

                stage_a(0)
                for ti in range(len(tiles)):
                    if ti + 1 < len(tiles):
                        stage_a(ti + 1)
                    stage_b(ti)
                P.flush()

        def retention_tables(l):
            with contextlib.ExitStack() as es:
                lgt = sb(es, "lgt", [128, 8], F32)
                t1 = sb(es, "rt1", [128, 128], F32)
                t2 = sb(es, "rt2", [128, 128], F32)
                c128 = sb(es, "c128", [128, 128], F32)
                P.dma("sp", lgt[:], logit_in[:, l, :], writes=[lgt.r])
                P.op("dve", lambda e: e.memset(c128[:], 128.0), writes=[c128.r])
                P.op("act", lambda e: e.activation(lgt[:], lgt[:], AF.Exp, scale=-1.0), reads=[lgt.r], writes=[lgt.r])
                P.op("act", lambda e: e.activation(lgt[:], lgt[:], AF.Ln, bias=1.0), reads=[lgt.r], writes=[lgt.r])
                P.op("dve", lambda e: e.tensor_scalar(lg[:], lgt[:], -1.0, None, op0=ALU.mult),
                     reads=[lgt.r], writes=[lg.r])
                EF, MF, EB, MB = (rcn[:, 0:128], rcn[:, 128:256], rcn[:, 256:384], rcn[:, 384:512])
                ZEf, ZEb, XE = rcn[:, 512:576], rcn[:, 576:640], rcn[:, 640:768]
                for h in range(4):
                    lf = lg[:, h:h + 1]
                    lb = lg[:, 4 + h:5 + h]
                    P.op("act", lambda e, lf=lf: e.activation(t1[:], EF, AF.Exp, scale=lf),
                         reads=[lg.r, rcn.r], writes=[t1.r])
                    P.op("dve", lambda e: e.tensor_tensor(t1[:], t1[:], MF, op=ALU.mult),
                         reads=[t1.r, rcn.r], writes=[t1.r])
                    P.op("act", lambda e, lb=lb: e.activation(t2[:], EB, AF.Exp, scale=lb),
                         reads=[lg.r, rcn.r], writes=[t2.r])
                    P.op("dve", lambda e: e.tensor_tensor(t2[:], t2[:], MB, op=ALU.mult),
                         reads=[t2.r, rcn.r], writes=[t2.r])
                    P.op("dve", lambda e, h=h: e.tensor_tensor(DT[:, h, :], t1[:], t2[:], op=ALU.add),
                         reads=[t1.r, t2.r], writes=[DT.r])
                    P.op("act", lambda e, h=h, lf=lf: e.activation(ZT[:, h, 0, :], ZEf, AF.Exp, scale=lf),
                         reads=[lg.r, rcn.r], writes=[ZT.r])
                    P.op("act", lambda e, h=h, lb=lb: e.activation(ZT[:, h, 1, :], ZEb, AF.Exp, scale=lb),
                         reads=[lg.r, rcn.r], writes=[ZT.r])
                    P.op("act", lambda e, h=h: e.activation(XI[0:64, h, :], rcn[0:64, 640:768], AF.Exp,
                                                            scale=lg[0:64, h:h + 1]),
                         reads=[lg.r, rcn.r], writes=[XI.r])
                    P.op("act", lambda e, h=h: e.activation(XI[64:128, h, :], rcn[64:128, 640:768], AF.Exp,
                                                            scale=lg[64:128, 4 + h:5 + h]),
                         reads=[lg.r, rcn.r], writes=[XI.r])
                    P.op("act", lambda e, h=h: e.activation(DEC[0:64, h, :], c128[0:64, :], AF.Exp,
                                                            scale=lg[0:64, h:h + 1]),
                         reads=[lg.r, c128.r], writes=[DEC.r])
                    P.op("act", lambda e, h=h: e.activation(DEC[64:128, h, :], c128[64:128, :], AF.Exp,
                                                            scale=lg[64:128, 4 + h:5 + h]),
                         reads=[lg.r, c128.r], writes=[DEC.r])
                P.flush()

        def proj_phase(l):
            with contextlib.ExitStack() as es:
                wb = [sb(es, "pw%d" % i, [128, 8, 512], BF16) for i in range(2)]
                wrot = sb(es, "pwrot", [128, 8, 512], BF16)
                stg = [sb(es, "pst%d" % i, [128, 4, 512], BF16) for i in range(2)]
                rC = sb(es, "ropeC", [128, 512], F32)
                rS = sb(es, "ropeS", [128, 512], F32)
                r1 = [sb(es, "pr1%d" % i, [128, 512], F32) for i in range(2)]
                r2 = [sb(es, "pr2%d" % i, [128, 512], F32) for i in range(2)]
                kzs = sb(es, "kzs", [128, 4, 512], BF16)
                wv = w_in[l].rearrange("(k p) n -> p k n", p=128)
                wcnt = [0]

                def loadw(c0, width=512):
                    wt = wb[wcnt[0] % 2]
                    wcnt[0] += 1
                    P.dma("pool", wt[:, :, 0:width], wv[:, :, c0:c0 + width], writes=[wt.r])
                    return wt

                scnt = [0]

                def nstg():
                    scnt[0] += 1
                    return stg[scnt[0] % 2]

                def fm_mm(pp, wt, c, t0, n):
                    P.group("pe", [lambda e, k=k: e.matmul(
                        pp[:, 0:n], wt[:, k, c * 128:(c + 1) * 128], XNh[0][:, k, t0:t0 + n],
                        start=(k == 0), stop=(k == 7)) for k in range(8)],
                        reads=[wt.r, XNh[0].r], writes=[pp.r])

                def fm_group(c0, dst, func, scale=1.0, nchunks=4, tiles=TILES):
                    wt = loadw(c0, nchunks * 128)
                    for (t0, n, s) in tiles:
                        st = nstg()
                        for c in range(nchunks):
                            pp = nps()
                            fm_mm(pp, wt, c, t0, n)
                            P.op("act", lambda e, st=st, c=c, pp=pp, n=n: e.activation(
                                st[:, c, 0:n], pp[:, 0:n], func, scale=scale), reads=[pp.r], writes=[st.r])
                        P.dma("act", dst[:, t0:t0 + n].rearrange("(c p) t -> p c t", p=128), st[:, 0:nchunks, 0:n],
                              reads=[st.r], writes=[dres_of[id(dst)]])

                def tm_group(c0, dst, name):
                    wt = loadw(c0)
                    for (t0, n, s) in TILES:
                        st = nstg()
                        for sub in range(n // 128):
                            pp = nps()
                            P.group("pe", [lambda e, k=k, pp=pp, sub=sub, t0=t0: e.matmul(
                                pp[:, :], XNh[0][:, k, t0 + sub * 128:t0 + (sub + 1) * 128], wt[:, k, :],
                                start=(k == 0), stop=(k == 7)) for k in range(8)],
                                reads=[wt.r, XNh[0].r], writes=[pp.r])
                            P.op("act", lambda e, st=st, sub=sub, pp=pp: e.activation(
                                st[:, sub, :], pp[:, :], AF.Identity), reads=[pp.r], writes=[st.r])
                        P.dma("act", dst[t0:t0 + n, :].rearrange("(s p) c -> p s c", p=128), st[:, 0:n // 128, :],
                              reads=[st.r], writes=[dres[name]])

                dres_of = {id(QT): dres["QT"], id(KT): dres["KT"], id(GT): dres["GT"]}
                fm_group(0, QT, AF.Identity, scale=0.125)
                fm_group(512, KT, AF.Identity)
                tm_group(1024, VV, "VV")
                tm_group(2048, RV, "RV")
                fm_group(2560, GT, AF.Silu)
                for b in range(3):
                    for hf in range(2):
                        dst = SG[b, hf * 512:(hf + 1) * 512, :]
                        dres_of[id(dst)] = dres["SG"]
                        fm_group(3584 + b * 1024 + hf * 512, dst, AF.Sigmoid)

                wt = loadw(1536)
                wvw = wt[:].rearrange("p k (b two i) -> p (k b) two i", two=2, i=16)
                wvr = wrot[:].rearrange("p k (b two i) -> p (k b) two i", two=2, i=16)
                P.op("dve", lambda e: e.tensor_copy(wvr[:, :, 0, :], wvw[:, :, 1, :]), reads=[wt.r], writes=[wrot.r])
                P.op("dve", lambda e: e.tensor_copy(wvr[:, :, 1, :], wvw[:, :, 0, :]), reads=[wt.r], writes=[wrot.r])
                for (t0, n, s) in TILES:
                    if s == 0:
                        P.dma("sp", rC[:], ropeC_in[:, t0:t0 + n], writes=[rC.r])
                        P.dma("sp", rS[:], ropeS_in[:, t0:t0 + n], writes=[rS.r])
                    st = nstg()
                    for c in range(4):
                        ksc = 1.0 if c < 2 else 0.125
                        pp = nps()
                        fm_mm(pp, wt, c, t0, n)
                        if s == 0:
                            pr = nps()
                            fm_mm(pr, wrot, c, t0, n)
                            a1 = r1[c % 2]
                            a2 = r2[c % 2]
                            P.op("dve", lambda e, a1=a1, pp=pp, ksc=ksc: e.scalar_tensor_tensor(
                                a1[:], pp[:], ksc, rC[:], op0=ALU.mult, op1=ALU.mult),
                                reads=[pp.r, rC.r], writes=[a1.r])
                            P.op("dve", lambda e, a2=a2, pr=pr, ksc=ksc: e.scalar_tensor_tensor(
                                a2[:], pr[:], ksc, rS[:], op0=ALU.mult, op1=ALU.mult),
                                reads=[pr.r, rS.r], writes=[a2.r])
                            P.op("dve", lambda e, st=st, c=c, a1=a1, a2=a2: e.tensor_tensor(
                                st[:, c, :], a1[:], a2[:], op=ALU.add), reads=[a1.r, a2.r], writes=[st.r])
                        else:
                            P.op("act", lambda e, st=st, c=c, pp=pp, n=n, ksc=ksc: e.activation(
                                st[:, c, 0:n], pp[:, 0:n], AF.Identity, scale=ksc), reads=[pp.r], writes=[st.r])
                    P.dma("act", RQT[:, t0:t0 + n].rearrange("(c p) t -> p c t", p=128), st[:, 0:2, 0:n],
                          reads=[st.r], writes=[dres["RQT"]])
                    P.dma("act", RKT[:, t0:t0 + n].rearrange("(c p) t -> p c t", p=128), st[:, 2:4, 0:n],
                          reads=[st.r], writes=[dres["RKT"]])
                    nsub = n // 128
                    for sub in range(nsub):
                        for c in range(2):
                            P.op("pe", lambda e, st=st, sub=sub, c=c: e.transpose(
                                psb[:, (sub * 2 + c) * 128:(sub * 2 + c + 1) * 128],
                                st[:, 2 + c, sub * 128:(sub + 1) * 128], identb[:]),
                                reads=[st.r, identb.r], writes=[psb.r])
                    for sub in range(nsub):
                        for d in range(2):
                            P.op("dve", lambda e, sub=sub, d=d: e.tensor_tensor(
                                kzs[:, sub, :].rearrange("p (h d k) -> p h d k", h=4, d=2)[:, :, d, :],
                                psb[:, sub * 256:(sub + 1) * 256].rearrange("p (h k) -> p h k", h=4),
                                ZT[:, :, d, :], op=ALU.mult), reads=[psb.r, ZT.r], writes=[kzs.r])
                    P.dma("act", KZ[t0:t0 + n, :].rearrange("(s p) c -> p s c", p=128), kzs[:, 0:nsub, :],
                          reads=[kzs.r], writes=[dres["KZ"]])
                P.flush()

        def pool_phase(l):
            with contextlib.ExitStack() as es:
                wt = sb(es, "plw", [128, 8, 512], BF16)
                pw = sb(es, "plpw", [128, 4, 128], BF16)
                U = sb(es, "plU", [128, TOK + 64], F32)
                A = sb(es, "plA", [128, TOK + 64], F32)
                B = sb(es, "plB", [128, TOK + 64], F32)
                rcp = sb(es, "plrc", [128, TOK], F32)
                pl = sb(es, "plpl", [128, TOK], BF16)
                st = [sb(es, "plst%d" % i, [128, 512], BF16) for i in range(2)]
                P.dma("pool", wt[:], w_in[l].rearrange("(k p) n -> p k n", p=128)[:, :, 3072:3584], writes=[wt.r])
                P.dma("pool", pw[:], pool_w[l].rearrange("g c d -> c g d"), writes=[pw.r])
                segs = [(16, SEQ, 0)] + ([(16 + SEQ + 32, CTX, SEQ)] if l == 0 else [])
                tiles = TILES if l == 0 else TILES[:8]
                for g in range(4):
                    w = (2, 4, 8, 16)[g]
                    P.dma("sp", rcp[:, 0:SEQ], prc_in[g:g + 1, :].broadcast_to([128, SEQ]), writes=[rcp.r])
                    if l == 0:
                        P.dma("sp", rcp[:, SEQ:TOK], prcc_in[g:g + 1, :].broadcast_to([128, CTX]), writes=[rcp.r])
                    if g == 0:
                        for buf in (U, A, B):
                            P.op("dve", lambda e, buf=buf: e.memset(buf[:], 0.0), writes=[buf.r])
                    for (t0, n, s) in tiles:
                        pp = nps()
                        P.group("pe", [lambda e, k=k, pp=pp, t0=t0, n=n, g=g: e.matmul(
                            pp[:, 0:n], wt[:, k, g * 128:(g + 1) * 128], XNh[0][:, k, t0:t0 + n],
                            start=(k == 0), stop=(k == 7)) for k in range(8)], reads=[wt.r, XNh[0].r], writes=[pp.r])
                        off = 16 + t0 if s == 0 else 16 + SEQ + 32
                        P.op("act", lambda e, pp=pp, off=off, n=n: e.activation(U[:, off:off + n], pp[:, 0:n], AF.Identity),
                             reads=[pp.r], writes=[U.r])
                    for (o, n, tb) in segs:
                        lo, hi = o - 12, o + n + 12
                        P.op("dve", lambda e, lo=lo, hi=hi: e.tensor_tensor(
                            A[:, lo:hi], U[:, lo - 1:hi - 1], U[:, lo:hi], op=ALU.add), reads=[U.r], writes=[A.r])
                        cur, oth = A, B
                        if w >= 4:
                            lo, hi = o - 10, o + n + 10
                            P.op("dve", lambda e, lo=lo, hi=hi: e.tensor_tensor(
                                B[:, lo:hi], A[:, lo - 1:hi - 1], A[:, lo + 1:hi + 1], op=ALU.add),
                                reads=[A.r], writes=[B.r])
                            cur, oth = B, A
                        if w >= 8:
                            lo, hi = o - 6, o + n + 6
                            P.op("dve", lambda e, lo=lo, hi=hi: e.tensor_tensor(
                                A[:, lo:hi], B[:, lo - 2:hi - 2], B[:, lo + 2:hi + 2], op=ALU.add),
                                reads=[B.r], writes=[A.r])
                            cur, oth = A, B
                        if w >= 16:
                            lo, hi = o, o + n
                            P.op("dve", lambda e, lo=lo, hi=hi: e.tensor_tensor(
                                B[:, lo:hi], A[:, lo - 4:hi - 4], A[:, lo + 4:hi + 4], op=ALU.add),
                                reads=[A.r], writes=[B.r])
                            cur, oth = B, A
                        P.op("dve", lambda e, cur=cur, o=o, n=n, tb=tb, w=w: e.scalar_tensor_tensor(
                            pl[:, tb:tb + n], cur[:, o:o + n], 1.0 / w, U[:, o:o + n], op0=ALU.mult, op1=ALU.subtract),
                            reads=[cur.r, U.r], writes=[pl.r])
                        for (eo, et) in ((o, tb), (o + n - 8, tb + n - 8)):
                            P.op("dve", lambda e, cur=cur, oth=oth, eo=eo, et=et: e.tensor_tensor(
                                oth[:, eo:eo + 8], cur[:, eo:eo + 8], rcp[:, et:et + 8], op=ALU.mult),
                                reads=[cur.r, rcp.r], writes=[oth.r])
                            P.op("dve", lambda e, oth=oth, eo=eo, et=et: e.tensor_tensor(
                                pl[:, et:et + 8], oth[:, eo:eo + 8], U[:, eo:eo + 8], op=ALU.subtract),
                                reads=[oth.r, U.r], writes=[pl.r])
                    for ti, (t0, n, s) in enumerate(tiles):
                        pp = nps()
                        P.group("pe", [lambda e, pp=pp, t0=t0, n=n, g=g: e.matmul(
                            pp[:, 0:n], pw[:, g, :], pl[:, t0:t0 + n], start=True, stop=True)],
                            reads=[pw.r, pl.r], writes=[pp.r])
                        so = st[ti % 2]
                        P.op("act", lambda e, so=so, pp=pp, n=n, g=g: e.activation(
                            so[:, 0:n], pp[:, 0:n], AF.Identity, scale=vecs[:, l, 20 + g:21 + g]),
                            reads=[pp.r, vecs.r], writes=[so.r])
                        P.dma("act", PT[g * 128:(g + 1) * 128, t0:t0 + n], so[:, 0:n], reads=[so.r], writes=[dres["PT"]])
                P.flush()

        def retention_phase(l):
            with contextlib.ExitStack() as es:
                ST = sb(es, "rST", [128, 34, 4, 128], BF16)
                Scur = sb(es, "rScur", [128, 4, 128], F32)
                stmp = sb(es, "rstmp", [128, 4, 128], F32)
                kzt = [sb(es, "rkz%d" % i, [128, 4, 512], BF16) for i in range(2)]
                rvt = [sb(es, "rrv%d" % i, [128, 4, 512], BF16) for i in range(2)]
                chunks_f = [32, 33] + list(range(32))
                chunks_b = [33, 32] + list(range(31, -1, -1))

                def tok_of(c):
                    return SEQ + (c - 32) * 128 if c >= 32 else c * 128

                lcnt = [0]

                def run_pass(chunks, lo, hi, Scur, stmp, kzt, rvt):
                    P.op("dve", lambda e: e.memset(Scur[lo:hi], 0.0), writes=[Scur.r])
                    loaded = {}
                    lc = 0
                    for idx, c in enumerate(chunks):
                        grp = c // 4 if c < 32 else 8
                        if grp not in loaded:
                            kz = kzt[lc % 2]
                            rv = rvt[lc % 2]
                            lc += 1
                            g0 = grp * 512 if grp < 8 else SEQ
                            ng = 4 if grp < 8 else 2
                            P.dma("sp", kz[:, 0:ng, :], KZ[g0:g0 + ng * 128, :].rearrange("(s p) c -> p s c", p=128),
                                  reads=[dres["KZ"]], writes=[kz.r])
                            P.dma("sp", rv[:, 0:ng, :], RV[g0:g0 + ng * 128, :].rearrange("(s p) c -> p s c", p=128),
                                  reads=[dres["RV"]], writes=[rv.r])
                            loaded = {grp: (kz, rv)}
                        kz, rv = loaded[grp]
                        ci = c % 4 if c < 32 else c - 32
                        pp = nps()
                        P.group("pe", [lambda e, h=h, pp=pp, kz=kz, rv=rv, ci=ci: e.matmul(
                            pp[:, h * 128:(h + 1) * 128], kz[:, ci, h * 128:(h + 1) * 128],
                            rv[:, ci, h * 128:(h + 1) * 128], start=True, stop=True) for h in range(4)],
                            reads=[kz.r, rv.r], writes=[pp.r])
                        P.op("act", lambda e, c=c: e.activation(ST[lo:hi, c], Scur[lo:hi], AF.Identity),
                             reads=[Scur.r], writes=[ST.r])
                        if c >= 32 and idx == 0:
                            P.op("dve", lambda e, pp=pp: e.tensor_copy(
                                Scur[lo:hi], pp[lo:hi, :].rearrange("p (h v) -> p h v", h=4)),
                                reads=[pp.r], writes=[Scur.r])
                        else:
                            P.op("dve", lambda e: e.tensor_tensor(stmp[lo:hi], Scur[lo:hi], DEC[lo:hi], op=ALU.mult),
                                 reads=[Scur.r, DEC.r], writes=[stmp.r])
                            P.op("dve", lambda e, pp=pp: e.tensor_tensor(
                                Scur[lo:hi], stmp[lo:hi], pp[lo:hi, :].rearrange("p (h v) -> p h v", h=4), op=ALU.add),
                                reads=[stmp.r, pp.r], writes=[Scur.r])
                        yield

                ScurB = sb(es, "rScurB", [128, 4, 128], F32)
                stmpB = sb(es, "rstmpB", [128, 4, 128], F32)
                kztB = [sb(es, "rkzB%d" % i, [128, 4, 512], BF16) for i in range(2)]
                rvtB = [sb(es, "rrvB%d" % i, [128, 4, 512], BF16) for i in range(2)]
                gf_ = run_pass(chunks_f, 0, 64, Scur, stmp, kzt, rvt)
                gb_ = run_pass(chunks_b, 64, 128, ScurB, stmpB, kztB, rvtB)
                for _ in range(len(chunks_f)):
                    next(gf_)
                    next(gb_)

                def two(name, shape, dt):
                    return [sb(es, "%s%d" % (name, i), shape, dt) for i in range(2)]
                QD = two("rQD", [128, 4, 512], BF16)
                KD = two("rKD", [64, 4, 512], BF16)
                RVt = two("rRVt", [128, 4, 512], BF16)
                GTt = two("rGT", [128, 4, 512], BF16)
                GG = two("rGG", [128, 4, 512], F32)
                rts = two("rrts", [128, 4, 512], BF16)
                ATt = two("rAT", [128, 4, 128], BF16)
                QX = two("rQX", [128, 4, 128], BF16)
                ob = two("rob", [128, 512], BF16)
                o2b = two("ro2b", [128, 512], BF16)
                mean = two("rmean", [128, 512], F32)
                msq = two("rmsq", [128, 512], F32)
                rstd = two("rrstd", [128, 512], F32)
                dd = two("rdd", [128, 512], F32)
                tiles = TILES if l == 0 else TILES[:8]
                clist = []
                for ti, (t0, n, s) in enumerate(tiles):
                    for ci in range(n // 128):
                        clist.append((ti, t0, n, s, ci))

                def load_tile(ti, t0, n, s):
                    b = ti % 2
                    nch = n // 128
                    for half in range(2):
                        P.dma("sp", QD[b][half * 64:(half + 1) * 64, :, 0:n],
                              RQT[:, t0:t0 + n].rearrange("(h d) t -> d h t", d=64),
                              reads=[dres["RQT"]], writes=[QD[b].r])
                    P.dma("sp", KD[b][:, :, 0:n], RKT[:, t0:t0 + n].rearrange("(h d) t -> d h t", d=64),
                          reads=[dres["RKT"]], writes=[KD[b].r])
                    P.dma("sp", RVt[b][:, 0:nch, :], RV[t0:t0 + n, :].rearrange("(s p) c -> p s c", p=128),
                          reads=[dres["RV"]], writes=[RVt[b].r])
                    P.dma("sp", GTt[b][:, :, 0:n], GT[:, t0:t0 + n].rearrange("(c p) t -> p c t", p=128),
                          reads=[dres["GT"]], writes=[GTt[b].r])
                    for h in range(4):
                        P.op("act", lambda e, h=h, n=n, b=b: e.activation(
                            GG[b][:, h, 0:n], GTt[b][:, h, 0:n], AF.Identity, scale=vecs[:, l, 16 + h:17 + h]),
                            reads=[GTt[b].r, vecs.r], writes=[GG[b].r])

                def stage_a(idx):
                    ti, t0, n, s, ci = clist[idx]
                    if ci == 0:
                        load_tile(ti, t0, n, s)
                    b = ti % 2
                    q = idx % 2
                    o = ci * 128
                    c = (t0 // 128 + ci) if s == 0 else 32 + ci
                    pI = ps[0]
                    pO = ps[1 + q]
                    pM = ps[3 + q]
                    pQ = ps[5 + q]
                    P.group("pe", [lambda e, h=h, o=o, b=b: e.matmul(
                        pI[:, h * 128:(h + 1) * 128], KD[b][0:64, h, o:o + 128], QD[b][0:64, h, o:o + 128],
                        start=True, stop=True) for h in range(4)], reads=[KD[b].r, QD[b].r], writes=[pI.r])
                    P.op("dve", lambda e, q=q: e.tensor_tensor(
                        ATt[q][:], pI[:].rearrange("p (h i) -> p h i", h=4), DT[:], op=ALU.mult),
                        reads=[pI.r, DT.r], writes=[ATt[q].r])
                    P.op("pool", lambda e, o=o, b=b, q=q: e.tensor_tensor(QX[q][:], QD[b][:, :, o:o + 128], XI[:], op=ALU.mult),
                         reads=[QD[b].r, XI.r], writes=[QX[q].r])
                    fns = []
                    for h in range(4):
                        fns.append(lambda e, h=h, pO=pO, ci=ci, b=b, q=q: e.matmul(
                            pO[:, h * 128:(h + 1) * 128], RVt[b][:, ci, h * 128:(h + 1) * 128], ATt[q][:, h, :],
                            start=True, stop=False))
                        fns.append(lambda e, h=h, pO=pO, c=c, q=q: e.matmul(
                            pO[:, h * 128:(h + 1) * 128], ST[:, c, h, :], QX[q][:, h, :], start=False, stop=True))
                    P.group("pe", fns, reads=[RVt[b].r, ATt[q].r, ST.r, QX[q].r], writes=[pO.r])
                    P.op("act", lambda e, pO=pO, q=q: e.activation(ob[q][:], pO[:], AF.Identity), reads=[pO.r], writes=[ob[q].r])
                    P.op("act", lambda e, pO=pO, q=q: e.activation(o2b[q][:], pO[:], AF.Square), reads=[pO.r], writes=[o2b[q].r])
                    P.group("pe", [lambda e, pM=pM, q=q: e.matmul(pM[:], onesb[:], ob[q][:], start=True, stop=True)],
                            reads=[onesb.r, ob[q].r], writes=[pM.r])
                    P.group("pe", [lambda e, pQ=pQ, q=q: e.matmul(pQ[:], onesb[:], o2b[q][:], start=True, stop=True)],
                            reads=[onesb.r, o2b[q].r], writes=[pQ.r])

                def stage_b(idx):
                    ti, t0, n, s, ci = clist[idx]
                    b = ti % 2
                    q = idx % 2
                    o = ci * 128
                    pO = ps[1 + q]
                    pM = ps[3 + q]
                    pQ = ps[5 + q]
                    P.op("act", lambda e, q=q, pM=pM: e.activation(mean[q][:], pM[:], AF.Identity, scale=1.0 / 128),
                         reads=[pM.r], writes=[mean[q].r])
                    P.op("act", lambda e, q=q, pM=pM: e.activation(msq[q][:], pM[:], AF.Square, scale=1.0 / 128),
                         reads=[pM.r], writes=[msq[q].r])
                    P.op("dve", lambda e, q=q, pQ=pQ: e.scalar_tensor_tensor(
                        rstd[q][:], pQ[:], 1.0 / 128, msq[q][:], op0=ALU.mult, op1=ALU.subtract),
                        reads=[pQ.r, msq[q].r], writes=[rstd[q].r])
                    P.op("act", lambda e, q=q: e.activation(rstd[q][:], rstd[q][:], AF.Sqrt, bias=epsD[:, 1:2]),
                         reads=[rstd[q].r, epsD.r], writes=[rstd[q].r])
                    P.op("dve", lambda e, q=q: e.reciprocal(rstd[q][:], rstd[q][:]), reads=[rstd[q].r], writes=[rstd[q].r])
                    P.op("dve", lambda e, q=q, pO=pO: e.tensor_tensor(dd[q][:], pO[:], mean[q][:], op=ALU.subtract),
                         reads=[pO.r, mean[q].r], writes=[dd[q].r])
                    P.op("dve", lambda e, q=q: e.tensor_tensor(dd[q][:], dd[q][:], rstd[q][:], op=ALU.mult),
                         reads=[dd[q].r, rstd[q].r], writes=[dd[q].r])
                    P.op("dve", lambda e, o=o, b=b, q=q: e.tensor_tensor(
                        rts[b][:, :, o:o + 128], dd[q][:].rearrange("p (h i) -> p h i", h=4), GG[b][:, :, o:o + 128],
                        op=ALU.mult), reads=[dd[q].r, GG[b].r], writes=[rts[b].r])
                    if ci == n // 128 - 1:
                        P.dma("act", RT[:, t0:t0 + n].rearrange("(c p) t -> p c t", p=128), rts[b][:, :, 0:n],
                              reads=[rts[b].r], writes=[dres["RT"]])

                stage_a(0)
                for idx in range(len(clist)):
                    if idx + 1 < len(clist):
                        stage_a(idx + 1)
                    stage_b(idx)
                P.flush()

        def na_phase(l, pre=None):
            with contextlib.ExitStack() as es:
                TL = sb(es, "aTL", [128, 8, 22 * 64], BF16)
                BM = sb(es, "aBM", [80, 8, 512], BF16)
                QAs = [sb(es, "aQA%d" % i, [80, 8, 512], BF16) for i in range(2)]
                KAs = [sb(es, "aKA%d" % i, [80, 8, 1024], BF16) for i in range(2)]
                VAs = [sb(es, "aVA%d" % i, [128, 8, 8, 65], BF16) for i in range(2)]
                KC = sb(es, "aKC", [64, 8, CTX], BF16)
                VC = sb(es, "aVC", [128, 2, 8, 65], BF16)
                PTt = [sb(es, "aPT%d" % i, [128, 512], BF16) for i in range(4)]
                rden = [sb(es, "arden%d" % i, [65, 512], F32) for i in range(2)]
                bcs = [sb(es, "abcs%d" % i, [64, 512], F32) for i in range(2)]
                ast = [sb(es, "aast%d" % i, [64, 512], BF16) for i in range(2)]
                P.dma("pool", TL[:].rearrange("p h x -> p (h x)"), tl_in[l], writes=[TL.r])
                P.dma("pool", BM[64:80], bmask_in, writes=[BM.r])
                for KA in KAs:
                    for hh in range(8):
                        P.dma("pool", KA[64:80, hh, :], amask_in, writes=[KA.r])
                for VA in VAs:
                    P.op("pool", lambda e, VA=VA: e.memset(VA[:, :, :, 64:65], 1.0), writes=[VA.r])
                if pre is not None:
                    pre()
                P.op("pool", lambda e: e.memset(VC[:, :, :, 64:65], 1.0), writes=[VC.r])
                P.dma("sp", KC[:], KT[:, SEQ:TOK].rearrange("(h d) t -> d h t", d=64), reads=[dres["KT"]], writes=[KC.r])
                for cs in range(2):
                    P.dma("sp", VC[:, cs, :, 0:64],
                          VV[SEQ + cs * 128:SEQ + (cs + 1) * 128, :].rearrange("p (h d) -> p h d", d=64),
                          reads=[dres["VV"]], writes=[VC.r])
                tiles = TILES if l == 0 else TILES[:8]
                pcnt = 0
                for qt, (t0, n, s) in enumerate(tiles):
                    QA, KA, VA = QAs[qt % 2], KAs[qt % 2], VAs[qt % 2]
                    P.dma("sp", QA[0:64, :, 0:n], QT[:, t0:t0 + n].rearrange("(h d) t -> d h t", d=64),
                          reads=[dres["QT"]], writes=[QA.r])
                    slots = []
                    if s == 0:
                        R = qt * 8
                        P.op("pool", lambda e, qt=qt, QA=QA: e.tensor_copy(
                            QA[64:80, :, :], BM[64:80, qt:qt + 1, :].broadcast_to([16, 8, 512])),
                            reads=[BM.r], writes=[QA.r])
                        slots = [sp_ for sp_ in range(8) if 0 <= R - 4 + 2 * sp_ <= 62]
                        s_lo, s_hi = slots[0], slots[-1] + 1
                        k0 = (R - 4 + 2 * s_lo) * 64
                        nk = (s_hi - s_lo) * 128
                        P.dma("sp", KA[0:64, :, s_lo * 128:s_hi * 128],
                              KT[:, k0:k0 + nk].rearrange("(h d) t -> d h t", d=64), reads=[dres["KT"]], writes=[KA.r])
                        for sp_ in range(s_lo, s_hi):
                            kk0 = k0 + (sp_ - s_lo) * 128
                            P.dma("sp", VA[:, sp_, :, 0:64],
                                  VV[kk0:kk0 + 128, :].rearrange("p (h d) -> p h d", d=64),
                                  reads=[dres["VV"]], writes=[VA.r])
                    items = []
                    for h in range(8):
                        its = [("w", sp_) for sp_ in slots] + [("c", 0), ("c", 1)]
                        for ii, (kind, sp_) in enumerate(its):
                            items.append((h, kind, sp_, ii, len(its) - 1))

                    def emit_s(j):
                        h, kind, sp_, ii, last = items[j]
                        pS = ps[2 + j % 4]
                        if kind == "w":
                            w0 = 14 - 2 * sp_
                            P.group("pe", [
                                lambda e, pS=pS, h=h, sp_=sp_, n=n, KA=KA, QA=QA: e.matmul(
                                    pS[:, 0:n], KA[0:80, h, sp_ * 128:(sp_ + 1) * 128], QA[0:80, h, 0:n],
                                    start=True, stop=False),
                                lambda e, pS=pS, h=h, w0=w0, n=n: e.matmul(
                                    pS[:, 0:n], identb[:], TL[:, h, w0 * 64:w0 * 64 + n], start=False, stop=True)],
                                reads=[KA.r, QA.r, identb.r, TL.r], writes=[pS.r])
                        else:
                            P.group("pe", [lambda e, pS=pS, h=h, sp_=sp_, n=n, QA=QA: e.matmul(
                                pS[:, 0:n], KC[0:64, h, sp_ * 128:(sp_ + 1) * 128], QA[0:64, h, 0:n],
                                start=True, stop=True)], reads=[KC.r, QA.r], writes=[pS.r])

                    def emit_rest(j):
                        h, kind, sp_, ii, last = items[j]
                        pS = ps[2 + j % 4]
                        pO = ps[h % 2]
                        pt_ = PTt[j % 4]
                        P.op("act", lambda e, pt_=pt_, pS=pS, n=n: e.activation(pt_[:, 0:n], pS[:, 0:n], AF.Exp),
                             reads=[pS.r], writes=[pt_.r])
                        lhs = (lambda sp_=sp_, h=h, VA=VA: VA[:, sp_, h, :]) if kind == "w" else \
                            (lambda sp_=sp_, h=h: VC[:, sp_, h, :])
                        vres = VA.r if kind == "w" else VC.r
                        P.group("pe", [lambda e, pO=pO, lhs=lhs, pt_=pt_, n=n, ii=ii, last=last: e.matmul(
                            pO[0:65, 0:n], lhs(), pt_[:, 0:n], start=(ii == 0), stop=(ii == last))],
                            reads=[vres, pt_.r], writes=[pO.r])
                        if ii == last:
                            fin_q.append((j + 2, lambda h=h, pO=pO: finalize(h, pO)))

                    def finalize(h, pO):
                        if True:
                            rd = rden[h % 2]
                            bc = bcs[h % 2]
                            P.op("dve", lambda e, pO=pO, n=n, rd=rd: e.reciprocal(rd[64:65, 0:n], pO[64:65, 0:n]),
                                 reads=[pO.r], writes=[rd.r])
                            pB = ps[6]
                            P.group("pe", [lambda e, pB=pB, n=n, rd=rd: e.matmul(
                                pB[0:64, 0:n], ones1[64:65, 0:64], rd[64:65, 0:n], start=True, stop=True)],
                                reads=[ones1.r, rd.r], writes=[pB.r])
                            P.op("act", lambda e, pB=pB, n=n, bc=bc: e.activation(bc[:, 0:n], pB[0:64, 0:n], AF.Identity),
                                 reads=[pB.r], writes=[bc.r])
                            a_ = ast[h % 2]
                            P.op("dve", lambda e, a_=a_, pO=pO, n=n, bc=bc: e.tensor_tensor(
                                a_[:, 0:n], pO[0:64, 0:n], bc[:, 0:n], op=ALU.mult), reads=[pO.r, bc.r], writes=[a_.r])
                            P.dma("act", AT[h * 64:(h + 1) * 64, t0:t0 + n], a_[:, 0:n], reads=[a_.r], writes=[dres["AT"]])

                    LOOK = 3
                    fin_q = []
                    for j in range(min(LOOK, len(items))):
                        emit_s(j)
                    for j in range(len(items)):
                        if j + LOOK < len(items):
                            emit_s(j + LOOK)
                        emit_rest(j)
                        while fin_q and fin_q[0][0] <= j:
                            fin_q.pop(0)[1]()
                    while fin_q:
                        fin_q.pop(0)[1]()
                P.flush()

        def merge_weights(es, l):
            WB = [sb(es, "mWB%d" % i, [128, 4, D], BF16) for i in range(3)]
            WO = sb(es, "mWO", [128, 8, D], BF16)

            def load():
                for i, wsrc in enumerate((w_ba, w_bb, w_bc)):
                    P.dma("pool", WB[i][:], wsrc[l].rearrange("(k p) n -> p k n", p=128), writes=[WB[i].r])
                P.dma("pool", WO[:], w_out[l].rearrange("(k p) n -> p k n", p=128), writes=[WO.r])
            return WB, WO, load

        def merge_phase(l, WB, WO):
            with contextlib.ExitStack() as es:
                bt = [sb(es, "mbt%d" % i, [128, 4, 512], BF16) for i in range(3)]
                sg = [sb(es, "msg%d" % i, [128, 8, 512], BF16) for i in range(3)]
                ht = sb(es, "mht", [128, 8, 512], F32)
                ho = sb(es, "mho", [128, 8, 512], F32)
                mT = sb(es, "mmT", [128, 8, 512], BF16)
                tt = [sb(es, "mtt%d" % i, [128, 512], F32) for i in range(3)]
                tiles = TILES if l == 0 else TILES[:8]
                srcs = (AT, RT, PT)
                names = ("AT", "RT", "PT")
                for (t0, n, s) in tiles:
                    for i in range(3):
                        P.dma("sp", bt[i][:, :, 0:n], srcs[i][:, t0:t0 + n].rearrange("(c p) t -> p c t", p=128),
                              reads=[dres[names[i]]], writes=[bt[i].r])
                        P.dma("sp", sg[i][:, :, 0:n], SG[i, :, t0:t0 + n].rearrange("(c p) t -> p c t", p=128),
                              reads=[dres["SG"]], writes=[sg[i].r])
                    P.dma("sp", ht[:, :, 0:n], hT[:, :, t0:t0 + n].rearrange("k p t -> p k t"),
                          reads=[dres["hT"]], writes=[ht.r])
                    for mo in range(8):
                        for i in range(3):
                            pp = nps()
                            P.group("pe", [lambda e, k=k, i=i, pp=pp, mo=mo, n=n: e.matmul(
                                pp[:, 0:n], WB[i][:, k, mo * 128:(mo + 1) * 128], bt[i][:, k, 0:n],
                                start=(k == 0), stop=(k == 3)) for k in range(4)],
                                reads=[WB[i].r, bt[i].r], writes=[pp.r])
                            P.op("dve", lambda e, i=i, pp=pp, mo=mo, n=n: e.tensor_tensor(
                                tt[i][:, 0:n], pp[:, 0:n], sg[i][:, mo, 0:n], op=ALU.mult),
                                reads=[pp.r, sg[i].r], writes=[tt[i].r])
                        P.op("pool", lambda e, n=n: e.tensor_tensor(tt[0][:, 0:n], tt[0][:, 0:n], tt[1][:, 0:n], op=ALU.add),
                             reads=[tt[0].r, tt[1].r], writes=[tt[0].r])
                        P.op("dve", lambda e, mo=mo, n=n: e.tensor_tensor(
                            mT[:, mo, 0:n], tt[0][:, 0:n], tt[2][:, 0:n], op=ALU.add),
                            reads=[tt[0].r, tt[2].r], writes=[mT.r])
                    for mo in range(8):
                        pp = nps()
                        P.group("pe", [lambda e, k=k, pp=pp, mo=mo, n=n: e.matmul(
                            pp[:, 0:n], WO[:, k, mo * 128:(mo + 1) * 128], mT[:, k, 0:n],
                            start=(k == 0), stop=(k == 7)) for k in range(8)], reads=[WO.r, mT.r], writes=[pp.r])
                        P.op("dve", lambda e, pp=pp, mo=mo, n=n, s=s: e.scalar_tensor_tensor(
                            ho[:, mo, 0:n], pp[:, 0:n], mod(l, 2, mo, s), ht[:, mo, 0:n], op0=ALU.mult, op1=ALU.add),
                            reads=[pp.r, mods.r, ht.r], writes=[ho.r])
                    P.dma("act", hT[:, :, t0:t0 + n].rearrange("k p t -> p k t"), ho[:, :, 0:n],
                          reads=[ho.r], writes=[dres["hT"]])
                P.flush()

        def ffn1_phase(l, pre=None):
            with contextlib.ExitStack() as es:
                wg = [sb(es, "fwg%d" % i, [128, 8, 256], BF16) for i in range(2)]
                wu = [sb(es, "fwu%d" % i, [128, 8, 256], BF16) for i in range(2)]
                sgt = [sb(es, "fsg%d" % i, [128, 512], F32) for i in range(2)]
                act = [sb(es, "fac%d" % i, [128, 512], BF16) for i in range(3)]
                tiles = TILES if l == 0 else TILES[:8]
                gv = w_fg[l].rearrange("(k p) n -> p k n", p=128)
                uv = w_fu[l].rearrange("(k p) n -> p k n", p=128)
                cnt = 0
                for jg in range(NHC // 2):
                    g_ = wg[jg % 2]
                    u_ = wu[jg % 2]
                    P.dma("pool", g_[:], gv[:, :, jg * 256:(jg + 1) * 256], writes=[g_.r])
                    P.dma("pool", u_[:], uv[:, :, jg * 256:(jg + 1) * 256], writes=[u_.r])
                    if jg == 1 and pre is not None:
                        pre()
                    for jj in range(2):
                        j = jg * 2 + jj
                        for (t0, n, s) in tiles:
                            pg = nps()
                            P.group("pe", [lambda e, k=k, pg=pg, g_=g_, jj=jj, t0=t0, n=n: e.matmul(
                                pg[:, 0:n], g_[:, k, jj * 128:(jj + 1) * 128], XNh[0][:, k, t0:t0 + n],
                                start=(k == 0), stop=(k == 7)) for k in range(8)], reads=[g_.r, XNh[0].r], writes=[pg.r])
                            pu = nps()
                            P.group("pe", [lambda e, k=k, pu=pu, u_=u_, jj=jj, t0=t0, n=n: e.matmul(
                                pu[:, 0:n], u_[:, k, jj * 128:(jj + 1) * 128], XNh[0][:, k, t0:t0 + n],
                                start=(k == 0), stop=(k == 7)) for k in range(8)], reads=[u_.r, XNh[0].r], writes=[pu.r])
                            sg_ = sgt[cnt % 2]
                            ac = act[cnt % 3]
                            cnt += 1
                            P.op("act", lambda e, sg_=sg_, pg=pg, n=n: e.activation(sg_[:, 0:n], pg[:, 0:n], AF.Silu),
                                 reads=[pg.r], writes=[sg_.r])
                            P.op("dve", lambda e, ac=ac, sg_=sg_, pu=pu, n=n: e.tensor_tensor(
                                ac[:, 0:n], pu[:, 0:n], sg_[:, 0:n], op=ALU.mult), reads=[pu.r, sg_.r], writes=[ac.r])
                            P.dma("act", ACTT[j, :, t0:t0 + n], ac[:, 0:n], reads=[ac.r], writes=[dres["ACTT"]])
                P.flush()

        def ffn2_phase(l, WD):
            with contextlib.ExitStack() as es:
                at = [sb(es, "fat%d" % i, [128, NHC, 512], BF16) for i in range(2)]
                ht = [sb(es, "fht%d" % i, [128, 8, 512], F32) for i in range(2)]
                ho = sb(es, "fho", [128, 8, 512], F32)
                tiles = TILES if l == 0 else TILES[:8]
                for ti, (t0, n, s) in enumerate(tiles):
                    a_ = at[ti % 2]
                    h_ = ht[ti % 2]
                    P.dma("sp", a_[:, :, 0:n], ACTT[:, :, t0:t0 + n].rearrange("j p t -> p j t"),
                          reads=[dres["ACTT"]], writes=[a_.r])
                    P.dma("sp", h_[:, :, 0:n], hT[:, :, t0:t0 + n].rearrange("k p t -> p k t"),
                          reads=[dres["hT"]], writes=[h_.r])
                    for mo in range(8):
                        pp = nps()
                        P.group("pe", [lambda e, j=j, pp=pp, mo=mo, a_=a_, n=n: e.matmul(
                            pp[:, 0:n], WD[:, j, mo * 128:(mo + 1) * 128], a_[:, j, 0:n],
                            start=(j == 0), stop=(j == NHC - 1)) for j in range(NHC)], reads=[WD.r, a_.r], writes=[pp.r])
                        P.op("dve", lambda e, pp=pp, mo=mo, n=n, s=s, h_=h_: e.scalar_tensor_tensor(
                            ho[:, mo, 0:n], pp[:, 0:n], mod(l, 5, mo, s), h_[:, mo, 0:n], op0=ALU.mult, op1=ALU.add),
                            reads=[pp.r, mods.r, h_.r], writes=[ho.r])
                    P.dma("act", hT[:, :, t0:t0 + n].rearrange("k p t -> p k t"), ho[:, :, 0:n],
                          reads=[ho.r], writes=[dres["hT"]])
                P.flush()

        def out_phase():
            with contextlib.ExitStack() as es:
                hb = [sb(es, "oh%d" % i, [128, 8, 512], F32) for i in range(2)]
                sqs = [sb(es, "osq%d" % i, [128, 8, 512], BF16) for i in range(2)]
                rss = [sb(es, "ors%d" % i, [128, 512], F32) for i in range(2)]
                yys = [sb(es, "oyy%d" % i, [128, 8, 512], F32) for i in range(2)]
                ot = [sb(es, "oot%d" % i, [128, D], F32) for i in range(2)]
                cnt = 0
                def stage_a(ti):
                    t0, n, s = TILES[ti]
                    ht, sq, rs = hb[ti % 2], sqs[ti % 2], rss[ti % 2]
                    P.dma("sp", ht[:], hT[:, :, t0:t0 + n].rearrange("k p t -> p k t"), reads=[dres["hT"]], writes=[ht.r])
                    P.op("act", lambda e: e.activation(sq[:], ht[:], AF.Square), reads=[ht.r], writes=[sq.r])
                    pp = nps()
                    P.group("pe", [lambda e, k=k: e.matmul(pp[:], onesb[:], sq[:, k, :], start=(k == 0), stop=(k == 7))
                                   for k in range(8)], reads=[onesb.r, sq.r], writes=[pp.r])
                    P.op("act", lambda e: e.activation(rs[:], pp[:], AF.Sqrt, bias=epsD[:, 0:1]),
                         reads=[pp.r, epsD.r], writes=[rs.r])
                    P.op("dve", lambda e: e.reciprocal(rs[:], rs[:]), reads=[rs.r], writes=[rs.r])

                def stage_b(ti):
                    t0, n, s = TILES[ti]
                    ht, rs, yy = hb[ti % 2], rss[ti % 2], yys[ti % 2]
                    for kc in range(8):
                        P.op("dve", lambda e, kc=kc: e.scalar_tensor_tensor(
                            yy[:, kc, :], ht[:, kc, :], gfin[:, kc:kc + 1], rs[:], op0=ALU.mult, op1=ALU.mult),
                            reads=[ht.r, gfin.r, rs.r], writes=[yy.r])
                    for sub in range(4):
                        o_ = ot[(ti * 4 + sub) % 2]
                        for half in range(2):
                            pt = nps()
                            for kk in range(4):
                                kc = half * 4 + kk
                                P.op("pe", lambda e, pt=pt, kk=kk, kc=kc, sub=sub: e.transpose(
                                    pt[:, kk * 128:(kk + 1) * 128], yy[:, kc, sub * 128:(sub + 1) * 128], ident[:]),
                                    reads=[yy.r, ident.r], writes=[pt.r])
                            if half == 0:
                                P.op("dve", lambda e, o_=o_, pt=pt: e.tensor_copy(o_[:, 0:512], pt[:]),
                                     reads=[pt.r], writes=[o_.r])
                            else:
                                P.op("act", lambda e, o_=o_, pt=pt: e.activation(o_[:, 512:1024], pt[:], AF.Identity),
                                     reads=[pt.r], writes=[o_.r])
                        P.dma("pool", out_ap[t0 + sub * 128:t0 + (sub + 1) * 128, :], o_[:],
                              reads=[o_.r], writes=[r_out], is_output=True)

                stage_a(0)
                for ti in range(8):
                    if ti + 1 < 8:
                        stage_a(ti + 1)
                    stage_b(ti)
                P.flush(final=True)

        def with_xn(fns):
            with contextlib.ExitStack() as xs:
                XNh[0] = sb(xs, "XN", [128, 8, TOK], BF16)
                for f in fns:
                    f()
                XNh[0] = None

        for l in range(depth):
            last = (l == DEPTH - 1)
            retention_tables(l)
            with_xn([lambda: norm_phase(l, 0, TILES, dump=("XNd" in debug_outs)),
                     lambda: proj_phase(l), lambda: pool_phase(l)])
            retention_phase(l)
            with contextlib.ExitStack() as ms:
                WB, WO, mload = merge_weights(ms, l)
                na_phase(l, pre=mload)
                merge_phase(l, WB, WO)
            with contextlib.ExitStack() as fs:
                WD = sb(fs, "fWD", [128, NHC, D], BF16)

                def wdload():
                    P.dma("pool", WD[:], w_fd[l].rearrange("(j p) n -> p j n", p=128), writes=[WD.r])
                with_xn([lambda: norm_phase(l, 1, TILES[:8] if last else TILES),
                         lambda: ffn1_phase(l, pre=wdload)])
                ffn2_phase(l, WD)
        out_phase()
    return nc


def _const_tables():
    f32 = np.float32
    ident = np.eye(128, dtype=f32)
    amask = np.zeros((16, 1024), f32)
    for i in range(16):
        amask[i, i * 64:(i + 1) * 64] = 1.0
    bmask = np.full((16, 8, 512), NEGM, f32)
    for qt in range(8):
        R = qt * 8
        for i in range(16):
            kr = R - 4 + i
            for rho in range(8):
                r = R + rho
                r0 = min(max(r - 4, 0), 56)
                if 0 <= kr <= 63 and r0 <= kr < r0 + 8:
                    bmask[i, qt, rho * 64:(rho + 1) * 64] = 0.0
    t = np.arange(SEQ)
    row = (t // 64).astype(f32)
    col = (t % 64).astype(f32)
    inv = (10000.0 ** (-np.arange(16, dtype=f32) / 16)).astype(f32)
    C = np.zeros((64, SEQ), f32)
    S = np.zeros((64, SEQ), f32)
    for d in range(64):
        pos = row if d < 32 else col
        i = d % 16
        ang = (pos * inv[i]).astype(f32)
        C[d] = np.cos(ang)
        sgn = -1.0 if (d % 32) < 16 else 1.0
        S[d] = sgn * np.sin(ang)
    ropeC = np.concatenate([C, C], 0)
    ropeS = np.concatenate([S, S], 0)
    j = np.arange(128)[:, None].astype(f32)
    i = np.arange(128)[None, :].astype(f32)
    EF = np.maximum(i - j, 0)
    MF = (i >= j).astype(f32)
    EB = np.maximum(j - i, 0)
    MB = (j > i).astype(f32)
    ZEf = np.repeat(127.0 - j, 64, 1)
    ZEb = np.repeat(j, 64, 1)
    XE = np.zeros((128, 128), f32)
    XE[0:64] = i + 1.0
    XE[64:128] = 128.0 - i
    rconst = np.concatenate([EF, MF, EB, MB, ZEf, ZEb, XE], 1).astype(f32)

    def rc(n):
        tt = np.arange(n)
        out = np.zeros((4, n), f32)
        for gi, w in enumerate((2, 4, 8, 16)):
            lo = np.clip(tt - w // 2, 0, n)
            hi = np.clip(tt - w // 2 + w, 0, n)
            out[gi] = 1.0 / (hi - lo).astype(f32)
        return out
    return dict(ident=ident, amask=amask, bmask=bmask, ropeC=ropeC, ropeS=ropeS, rconst=rconst,
                prc=rc(SEQ), prcc=rc(CTX))


def _tl_table(rpb):
    L = rpb.shape[0]
    a = np.arange(2)[:, None, None, None]
    kc = np.arange(64)[None, :, None, None]
    u = np.arange(22)[None, None, :, None]
    qc = np.arange(64)[None, None, None, :]
    dr = a + 10 - u + 0 * kc + 0 * qc
    wstart = np.clip(qc - 8, 0, 48)
    ok = (np.abs(dr) <= 7) & (kc >= wstart) & (kc < wstart + 16)
    dri = np.clip(dr + 7, 0, 14)
    dci = np.clip(kc - qc + 15 + 0 * dr, 0, 30)
    out = np.empty((L, 2, 64, 8, 22, 64), np.float32)
    for l in range(L):
        for h in range(8):
            g = rpb[l, h][dri, dci]
            out[l, :, :, h] = np.where(ok, g, np.float32(NEGM))
    return out.reshape(L, 128, 8 * 22 * 64)


_CONSTS = None


def make_in_maps(inp):
    global _CONSTS
    if _CONSTS is None:
        _CONSTS = _const_tables()
    f32 = np.float32
    g = {k: np.asarray(v, dtype=f32) for k, v in inp.items()}
    L = DEPTH

    def col(v):
        return np.ascontiguousarray(v.reshape(L, -1, 128).transpose(2, 0, 1))
    fin = np.broadcast_to(g["final_norm_g"][None], (L, D))
    vecs = np.concatenate([col(g["norm1_g"]), col(g["norm2_g"]), col(g["ret_gn_g"]), col(g["pool_scale"]),
                           col(np.ascontiguousarray(fin))], axis=2)
    logits = np.ascontiguousarray(np.broadcast_to(
        np.concatenate([g["ret_logit_f"], g["ret_logit_b"]], 1)[None], (128, L, 8)))
    shared = dict(w_ada=g["w_ada"], b_ada=g["b_ada"].reshape(L, 1, 6 * D), vecs=np.ascontiguousarray(vecs),
                  logits=logits, w_in=g["w_in"], pool_w=g["pool_w"], w_branch_a=g["w_branch_a"],
                  w_branch_b=g["w_branch_b"], w_branch_c=g["w_branch_c"], w_out=g["w_out"],
                  w_ffn_gate=g["w_ffn_gate"], w_ffn_up=g["w_ffn_up"], w_ffn_down=g["w_ffn_down"],
                  tl=_tl_table(g["na_rpb"]))
    shared.update(_CONSTS)
    maps = []
    cc = g["c_ctx"].reshape(128, 8)
    for b in range(g["x"].shape[0]):
        m = dict(shared)
        m["x"] = np.ascontiguousarray(g["x"][b])
        m["ctx"] = np.ascontiguousarray(g["ctx"][b])
        m["cvec"] = np.ascontiguousarray(np.stack([g["c"][b].reshape(128, 8), cc], axis=2))
        maps.append(m)
    return maps


_NC = None


def kernel(**inputs):
    global _NC
    maps = make_in_maps(inputs)
    if _NC is None:
        _NC = build_program()
    res = run_bass_kernel_spmd(_NC, maps, core_ids=list(range(len(maps))))
    return np.stack([np.asarray(r["out"], dtype=np.float32) for r in res.results], axis=0)
```

```python
import contextlib
import numpy as np
import concourse.bass as bass
import concourse.mybir as mybir
from concourse.bass_utils import run_bass_kernel_spmd

F32 = mybir.dt.float32
BF16 = mybir.dt.bfloat16
AF = mybir.ActivationFunctionType
ALU = mybir.AluOpType

ENGS = ("pe", "act", "dve", "pool", "sp")
ENGN = {"pe": "tensor", "act": "scalar", "dve": "vector", "pool": "gpsimd", "sp": "sync"}
SEM_CAP = 30000
DMA_ROT = 8
NSEM = 96

D = 1024
SEQ = 4096
CTX = 256
TOK = SEQ + CTX
DEPTH = 2
HID = 2816
NHC = HID // 128
NORM_EPS = 1e-6
GN_EPS = 1e-5
NEGM = -30000.0
TILES = [(i * 512, 512, 0) for i in range(8)] + [(SEQ, CTX, 1)]
DEBUG_OUTS = ()


class Res:
    __slots__ = ("w", "r")

    def __init__(self):
        self.w = None
        self.r = {}


class Prog:
    def __init__(self, nc, sems):
        self.nc = nc
        self.sems = sems
        self.semmap = {}
        self.q = {e: [] for e in ENGS}
        self.stream = {}
        self.waited = {e: {} for e in ENGS}
        self.dma_rot = {e: 0 for e in ENGS}
        self.out_events = []

    def _sem(self, key):
        if key not in self.semmap:
            self.semmap[key] = self.sems[len(self.semmap)]
        return self.semmap[key]

    def _next_event(self, stream, inc):
        st = self.stream.setdefault(stream, [0, 0])
        if st[1] + inc > SEM_CAP:
            st[0] += 1
            st[1] = 0
        st[1] += inc
        return ((stream, st[0]), st[1])

    def _peek_prev(self, stream):
        st = self.stream.get(stream)
        if st is None or st[1] == 0:
            return None
        return ((stream, st[0]), st[1])

    def _waits(self, eng, reads, writes, extra=()):
        need = {}

        def add(ev):
            if ev is None:
                return
            k, v = ev
            if need.get(k, 0) < v:
                need[k] = v
        for r in reads:
            add(r.w)
        for w in writes:
            add(w.w)
            for k, v in w.r.items():
                add((k, v))
        for ev in extra:
            add(ev)
        wd = self.waited[eng]
        for k, v in need.items():
            if eng == "pe" and k[0] == "c:pe":
                continue
            if wd.get(k, 0) >= v:
                continue
            wd[k] = v
            self.q[eng].append(("wait", k, v))

    def _mark(self, ev, reads, writes):
        k, v = ev
        for r in reads:
            if r.r.get(k, 0) < v:
                r.r[k] = v
        for w in writes:
            w.w = ev
            w.r = {}

    def op(self, eng, fn, reads=(), writes=()):
        self._waits(eng, reads, writes)
        ev = self._next_event("c:" + eng, 1)
        self.q[eng].append(("op", fn, ev[0]))
        self._mark(ev, reads, writes)

    def group(self, eng, fns, reads=(), writes=()):
        self._waits(eng, reads, writes)
        ev = self._next_event("c:" + eng, 1)
        for f in fns[:-1]:
            self.q[eng].append(("op", f, None))
        self.q[eng].append(("op", fns[-1], ev[0]))
        self._mark(ev, reads, writes)

    def dma(self, eng, out, in_, reads=(), writes=(), is_output=False, **kw):
        j = self.dma_rot[eng]
        self.dma_rot[eng] = (j + 1) % DMA_ROT
        stream = "d:%s:%d" % (eng, j)
        prev = self._peek_prev(stream)
        self._waits(eng, reads, writes, extra=(prev,) if prev else ())
        ev = self._next_event(stream, 16)
        self.q[eng].append(("dma", out, in_, ev[0], kw))
        self._mark(ev, reads, writes)
        if is_output:
            self.out_events.append(ev)

    def barrier(self):
        for e in ENGS:
            for stream, st in self.stream.items():
                if st[1] == 0 or (e == "pe" and stream == "c:pe"):
                    continue
                key, v = (stream, st[0]), st[1]
                if self.waited[e].get(key, 0) < v:
                    self.waited[e][key] = v
                    self.q[e].append(("wait", key, v))

    def flush(self, final=False):
        self.barrier()
        if final:
            for k, v in self.out_events:
                if self.waited["sp"].get(k, 0) < v:
                    self.waited["sp"][k] = v
                    self.q["sp"].append(("wait", k, v))
        nc = self.nc
        with nc.Block() as block:
            for e in ENGS:
                ql = self.q[e]

                def run(engine, ql=ql):
                    pend = []
                    for it in ql:
                        if it[0] == "wait":
                            pend.append(it)
                            continue
                        for w in pend[:-1]:
                            engine.wait_ge(self._sem(w[1]), w[2])
                        if it[0] == "op":
                            ins = it[1](engine)
                            if pend:
                                ins._wait_ge(self._sem(pend[-1][1]), pend[-1][2])
                            if it[2] is not None:
                                ins.then_inc(self._sem(it[2]), 1)
                        else:
                            _, out, in_, k, kw = it
                            ins = engine.dma_start(out=out, in_=in_, **kw)
                            if pend:
                                ins._wait_ge(self._sem(pend[-1][1]), pend[-1][2])
                            ins.then_inc(self._sem(k), 16)
                        pend = []
                    for w in pend:
                        engine.wait_ge(self._sem(w[1]), w[2])
                getattr(block, ENGN[e])(run)
        self.q = {e: [] for e in ENGS}


class T:
    def __init__(self, h):
        self.h = h
        self.r = Res()

    def __getitem__(self, k):
        return self.h[k]


def build_program(depth=DEPTH, debug_outs=(), stop_after=None):
    nc = bass.Bass("TRN2", target_bir_lowering=False)

    def din(name, shape, dt=F32):
        return nc.dram_tensor(name, list(shape), dt, kind="ExternalInput").ap()

    dres = {}

    def dscr(name, shape, dt):
        kind = "ExternalOutput" if name in debug_outs else "Internal"
        ap = nc.dram_tensor(name, list(shape), dt, kind=kind).ap()
        dres[name] = Res()
        return ap

    x_in = din("x", [SEQ, D])
    ctx_in = din("ctx", [CTX, D])
    cvec_in = din("cvec", [128, 8, 2])
    w_ada = din("w_ada", [DEPTH, D, 6 * D])
    b_ada = din("b_ada", [DEPTH, 1, 6 * D])
    vecs_in = din("vecs", [128, DEPTH, 32])
    logit_in = din("logits", [128, DEPTH, 8])
    w_in = din("w_in", [DEPTH, D, 6656])
    pool_w = din("pool_w", [DEPTH, 4, 128, 128])
    w_ba = din("w_branch_a", [DEPTH, 512, D])
    w_bb = din("w_branch_b", [DEPTH, 512, D])
    w_bc = din("w_branch_c", [DEPTH, 512, D])
    w_out = din("w_out", [DEPTH, D, D])
    w_fg = din("w_ffn_gate", [DEPTH, D, HID])
    w_fu = din("w_ffn_up", [DEPTH, D, HID])
    w_fd = din("w_ffn_down", [DEPTH, HID, D])
    tl_in = din("tl", [DEPTH, 128, 8 * 22 * 64])
    ident_in = din("ident", [128, 128])
    amask_in = din("amask", [16, 1024])
    bmask_in = din("bmask", [16, 8, 512])
    ropeC_in = din("ropeC", [128, SEQ])
    ropeS_in = din("ropeS", [128, SEQ])
    rc_in = din("rconst", [128, 768])
    prc_in = din("prc", [4, SEQ])
    prcc_in = din("prcc", [4, CTX])
    out_ap = nc.dram_tensor("out", [SEQ, D], F32, kind="ExternalOutput").ap()
    r_out = Res()

    hT = dscr("hT", [8, 128, TOK], F32)
    QT = dscr("QT", [512, TOK], BF16)
    KT = dscr("KT", [512, TOK], BF16)
    VV = dscr("VV", [TOK, 512], BF16)
    RQT = dscr("RQT", [256, TOK], BF16)
    RKT = dscr("RKT", [256, TOK], BF16)
    KZ = dscr("KZ", [TOK, 512], BF16)
    RV = dscr("RV", [TOK, 512], BF16)
    GT = dscr("GT", [512, TOK], BF16)
    SG = dscr("SG", [3, D, TOK], BF16)
    PT = dscr("PT", [512, TOK], BF16)
    AT = dscr("AT", [512, TOK], BF16)
    RT = dscr("RT", [512, TOK], BF16)
    ACTT = dscr("ACTT", [NHC, 128, TOK], BF16)
    XNd = dscr("XNd", [8, 128, TOK], BF16)

    with contextlib.ExitStack() as top:
        sems = [top.enter_context(nc.semaphore("s%d" % i)) for i in range(NSEM)]
        P = Prog(nc, sems)

        uniq = [0]

        def sb(es, name, shape, dt):
            uniq[0] += 1
            return T(es.enter_context(nc.sbuf_tensor("%s_%d" % (name, uniq[0]), list(shape), dt)))

        def psum(es, name, shape, dt=F32):
            return T(es.enter_context(nc.psum_tensor(name, list(shape), dt)))

        ps = [psum(top, "ps%d" % i, [128, 512]) for i in range(7)]
        psb = psum(top, "psb", [128, 1024], BF16)
        ident = sb(top, "ident", [128, 128], F32)
        identb = sb(top, "identb", [128, 128], BF16)
        onesb = sb(top, "onesb", [128, 128], BF16)
        ones1 = sb(top, "ones1", [128, 64], F32)
        mods = sb(top, "mods", [128, DEPTH, 48, 2], F32)
        vecs = sb(top, "vecs", [128, DEPTH, 32], F32)
        gs = sb(top, "gs", [128, DEPTH, 2, 8, 2], F32)
        gfin = sb(top, "gfin", [128, 8], F32)
        XNh = [None]
        rcn = sb(top, "rcn", [128, 768], F32)
        DT = sb(top, "DT", [128, 4, 128], F32)
        ZT = sb(top, "ZT", [128, 4, 2, 64], F32)
        XI = sb(top, "XI", [128, 4, 128], F32)
        DEC = sb(top, "DEC", [128, 4, 128], F32)
        lg = sb(top, "lg", [128, 8], F32)

        psi = [0]

        def nps():
            psi[0] = (psi[0] + 1) % 7
            return ps[psi[0]]

        P.dma("sp", ident[:], ident_in, writes=[ident.r])
        P.dma("pool", identb[:], ident_in, writes=[identb.r])
        P.dma("sp", vecs[:], vecs_in, writes=[vecs.r])
        P.dma("sp", rcn[:], rc_in, writes=[rcn.r])
        P.op("dve", lambda e: e.memset(onesb[:], 1.0), writes=[onesb.r])
        P.op("dve", lambda e: e.memset(ones1[:], 1.0), writes=[ones1.r])
        epsD = sb(top, "epsD", [128, 2], F32)
        P.op("dve", lambda e: e.memset(epsD[:, 0:1], float(D * NORM_EPS)), writes=[epsD.r])
        P.op("dve", lambda e: e.memset(epsD[:, 1:2], float(GN_EPS)), writes=[epsD.r])

        def mod(l, j, kc, s):
            return mods[:, l, j * 8 + kc, s:s + 1]

        with contextlib.ExitStack() as es:
            cv = sb(es, "cv", [128, 8, 2], F32)
            sc = sb(es, "sc", [128, 8, 2], F32)
            ba = sb(es, "ba", [2, DEPTH, 6 * D], F32)
            mrow = sb(es, "mrow", [2, DEPTH, 6 * D], F32)
            wa = [sb(es, "wa%d" % i, [128, 8, 256], F32) for i in range(3)]
            xt = [sb(es, "xt%d" % i, [128, D], F32) for i in range(2)]
            hst = [sb(es, "hst%d" % i, [128, 8, 512], F32) for i in range(2)]
            tmpm = sb(es, "tmpm", [128, 8, 2], F32)
            P.dma("sp", cv[:], cvec_in, writes=[cv.r])
            P.dma("sp", ba[:], b_ada.rearrange("l o n -> o l n").broadcast_to([2, DEPTH, 6 * D]), writes=[ba.r])
            P.op("act", lambda e: e.activation(sc[:], cv[:], AF.Silu), reads=[cv.r], writes=[sc.r])
            pm = ps[0]

            NAG = 24

            def ada_group(l, cg):
                wav = w_ada[l].rearrange("(p k) n -> p k n", k=8)
                wt = wa[cg % 3]
                c0 = cg * 256
                hf = (cg % 2) * 256
                P.dma("sp", wt[:], wav[:, :, c0:c0 + 256], writes=[wt.r])
                P.group("pe", [lambda e, k=k: e.matmul(
                    pm[0:2, hf:hf + 256], sc[:, k, :], wt[:, k, :], start=(k == 0), stop=(k == 7)) for k in range(8)],
                    reads=[wt.r, sc.r], writes=[pm.r])
                P.op("dve", lambda e: e.tensor_copy(mrow[0:2, l, c0:c0 + 256], pm[0:2, hf:hf + 256]),
                     reads=[pm.r], writes=[mrow.r])

            for cg in range(NAG):
                ada_group(0, cg)
            pending = [(l, cg) for l in range(1, depth) for cg in range(NAG)]
            xb = [0]

            def xps():
                xb[0] = xb[0] % 6 + 1
                return ps[xb[0]]
            cnt = 0
            for ti, (t0, n, s) in enumerate(TILES):
                hs = hst[ti % 2]
                for sub in range(n // 128):
                    xx = xt[cnt % 2]
                    cnt += 1
                    src = x_in[t0 + sub * 128:t0 + (sub + 1) * 128, :] if s == 0 else \
                        ctx_in[sub * 128:(sub + 1) * 128, :]
                    P.dma("sp", xx[:], src, writes=[xx.r])
                    for half in range(2):
                        pt = xps()
                        for kk in range(4):
                            kc = half * 4 + kk
                            P.op("pe", lambda e, pt=pt, kk=kk, kc=kc, xx=xx: e.transpose(
                                pt[:, kk * 128:(kk + 1) * 128], xx[:, kc * 128:(kc + 1) * 128], ident[:]),
                                reads=[xx.r, ident.r], writes=[pt.r])
                        if half == 0:
                            P.op("dve", lambda e, pt=pt, hs=hs, half=half, sub=sub: e.tensor_copy(
                                hs[:, half * 4:half * 4 + 4, sub * 128:(sub + 1) * 128],
                                pt[:].rearrange("p (k t) -> p k t", k=4)), reads=[pt.r], writes=[hs.r])
                        else:
                            P.op("act", lambda e, pt=pt, hs=hs, half=half, sub=sub: e.activation(
                                hs[:, half * 4:half * 4 + 4, sub * 128:(sub + 1) * 128],
                                pt[:].rearrange("p (k t) -> p k t", k=4), AF.Identity), reads=[pt.r], writes=[hs.r])
                    if pending and cnt % 2 == 0:
                        ada_group(*pending.pop(0))
                P.dma("pool", hT[:, :, t0:t0 + n].rearrange("k p t -> p k t"), hs[:, :, 0:n],
                      reads=[hs.r], writes=[dres["hT"]])
            while pending:
                ada_group(*pending.pop(0))
            P.op("dve", lambda e: e.tensor_tensor(mrow[:], mrow[:], ba[:], op=ALU.add),
                 reads=[mrow.r, ba.r], writes=[mrow.r])
            fns = []
            for l in range(depth):
                for m in range(48):
                    col = (l * 48 + m) * 2
                    fns.append(lambda e, l=l, m=m, col=col: e.matmul(
                        pm[:, col:col + 2], mrow[0:2, l, m * 128:(m + 1) * 128], ident[0:2, 0:2],
                        start=True, stop=True))
            P.group("pe", fns, reads=[mrow.r, ident.r], writes=[pm.r])
            P.op("dve", lambda e: e.tensor_copy(
                mods[:, 0:depth].rearrange("p l m s -> p (l m s)"), pm[:, 0:depth * 96]),
                reads=[pm.r], writes=[mods.r])
            for l in range(depth):
                for w in range(2):
                    sc0 = 8 + 24 * w
                    P.op("dve", lambda e, l=l, sc0=sc0: e.tensor_scalar(
                        tmpm[:], mods[:, l, sc0:sc0 + 8, :], 1.0, 32.0, op0=ALU.add, op1=ALU.mult),
                        reads=[mods.r], writes=[tmpm.r])
                    for s in range(2):
                        P.op("dve", lambda e, l=l, w=w, s=s: e.tensor_tensor(
                            gs[:, l, w, :, s], tmpm[:, :, s], vecs[:, l, 8 * w:8 * w + 8], op=ALU.mult),
                            reads=[tmpm.r, vecs.r], writes=[gs.r])
            P.op("dve", lambda e: e.tensor_scalar(gfin[:], vecs[:, 0, 24:32], 32.0, None, op0=ALU.mult),
                 reads=[vecs.r], writes=[gfin.r])
            P.flush()

        def norm_phase(l, w, tiles, dump=False):
            with contextlib.ExitStack() as es:
                hb = [sb(es, "nh%d" % i, [128, 8, 512], F32) for i in range(2)]
                sqs = [sb(es, "nsq%d" % i, [128, 8, 512], BF16) for i in range(2)]
                rss = [sb(es, "nrs%d" % i, [128, 512], F32) for i in range(2)]
                tmp = [sb(es, "ntm%d" % i, [128, 512], F32) for i in range(4)]
                def stage_a(ti):
                    t0, n, s = tiles[ti]
                    ht, sq, rs = hb[ti % 2], sqs[ti % 2], rss[ti % 2]
                    P.dma("sp", ht[:, :, 0:n], hT[:, :, t0:t0 + n].rearrange("k p t -> p k t"),
                          reads=[dres["hT"]], writes=[ht.r])
                    P.op("act", lambda e: e.activation(sq[:, :, 0:n], ht[:, :, 0:n], AF.Square),
                         reads=[ht.r], writes=[sq.r])
                    pp = nps()
                    P.group("pe", [lambda e, k=k: e.matmul(
                        pp[:, 0:n], onesb[:], sq[:, k, 0:n], start=(k == 0), stop=(k == 7)) for k in range(8)],
                        reads=[onesb.r, sq.r], writes=[pp.r])
                    P.op("act", lambda e: e.activation(
                        rs[:, 0:n], pp[:, 0:n], AF.Sqrt, bias=epsD[:, 0:1]), reads=[pp.r, epsD.r], writes=[rs.r])
                    P.op("dve", lambda e: e.reciprocal(rs[:, 0:n], rs[:, 0:n]), reads=[rs.r], writes=[rs.r])

                def stage_b(ti):
                    t0, n, s = tiles[ti]
                    ht, rs = hb[ti % 2], rss[ti % 2]
                    for kc in range(8):
                        tm = tmp[kc % 4]
                        P.op("dve", lambda e, tm=tm, kc=kc: e.scalar_tensor_tensor(
                            tm[:, 0:n], ht[:, kc, 0:n], gs[:, l, w, kc, s:s + 1], rs[:, 0:n],
                            op0=ALU.mult, op1=ALU.mult), reads=[ht.r, gs.r, rs.r], writes=[tm.r])
                        P.op("act", lambda e, tm=tm, kc=kc: e.activation(
                            XNh[0][:, kc, t0:t0 + n], tm[:, 0:n], AF.Identity, bias=mod(l, 3 * w, kc, s)),
                            reads=[tm.r, mods.r], writes=[XNh[0].r])
                    if dump:
                        P.dma("act", XNd[:, :, t0:t0 + n].rearrange("k p t -> p k t"), XNh[0][:, :, t0:t0 + n],
                              reads=[XNh[0].r], writes=[dres["XNd"]])

                stage_a(0)
                for ti in range(len(tiles)):
                    if ti + 1 < len(tiles):
                        stage_a(ti + 1)
                    stage_b(ti)
                P.flush()

        def retention_tables(l):
            with contextlib.ExitStack() as es:
                lgt = sb(es, "lgt", [128, 8], F32)
                t1 = sb(es, "rt1", [128, 128], F32)
                t2 = sb(es, "rt2", [128, 128], F32)
                c128 = sb(es, "c128", [128, 128], F32)
                P.dma("sp", lgt[:], logit_in[:, l, :], writes=[lgt.r])
                P.op("dve", lambda e: e.memset(c128[:], 128.0), writes=[c128.r])
                P.op("act", lambda e: e.activation(lgt[:], lgt[:], AF.Exp, scale=-1.0), reads=[lgt.r], writes=[lgt.r])
                P.op("act", lambda e: e.activation(lgt[:], lgt[:], AF.Ln, bias=1.0), reads=[lgt.r], writes=[lgt.r])
                P.op("dve", lambda e: e.tensor_scalar(lg[:], lgt[:], -1.0, None, op0=ALU.mult),
                     reads=[lgt.r], writes=[lg.r])
                EF, MF, EB, MB = (rcn[:, 0:128], rcn[:, 128:256], rcn[:, 256:384], rcn[:, 384:512])
                ZEf, ZEb, XE = rcn[:, 512:576], rcn[:, 576:640], rcn[:, 640:768]
                for h in range(4):
                    lf = lg[:, h:h + 1]
                    lb = lg[:, 4 + h:5 + h]
                    P.op("act", lambda e, lf=lf: e.activation(t1[:], EF, AF.Exp, scale=lf),
                         reads=[lg.r, rcn.r], writes=[t1.r])
                    P.op("dve", lambda e: e.tensor_tensor(t1[:], t1[:], MF, op=ALU.mult),
                         reads=[t1.r, rcn.r], writes=[t1.r])
                    P.op("act", lambda e, lb=lb: e.activation(t2[:], EB, AF.Exp, scale=lb),
                         reads=[lg.r, rcn.r], writes=[t2.r])
                    P.op("dve", lambda e: e.tensor_tensor(t2[:], t2[:], MB, op=ALU.mult),
                         reads=[t2.r, rcn.r], writes=[t2.r])
                    P.op("dve", lambda e, h=h: e.tensor_tensor(DT[:, h, :], t1[:], t2[:], op=ALU.add),
                         reads=[t1.r, t2.r], writes=[DT.r])
                    P.op("act", lambda e, h=h, lf=lf: e.activation(ZT[:, h, 0, :], ZEf, AF.Exp, scale=lf),
                         reads=[lg.r, rcn.r], writes=[ZT.r])
                    P.op("act", lambda e, h=h, lb=lb: e.activation(ZT[:, h, 1, :], ZEb, AF.Exp, scale=lb),
                         reads=[lg.r, rcn.r], writes=[ZT.r])
                    P.op("act", lambda e, h=h: e.activation(XI[0:64, h, :], rcn[0:64, 640:768], AF.Exp,
                                                            scale=lg[0:64, h:h + 1]),
                         reads=[lg.r, rcn.r], writes=[XI.r])
                    P.op("act", lambda e, h=h: e.activation(XI[64:128, h, :], rcn[64:128, 640:768], AF.Exp,
                                                            scale=lg[64:128, 4 + h:5 + h]),
                         reads=[lg.r, rcn.r], writes=[XI.r])
                    P.op("act", lambda e, h=h: e.activation(DEC[0:64, h, :], c128[0:64, :], AF.Exp,
                                                            scale=lg[0:64, h:h + 1]),
                         reads=[lg.r, c128.r], writes=[DEC.r])
                    P.op("act", lambda e, h=h: e.activation(DEC[64:128, h, :], c128[64:128, :], AF.Exp,
                                                            scale=lg[64:128, 4 + h:5 + h]),
                         reads=[lg.r, c128.r], writes=[DEC.r])
                P.flush()

        def proj_phase(l):
            with contextlib.ExitStack() as es:
                wb = [sb(es, "pw%d" % i, [128, 8, 512], BF16) for i in range(2)]
                wrot = sb(es, "pwrot", [128, 8, 512], BF16)
                stg = [sb(es, "pst%d" % i, [128, 4, 512], BF16) for i in range(2)]
                rC = sb(es, "ropeC", [128, 512], F32)
                rS = sb(es, "ropeS", [128, 512], F32)
                r1 = [sb(es, "pr1%d" % i, [128, 512], F32) for i in range(2)]
                r2 = [sb(es, "pr2%d" % i, [128, 512], F32) for i in range(2)]
                kzs = sb(es, "kzs", [128, 4, 512], BF16)
                wv = w_in[l].rearrange("(k p) n -> p k n", p=128)
                wcnt = [0]

                def loadw(c0, width=512):
                    wt = wb[wcnt[0] % 2]
                    wcnt[0] += 1
                    P.dma("pool", wt[:, :, 0:width], wv[:, :, c0:c0 + width], writes=[wt.r])
                    return wt

                scnt = [0]

                def nstg():
                    scnt[0] += 1
                    return stg[scnt[0] % 2]

                def fm_mm(pp, wt, c, t0, n):
                    P.group("pe", [lambda e, k=k: e.matmul(
                        pp[:, 0:n], wt[:, k, c * 128:(c + 1) * 128], XNh[0][:, k, t0:t0 + n],
                        start=(k == 0), stop=(k == 7)) for k in range(8)],
                        reads=[wt.r, XNh[0].r], writes=[pp.r])

                def fm_group(c0, dst, func, scale=1.0, nchunks=4, tiles=TILES):
                    wt = loadw(c0, nchunks * 128)
                    for (t0, n, s) in tiles:
                        st = nstg()
                        for c in range(nchunks):
                            pp = nps()
                            fm_mm(pp, wt, c, t0, n)
                            P.op("act", lambda e, st=st, c=c, pp=pp, n=n: e.activation(
                                st[:, c, 0:n], pp[:, 0:n], func, scale=scale), reads=[pp.r], writes=[st.r])
                        P.dma("act", dst[:, t0:t0 + n].rearrange("(c p) t -> p c t", p=128), st[:, 0:nchunks, 0:n],
                              reads=[st.r], writes=[dres_of[id(dst)]])

                def tm_group(c0, dst, name):
                    wt = loadw(c0)
                    for (t0, n, s) in TILES:
                        st = nstg()
                        for sub in range(n // 128):
                            pp = nps()
                            P.group("pe", [lambda e, k=k, pp=pp, sub=sub, t0=t0: e.matmul(
                                pp[:, :], XNh[0][:, k, t0 + sub * 128:t0 + (sub + 1) * 128], wt[:, k, :],
                                start=(k == 0), stop=(k == 7)) for k in range(8)],
                                reads=[wt.r, XNh[0].r], writes=[pp.r])
                            P.op("act", lambda e, st=st, sub=sub, pp=pp: e.activation(
                                st[:, sub, :], pp[:, :], AF.Identity), reads=[pp.r], writes=[st.r])
                        P.dma("act", dst[t0:t0 + n, :].rearrange("(s p) c -> p s c", p=128), st[:, 0:n // 128, :],
                              reads=[st.r], writes=[dres[name]])

                dres_of = {id(QT): dres["QT"], id(KT): dres["KT"], id(GT): dres["GT"]}
                pgen = pool_gen(es, l)

                def pstep():
                    next(pgen, None)
                fm_group(0, QT, AF.Identity, scale=0.125)
                pstep()
                fm_group(512, KT, AF.Identity)
                pstep()
                tm_group(1024, VV, "VV")
                pstep()
                tm_group(2048, RV, "RV")
                pstep()
                fm_group(2560, GT, AF.Silu)
                pstep()
                for b in range(3):
                    for hf in range(2):
                        dst = SG[b, hf * 512:(hf + 1) * 512, :]
                        dres_of[id(dst)] = dres["SG"]
                        fm_group(3584 + b * 1024 + hf * 512, dst, AF.Sigmoid)
                        pstep()

                wt = loadw(1536)
                wvw = wt[:].rearrange("p k (b two i) -> p (k b) two i", two=2, i=16)
                wvr = wrot[:].rearrange("p k (b two i) -> p (k b) two i", two=2, i=16)
                P.op("dve", lambda e: e.tensor_copy(wvr[:, :, 0, :], wvw[:, :, 1, :]), reads=[wt.r], writes=[wrot.r])
                P.op("dve", lambda e: e.tensor_copy(wvr[:, :, 1, :], wvw[:, :, 0, :]), reads=[wt.r], writes=[wrot.r])
                for (t0, n, s) in TILES:
                    if s == 0:
                        P.dma("sp", rC[:], ropeC_in[:, t0:t0 + n], writes=[rC.r])
                        P.dma("sp", rS[:], ropeS_in[:, t0:t0 + n], writes=[rS.r])
                    st = nstg()
                    for c in range(4):
                        ksc = 1.0 if c < 2 else 0.125
                        pp = nps()
                        fm_mm(pp, wt, c, t0, n)
                        if s == 0:
                            pr = nps()
                            fm_mm(pr, wrot, c, t0, n)
                            a1 = r1[c % 2]
                            a2 = r2[c % 2]
                            P.op("dve", lambda e, a1=a1, pp=pp, ksc=ksc: e.scalar_tensor_tensor(
                                a1[:], pp[:], ksc, rC[:], op0=ALU.mult, op1=ALU.mult),
                                reads=[pp.r, rC.r], writes=[a1.r])
                            P.op("dve", lambda e, a2=a2, pr=pr, ksc=ksc: e.scalar_tensor_tensor(
                                a2[:], pr[:], ksc, rS[:], op0=ALU.mult, op1=ALU.mult),
                                reads=[pr.r, rS.r], writes=[a2.r])
                            P.op("dve", lambda e, st=st, c=c, a1=a1, a2=a2: e.tensor_tensor(
                                st[:, c, :], a1[:], a2[:], op=ALU.add), reads=[a1.r, a2.r], writes=[st.r])
                        else:
                            P.op("act", lambda e, st=st, c=c, pp=pp, n=n, ksc=ksc: e.activation(
                                st[:, c, 0:n], pp[:, 0:n], AF.Identity, scale=ksc), reads=[pp.r], writes=[st.r])
                    P.dma("act", RQT[:, t0:t0 + n].rearrange("(c p) t -> p c t", p=128), st[:, 0:2, 0:n],
                          reads=[st.r], writes=[dres["RQT"]])
                    P.dma("act", RKT[:, t0:t0 + n].rearrange("(c p) t -> p c t", p=128), st[:, 2:4, 0:n],
                          reads=[st.r], writes=[dres["RKT"]])
                    nsub = n // 128
                    for sub in range(nsub):
                        for c in range(2):
                            P.op("pe", lambda e, st=st, sub=sub, c=c: e.transpose(
                                psb[:, (sub * 2 + c) * 128:(sub * 2 + c + 1) * 128],
                                st[:, 2 + c, sub * 128:(sub + 1) * 128], identb[:]),
                                reads=[st.r, identb.r], writes=[psb.r])
                    for sub in range(nsub):
                        for d in range(2):
                            P.op("dve", lambda e, sub=sub, d=d: e.tensor_tensor(
                                kzs[:, sub, :].rearrange("p (h d k) -> p h d k", h=4, d=2)[:, :, d, :],
                                psb[:, sub * 256:(sub + 1) * 256].rearrange("p (h k) -> p h k", h=4),
                                ZT[:, :, d, :], op=ALU.mult), reads=[psb.r, ZT.r], writes=[kzs.r])
                    P.dma("act", KZ[t0:t0 + n, :].rearrange("(s p) c -> p s c", p=128), kzs[:, 0:nsub, :],
                          reads=[kzs.r], writes=[dres["KZ"]])
                P.flush()

        def pool_gen(es, l):
            if True:
                wt = sb(es, "plw", [128, 8, 512], BF16)
                pw = sb(es, "plpw", [128, 4, 128], BF16)
                U = sb(es, "plU", [128, TOK + 64], F32)
                A = sb(es, "plA", [128, TOK + 64], F32)
                B = sb(es, "plB", [128, TOK + 64], F32)
                rce = sb(es, "plrce", [128, 4, 4, 8], F32)
                pl = sb(es, "plpl", [128, TOK], BF16)
                st = [sb(es, "plst%d" % i, [128, 512], BF16) for i in range(2)]
                P.dma("pool", wt[:], w_in[l].rearrange("(k p) n -> p k n", p=128)[:, :, 3072:3584], writes=[wt.r])
                P.dma("pool", pw[:], pool_w[l].rearrange("g c d -> c g d"), writes=[pw.r])
                segs = [(16, SEQ, 0)] + ([(16 + SEQ + 32, CTX, SEQ)] if l == 0 else [])
                tiles = TILES if l == 0 else TILES[:8]
                for g in range(4):
                    w = (2, 4, 8, 16)[g]
                    P.dma("sp", rce[:, g, 0, :], prc_in[g:g + 1, 0:8].broadcast_to([128, 8]), writes=[rce.r])
                    P.dma("sp", rce[:, g, 1, :], prc_in[g:g + 1, SEQ - 8:SEQ].broadcast_to([128, 8]), writes=[rce.r])
                    P.dma("sp", rce[:, g, 2, :], prcc_in[g:g + 1, 0:8].broadcast_to([128, 8]), writes=[rce.r])
                    P.dma("sp", rce[:, g, 3, :], prcc_in[g:g + 1, CTX - 8:CTX].broadcast_to([128, 8]), writes=[rce.r])
                    if g == 0:
                        for buf in (U, A, B):
                            P.op("dve", lambda e, buf=buf: e.memset(buf[:], 0.0), writes=[buf.r])
                    for (t0, n, s) in tiles:
                        pp = nps()
                        P.group("pe", [lambda e, k=k, pp=pp, t0=t0, n=n, g=g: e.matmul(
                            pp[:, 0:n], wt[:, k, g * 128:(g + 1) * 128], XNh[0][:, k, t0:t0 + n],
                            start=(k == 0), stop=(k == 7)) for k in range(8)], reads=[wt.r, XNh[0].r], writes=[pp.r])
                        off = 16 + t0 if s == 0 else 16 + SEQ + 32
                        P.op("act", lambda e, pp=pp, off=off, n=n: e.activation(U[:, off:off + n], pp[:, 0:n], AF.Identity),
                             reads=[pp.r], writes=[U.r])
                    for si, (o, n, tb) in enumerate(segs):
                        lo, hi = o - 12, o + n + 12
                        P.op("dve", lambda e, lo=lo, hi=hi: e.tensor_tensor(
                            A[:, lo:hi], U[:, lo - 1:hi - 1], U[:, lo:hi], op=ALU.add), reads=[U.r], writes=[A.r])
                        cur, oth = A, B
                        if w >= 4:
                            lo, hi = o - 10, o + n + 10
                            P.op("dve", lambda e, lo=lo, hi=hi: e.tensor_tensor(
                                B[:, lo:hi], A[:, lo - 1:hi - 1], A[:, lo + 1:hi + 1], op=ALU.add),
                                reads=[A.r], writes=[B.r])
                            cur, oth = B, A
                        if w >= 8:
                            lo, hi = o - 6, o + n + 6
                            P.op("dve", lambda e, lo=lo, hi=hi: e.tensor_tensor(
                                A[:, lo:hi], B[:, lo - 2:hi - 2], B[:, lo + 2:hi + 2], op=ALU.add),
                                reads=[B.r], writes=[A.r])
                            cur, oth = A, B
                        if w >= 16:
                            lo, hi = o, o + n
                            P.op("dve", lambda e, lo=lo, hi=hi: e.tensor_tensor(
                                B[:, lo:hi], A[:, lo - 4:hi - 4], A[:, lo + 4:hi + 4], op=ALU.add),
                                reads=[A.r], writes=[B.r])
                            cur, oth = B, A
                        P.op("dve", lambda e, cur=cur, o=o, n=n, tb=tb, w=w: e.scalar_tensor_tensor(
                            pl[:, tb:tb + n], cur[:, o:o + n], 1.0 / w, U[:, o:o + n], op0=ALU.mult, op1=ALU.subtract),
                            reads=[cur.r, U.r], writes=[pl.r])
                        for ei, (eo, et) in enumerate(((o, tb), (o + n - 8, tb + n - 8))):
                            P.op("dve", lambda e, cur=cur, oth=oth, eo=eo, g=g, ee=si * 2 + ei: e.tensor_tensor(
                                oth[:, eo:eo + 8], cur[:, eo:eo + 8], rce[:, g, ee, :], op=ALU.mult),
                                reads=[cur.r, rce.r], writes=[oth.r])
                            P.op("dve", lambda e, oth=oth, eo=eo, et=et: e.tensor_tensor(
                                pl[:, et:et + 8], oth[:, eo:eo + 8], U[:, eo:eo + 8], op=ALU.subtract),
                                reads=[oth.r, U.r], writes=[pl.r])
                    yield
                    for ti, (t0, n, s) in enumerate(tiles):
                        pp = nps()
                        P.group("pe", [lambda e, pp=pp, t0=t0, n=n, g=g: e.matmul(
                            pp[:, 0:n], pw[:, g, :], pl[:, t0:t0 + n], start=True, stop=True)],
                            reads=[pw.r, pl.r], writes=[pp.r])
                        so = st[ti % 2]
                        P.op("act", lambda e, so=so, pp=pp, n=n, g=g: e.activation(
                            so[:, 0:n], pp[:, 0:n], AF.Identity, scale=vecs[:, l, 20 + g:21 + g]),
                            reads=[pp.r, vecs.r], writes=[so.r])
                        P.dma("act", PT[g * 128:(g + 1) * 128, t0:t0 + n], so[:, 0:n], reads=[so.r], writes=[dres["PT"]])

        def retention_phase(l):
            with contextlib.ExitStack() as es:
                ST = sb(es, "rST", [128, 34, 4, 128], BF16)
                Scur = sb(es, "rScur", [128, 4, 128], F32)
                stmp = sb(es, "rstmp", [128, 4, 128], F32)
                kzt = [sb(es, "rkz%d" % i, [128, 4, 512], BF16) for i in range(2)]
                rvt = [sb(es, "rrv%d" % i, [128, 4, 512], BF16) for i in range(2)]
                chunks_f = [32, 33] + list(range(32))
                chunks_b = [33, 32] + list(range(31, -1, -1))

                def tok_of(c):
                    return SEQ + (c - 32) * 128 if c >= 32 else c * 128

                lcnt = [0]

                def run_pass(chunks, lo, hi, Scur, stmp, kzt, rvt):
                    P.op("dve", lambda e: e.memset(Scur[lo:hi], 0.0), writes=[Scur.r])
                    loaded = {}
                    lc = 0
                    for idx, c in enumerate(chunks):
                        grp = c // 4 if c < 32 else 8
                        if grp not in loaded:
                            kz = kzt[lc % 2]
                            rv = rvt[lc % 2]
                            lc += 1
                            g0 = grp * 512 if grp < 8 else SEQ
                            ng = 4 if grp < 8 else 2
                            P.dma("sp", kz[:, 0:ng, :], KZ[g0:g0 + ng * 128, :].rearrange("(s p) c -> p s c", p=128),
                                  reads=[dres["KZ"]], writes=[kz.r])
                            P.dma("sp", rv[:, 0:ng, :], RV[g0:g0 + ng * 128, :].rearrange("(s p) c -> p s c", p=128),
                                  reads=[dres["RV"]], writes=[rv.r])
                            loaded = {grp: (kz, rv)}
                        kz, rv = loaded[grp]
                        ci = c % 4 if c < 32 else c - 32
                        pp = nps()
                        P.group("pe", [lambda e, h=h, pp=pp, kz=kz, rv=rv, ci=ci: e.matmul(
                            pp[:, h * 128:(h + 1) * 128], kz[:, ci, h * 128:(h + 1) * 128],
                            rv[:, ci, h * 128:(h + 1) * 128], start=True, stop=True) for h in range(4)],
                            reads=[kz.r, rv.r], writes=[pp.r])
                        P.op("act", lambda e, c=c: e.activation(ST[lo:hi, c], Scur[lo:hi], AF.Identity),
                             reads=[Scur.r], writes=[ST.r])
                        if c >= 32 and idx == 0:
                            P.op("dve", lambda e, pp=pp: e.tensor_copy(
                                Scur[lo:hi], pp[lo:hi, :].rearrange("p (h v) -> p h v", h=4)),
                                reads=[pp.r], writes=[Scur.r])
                        else:
                            P.op("dve", lambda e: e.tensor_tensor(stmp[lo:hi], Scur[lo:hi], DEC[lo:hi], op=ALU.mult),
                                 reads=[Scur.r, DEC.r], writes=[stmp.r])
                            P.op("dve", lambda e, pp=pp: e.tensor_tensor(
                                Scur[lo:hi], stmp[lo:hi], pp[lo:hi, :].rearrange("p (h v) -> p h v", h=4), op=ALU.add),
                                reads=[stmp.r, pp.r], writes=[Scur.r])
                        yield

                ScurB = sb(es, "rScurB", [128, 4, 128], F32)
                stmpB = sb(es, "rstmpB", [128, 4, 128], F32)
                kztB = [sb(es, "rkzB%d" % i, [128, 4, 512], BF16) for i in range(2)]
                rvtB = [sb(es, "rrvB%d" % i, [128, 4, 512], BF16) for i in range(2)]
                gf_ = run_pass(chunks_f, 0, 64, Scur, stmp, kzt, rvt)
                gb_ = run_pass(chunks_b, 64, 128, ScurB, stmpB, kztB, rvtB)
                for _ in range(len(chunks_f)):
                    next(gf_)
                    next(gb_)

                def two(name, shape, dt):
                    return [sb(es, "%s%d" % (name, i), shape, dt) for i in range(2)]
                QD = two("rQD", [128, 4, 512], BF16)
                KD = two("rKD", [64, 4, 512], BF16)
                RVt = two("rRVt", [128, 4, 512], BF16)
                GTt = two("rGT", [128, 4, 512], BF16)
                GG = two("rGG", [128, 4, 512], F32)
                rts = two("rrts", [128, 4, 512], BF16)
                ATt = two("rAT", [128, 4, 128], BF16)
                QX = two("rQX", [128, 4, 128], BF16)
                ob = two("rob", [128, 512], BF16)
                o2b = two("ro2b", [128, 512], BF16)
                mean = two("rmean", [128, 512], F32)
                msq = two("rmsq", [128, 512], F32)
                rstd = two("rrstd", [128, 512], F32)
                dd = two("rdd", [128, 512], F32)
                tiles = TILES if l == 0 else TILES[:8]
                clist = []
                for ti, (t0, n, s) in enumerate(tiles):
                    for ci in range(n // 128):
                        clist.append((ti, t0, n, s, ci))

                def load_tile(ti, t0, n, s):
                    b = ti % 2
                    nch = n // 128
                    for half in range(2):
                        P.dma("sp", QD[b][half * 64:(half + 1) * 64, :, 0:n],
                              RQT[:, t0:t0 + n].rearrange("(h d) t -> d h t", d=64),
                              reads=[dres["RQT"]], writes=[QD[b].r])
                    P.dma("sp", KD[b][:, :, 0:n], RKT[:, t0:t0 + n].rearrange("(h d) t -> d h t", d=64),
                          reads=[dres["RKT"]], writes=[KD[b].r])
                    P.dma("sp", RVt[b][:, 0:nch, :], RV[t0:t0 + n, :].rearrange("(s p) c -> p s c", p=128),
                          reads=[dres["RV"]], writes=[RVt[b].r])
                    P.dma("sp", GTt[b][:, :, 0:n], GT[:, t0:t0 + n].rearrange("(c p) t -> p c t", p=128),
                          reads=[dres["GT"]], writes=[GTt[b].r])
                    for h in range(4):
                        P.op("act", lambda e, h=h, n=n, b=b: e.activation(
                            GG[b][:, h, 0:n], GTt[b][:, h, 0:n], AF.Identity, scale=vecs[:, l, 16 + h:17 + h]),
                            reads=[GTt[b].r, vecs.r], writes=[GG[b].r])

                def stage_a(idx):
                    ti, t0, n, s, ci = clist[idx]
                    if ci == 0:
                        load_tile(ti, t0, n, s)
                    b = ti % 2
                    q = idx % 2
                    o = ci * 128
                    c = (t0 // 128 + ci) if s == 0 else 32 + ci
                    pI = ps[0]
                    pO = ps[1 + q]
                    pM = ps[3 + q]
                    pQ = ps[5 + q]
                    P.group("pe", [lambda e, h=h, o=o, b=b: e.matmul(
                        pI[:, h * 128:(h + 1) * 128], KD[b][0:64, h, o:o + 128], QD[b][0:64, h, o:o + 128],
                        start=True, stop=True) for h in range(4)], reads=[KD[b].r, QD[b].r], writes=[pI.r])
                    P.op("dve", lambda e, q=q: e.tensor_tensor(
                        ATt[q][:], pI[:].rearrange("p (h i) -> p h i", h=4), DT[:], op=ALU.mult),
                        reads=[pI.r, DT.r], writes=[ATt[q].r])
                    P.op("pool", lambda e, o=o, b=b, q=q: e.tensor_tensor(QX[q][:], QD[b][:, :, o:o + 128], XI[:], op=ALU.mult),
                         reads=[QD[b].r, XI.r], writes=[QX[q].r])
                    fns = []
                    for h in range(4):
                        fns.append(lambda e, h=h, pO=pO, ci=ci, b=b, q=q: e.matmul(
                            pO[:, h * 128:(h + 1) * 128], RVt[b][:, ci, h * 128:(h + 1) * 128], ATt[q][:, h, :],
                            start=True, stop=False))
                        fns.append(lambda e, h=h, pO=pO, c=c, q=q: e.matmul(
                            pO[:, h * 128:(h + 1) * 128], ST[:, c, h, :], QX[q][:, h, :], start=False, stop=True))
                    P.group("pe", fns, reads=[RVt[b].r, ATt[q].r, ST.r, QX[q].r], writes=[pO.r])
                    P.op("act", lambda e, pO=pO, q=q: e.activation(ob[q][:], pO[:], AF.Identity), reads=[pO.r], writes=[ob[q].r])
                    P.op("act", lambda e, pO=pO, q=q: e.activation(o2b[q][:], pO[:], AF.Square), reads=[pO.r], writes=[o2b[q].r])
                    P.group("pe", [lambda e, pM=pM, q=q: e.matmul(pM[:], onesb[:], ob[q][:], start=True, stop=True)],
                            reads=[onesb.r, ob[q].r], writes=[pM.r])
                    P.group("pe", [lambda e, pQ=pQ, q=q: e.matmul(pQ[:], onesb[:], o2b[q][:], start=True, stop=True)],
                            reads=[onesb.r, o2b[q].r], writes=[pQ.r])

                def stage_b(idx):
                    ti, t0, n, s, ci = clist[idx]
                    b = ti % 2
                    q = idx % 2
                    o = ci * 128
                    pO = ps[1 + q]
                    pM = ps[3 + q]
                    pQ = ps[5 + q]
                    P.op("act", lambda e, q=q, pM=pM: e.activation(mean[q][:], pM[:], AF.Identity, scale=1.0 / 128),
                         reads=[pM.r], writes=[mean[q].r])
                    P.op("act", lambda e, q=q, pM=pM: e.activation(msq[q][:], pM[:], AF.Square, scale=1.0 / 128),
                         reads=[pM.r], writes=[msq[q].r])
                    P.op("dve", lambda e, q=q, pQ=pQ: e.scalar_tensor_tensor(
                        rstd[q][:], pQ[:], 1.0 / 128, msq[q][:], op0=ALU.mult, op1=ALU.subtract),
                        reads=[pQ.r, msq[q].r], writes=[rstd[q].r])
                    P.op("act", lambda e, q=q: e.activation(rstd[q][:], rstd[q][:], AF.Sqrt, bias=epsD[:, 1:2]),
                         reads=[rstd[q].r, epsD.r], writes=[rstd[q].r])
                    P.op("dve", lambda e, q=q: e.reciprocal(rstd[q][:], rstd[q][:]), reads=[rstd[q].r], writes=[rstd[q].r])
                    P.op("dve", lambda e, q=q, pO=pO: e.tensor_tensor(dd[q][:], pO[:], mean[q][:], op=ALU.subtract),
                         reads=[pO.r, mean[q].r], writes=[dd[q].r])
                    P.op("dve", lambda e, q=q: e.tensor_tensor(dd[q][:], dd[q][:], rstd[q][:], op=ALU.mult),
                         reads=[dd[q].r, rstd[q].r], writes=[dd[q].r])
                    P.op("dve", lambda e, o=o, b=b, q=q: e.tensor_tensor(
                        rts[b][:, :, o:o + 128], dd[q][:].rearrange("p (h i) -> p h i", h=4), GG[b][:, :, o:o + 128],
                        op=ALU.mult), reads=[dd[q].r, GG[b].r], writes=[rts[b].r])
                    if ci == n // 128 - 1:
                        P.dma("act", RT[:, t0:t0 + n].rearrange("(c p) t -> p c t", p=128), rts[b][:, :, 0:n],
                              reads=[rts[b].r], writes=[dres["RT"]])

                stage_a(0)
                for idx in range(len(clist)):
                    if idx + 1 < len(clist):
                        stage_a(idx + 1)
                    stage_b(idx)
                P.flush()

        def na_phase(l, pre=None):
            with contextlib.ExitStack() as es:
                TL = sb(es, "aTL", [128, 8, 22 * 64], BF16)
                BM = sb(es, "aBM", [80, 8, 512], BF16)
                QAs = [sb(es, "aQA%d" % i, [80, 8, 512], BF16) for i in range(2)]
                KAs = [sb(es, "aKA%d" % i, [80, 8, 1024], BF16) for i in range(2)]
                VAs = [sb(es, "aVA%d" % i, [128, 8, 8, 65], BF16) for i in range(2)]
                KC = sb(es, "aKC", [64, 8, CTX], BF16)
                VC = sb(es, "aVC", [128, 2, 8, 65], BF16)
                PTt = [sb(es, "aPT%d" % i, [128, 512], BF16) for i in range(4)]
                rden = [sb(es, "arden%d" % i, [65, 512], F32) for i in range(2)]
                bcs = [sb(es, "abcs%d" % i, [64, 512], F32) for i in range(2)]
                ast = [sb(es, "aast%d" % i, [64, 512], BF16) for i in range(2)]
                P.dma("pool", TL[:].rearrange("p h x -> p (h x)"), tl_in[l], writes=[TL.r])
                P.dma("pool", BM[64:80], bmask_in, writes=[BM.r])
                for KA in KAs:
                    for hh in range(8):
                        P.dma("pool", KA[64:80, hh, :], amask_in, writes=[KA.r])
                for VA in VAs:
                    P.op("pool", lambda e, VA=VA: e.memset(VA[:, :, :, 64:65], 1.0), writes=[VA.r])
                if pre is not None:
                    pre()
                P.op("pool", lambda e: e.memset(VC[:, :, :, 64:65], 1.0), writes=[VC.r])
                P.dma("sp", KC[:], KT[:, SEQ:TOK].rearrange("(h d) t -> d h t", d=64), reads=[dres["KT"]], writes=[KC.r])
                for cs in range(2):
                    P.dma("sp", VC[:, cs, :, 0:64],
                          VV[SEQ + cs * 128:SEQ + (cs + 1) * 128, :].rearrange("p (h d) -> p h d", d=64),
                          reads=[dres["VV"]], writes=[VC.r])
                tiles = TILES if l == 0 else TILES[:8]
                pcnt = 0
                for qt, (t0, n, s) in enumerate(tiles):
                    QA, KA, VA = QAs[qt % 2], KAs[qt % 2], VAs[qt % 2]
                    P.dma("sp", QA[0:64, :, 0:n], QT[:, t0:t0 + n].rearrange("(h d) t -> d h t", d=64),
                          reads=[dres["QT"]], writes=[QA.r])
                    slots = []
                    if s == 0:
                        R = qt * 8
                        P.op("pool", lambda e, qt=qt, QA=QA: e.tensor_copy(
                            QA[64:80, :, :], BM[64:80, qt:qt + 1, :].broadcast_to([16, 8, 512])),
                            reads=[BM.r], writes=[QA.r])
                        slots = [sp_ for sp_ in range(8) if 0 <= R - 4 + 2 * sp_ <= 62]
                        s_lo, s_hi = slots[0], slots[-1] + 1
                        k0 = (R - 4 + 2 * s_lo) * 64
                        nk = (s_hi - s_lo) * 128
                        P.dma("sp", KA[0:64, :, s_lo * 128:s_hi * 128],
                              KT[:, k0:k0 + nk].rearrange("(h d) t -> d h t", d=64), reads=[dres["KT"]], writes=[KA.r])
                        for sp_ in range(s_lo, s_hi):
                            kk0 = k0 + (sp_ - s_lo) * 128
                            P.dma("sp", VA[:, sp_, :, 0:64],
                                  VV[kk0:kk0 + 128, :].rearrange("p (h d) -> p h d", d=64),
                                  reads=[dres["VV"]], writes=[VA.r])
                    items = []
                    for h in range(8):
                        its = [("w", sp_) for sp_ in slots] + [("c", 0), ("c", 1)]
                        for ii, (kind, sp_) in enumerate(its):
                            items.append((h, kind, sp_, ii, len(its) - 1))

                    def emit_s(j):
                        h, kind, sp_, ii, last = items[j]
                        pS = ps[2 + j % 4]
                        if kind == "w":
                            w0 = 14 - 2 * sp_
                            P.group("pe", [
                                lambda e, pS=pS, h=h, sp_=sp_, n=n, KA=KA, QA=QA: e.matmul(
                                    pS[:, 0:n], KA[0:80, h, sp_ * 128:(sp_ + 1) * 128], QA[0:80, h, 0:n],
                                    start=True, stop=False),
                                lambda e, pS=pS, h=h, w0=w0, n=n: e.matmul(
                                    pS[:, 0:n], identb[:], TL[:, h, w0 * 64:w0 * 64 + n], start=False, stop=True)],
                                reads=[KA.r, QA.r, identb.r, TL.r], writes=[pS.r])
                        else:
                            P.group("pe", [lambda e, pS=pS, h=h, sp_=sp_, n=n, QA=QA: e.matmul(
                                pS[:, 0:n], KC[0:64, h, sp_ * 128:(sp_ + 1) * 128], QA[0:64, h, 0:n],
                                start=True, stop=True)], reads=[KC.r, QA.r], writes=[pS.r])

                    def emit_rest(j):
                        h, kind, sp_, ii, last = items[j]
                        pS = ps[2 + j % 4]
                        pO = ps[h % 2]
                        pt_ = PTt[j % 4]
                        P.op("act", lambda e, pt_=pt_, pS=pS, n=n: e.activation(pt_[:, 0:n], pS[:, 0:n], AF.Exp),
                             reads=[pS.r], writes=[pt_.r])
                        lhs = (lambda sp_=sp_, h=h, VA=VA: VA[:, sp_, h, :]) if kind == "w" else \
                            (lambda sp_=sp_, h=h: VC[:, sp_, h, :])
                        vres = VA.r if kind == "w" else VC.r
                        P.group("pe", [lambda e, pO=pO, lhs=lhs, pt_=pt_, n=n, ii=ii, last=last: e.matmul(
                            pO[0:65, 0:n], lhs(), pt_[:, 0:n], start=(ii == 0), stop=(ii == last))],
                            reads=[vres, pt_.r], writes=[pO.r])
                        if ii == last:
                            fin_q.append((j + 2, lambda h=h, pO=pO: finalize(h, pO)))

                    def finalize(h, pO):
                        if True:
                            rd = rden[h % 2]
                            bc = bcs[h % 2]
                            P.op("dve", lambda e, pO=pO, n=n, rd=rd: e.reciprocal(rd[64:65, 0:n], pO[64:65, 0:n]),
                                 reads=[pO.r], writes=[rd.r])
                            pB = ps[6]
                            P.group("pe", [lambda e, pB=pB, n=n, rd=rd: e.matmul(
                                pB[0:64, 0:n], ones1[64:65, 0:64], rd[64:65, 0:n], start=True, stop=True)],
                                reads=[ones1.r, rd.r], writes=[pB.r])
                            P.op("act", lambda e, pB=pB, n=n, bc=bc: e.activation(bc[:, 0:n], pB[0:64, 0:n], AF.Identity),
                                 reads=[pB.r], writes=[bc.r])
                            a_ = ast[h % 2]
                            P.op("dve", lambda e, a_=a_, pO=pO, n=n, bc=bc: e.tensor_tensor(
                                a_[:, 0:n], pO[0:64, 0:n], bc[:, 0:n], op=ALU.mult), reads=[pO.r, bc.r], writes=[a_.r])
                            P.dma("act", AT[h * 64:(h + 1) * 64, t0:t0 + n], a_[:, 0:n], reads=[a_.r], writes=[dres["AT"]])

                    LOOK = 3
                    fin_q = []
                    for j in range(min(LOOK, len(items))):
                        emit_s(j)
                    for j in range(len(items)):
                        if j + LOOK < len(items):
                            emit_s(j + LOOK)
                        emit_rest(j)
                        while fin_q and fin_q[0][0] <= j:
                            fin_q.pop(0)[1]()
                    while fin_q:
                        fin_q.pop(0)[1]()
                P.flush()

        def merge_weights(es, l):
            WB = [sb(es, "mWB%d" % i, [128, 4, D], BF16) for i in range(3)]
            WO = sb(es, "mWO", [128, 8, D], BF16)

            def load():
                for i, wsrc in enumerate((w_ba, w_bb, w_bc)):
                    P.dma("pool", WB[i][:], wsrc[l].rearrange("(k p) n -> p k n", p=128), writes=[WB[i].r])
                P.dma("pool", WO[:], w_out[l].rearrange("(k p) n -> p k n", p=128), writes=[WO.r])
            return WB, WO, load

        def merge_phase(l, WB, WO):
            with contextlib.ExitStack() as es:
                bt = [sb(es, "mbt%d" % i, [128, 4, 512], BF16) for i in range(3)]
                sg = [sb(es, "msg%d" % i, [128, 8, 512], BF16) for i in range(3)]
                ht = sb(es, "mht", [128, 8, 512], F32)
                ho = sb(es, "mho", [128, 8, 512], F32)
                mT = sb(es, "mmT", [128, 8, 512], BF16)
                tt = [sb(es, "mtt%d" % i, [128, 512], F32) for i in range(3)]
                tiles = TILES if l == 0 else TILES[:8]
                srcs = (AT, RT, PT)
                names = ("AT", "RT", "PT")
                for (t0, n, s) in tiles:
                    for i in range(3):
                        P.dma("sp", bt[i][:, :, 0:n], srcs[i][:, t0:t0 + n].rearrange("(c p) t -> p c t", p=128),
                              reads=[dres[names[i]]], writes=[bt[i].r])
                        P.dma("sp", sg[i][:, :, 0:n], SG[i, :, t0:t0 + n].rearrange("(c p) t -> p c t", p=128),
                              reads=[dres["SG"]], writes=[sg[i].r])
                    P.dma("sp", ht[:, :, 0:n], hT[:, :, t0:t0 + n].rearrange("k p t -> p k t"),
                          reads=[dres["hT"]], writes=[ht.r])
                    for mo in range(8):
                        for i in range(3):
                            pp = nps()
                            P.group("pe", [lambda e, k=k, i=i, pp=pp, mo=mo, n=n: e.matmul(
                                pp[:, 0:n], WB[i][:, k, mo * 128:(mo + 1) * 128], bt[i][:, k, 0:n],
                                start=(k == 0), stop=(k == 3)) for k in range(4)],
                                reads=[WB[i].r, bt[i].r], writes=[pp.r])
                            P.op("dve", lambda e, i=i, pp=pp, mo=mo, n=n: e.tensor_tensor(
                                tt[i][:, 0:n], pp[:, 0:n], sg[i][:, mo, 0:n], op=ALU.mult),
                                reads=[pp.r, sg[i].r], writes=[tt[i].r])
                        P.op("pool", lambda e, n=n: e.tensor_tensor(tt[0][:, 0:n], tt[0][:, 0:n], tt[1][:, 0:n], op=ALU.add),
                             reads=[tt[0].r, tt[1].r], writes=[tt[0].r])
                        P.op("dve", lambda e, mo=mo, n=n: e.tensor_tensor(
                            mT[:, mo, 0:n], tt[0][:, 0:n], tt[2][:, 0:n], op=ALU.add),
                            reads=[tt[0].r, tt[2].r], writes=[mT.r])
                    for mo in range(8):
                        pp = nps()
                        P.group("pe", [lambda e, k=k, pp=pp, mo=mo, n=n: e.matmul(
                            pp[:, 0:n], WO[:, k, mo * 128:(mo + 1) * 128], mT[:, k, 0:n],
                            start=(k == 0), stop=(k == 7)) for k in range(8)], reads=[WO.r, mT.r], writes=[pp.r])
                        P.op("dve", lambda e, pp=pp, mo=mo, n=n, s=s: e.scalar_tensor_tensor(
                            ho[:, mo, 0:n], pp[:, 0:n], mod(l, 2, mo, s), ht[:, mo, 0:n], op0=ALU.mult, op1=ALU.add),
                            reads=[pp.r, mods.r, ht.r], writes=[ho.r])
                    P.dma("act", hT[:, :, t0:t0 + n].rearrange("k p t -> p k t"), ho[:, :, 0:n],
                          reads=[ho.r], writes=[dres["hT"]])
                P.flush()

        def ffn1_phase(l, pre=None):
            with contextlib.ExitStack() as es:
                wg = [sb(es, "fwg%d" % i, [128, 8, 256], BF16) for i in range(2)]
                wu = [sb(es, "fwu%d" % i, [128, 8, 256], BF16) for i in range(2)]
                sgt = [sb(es, "fsg%d" % i, [128, 512], F32) for i in range(2)]
                act = [sb(es, "fac%d" % i, [128, 512], BF16) for i in range(3)]
                tiles = TILES if l == 0 else TILES[:8]
                gv = w_fg[l].rearrange("(k p) n -> p k n", p=128)
                uv = w_fu[l].rearrange("(k p) n -> p k n", p=128)
                cnt = 0
                for jg in range(NHC // 2):
                    g_ = wg[jg % 2]
                    u_ = wu[jg % 2]
                    P.dma("pool", g_[:], gv[:, :, jg * 256:(jg + 1) * 256], writes=[g_.r])
                    P.dma("pool", u_[:], uv[:, :, jg * 256:(jg + 1) * 256], writes=[u_.r])
                    if jg == 1 and pre is not None:
                        pre()
                    for jj in range(2):
                        j = jg * 2 + jj
                        for (t0, n, s) in tiles:
                            pg = nps()
                            P.group("pe", [lambda e, k=k, pg=pg, g_=g_, jj=jj, t0=t0, n=n: e.matmul(
                                pg[:, 0:n], g_[:, k, jj * 128:(jj + 1) * 128], XNh[0][:, k, t0:t0 + n],
                                start=(k == 0), stop=(k == 7)) for k in range(8)], reads=[g_.r, XNh[0].r], writes=[pg.r])
                            pu = nps()
                            P.group("pe", [lambda e, k=k, pu=pu, u_=u_, jj=jj, t0=t0, n=n: e.matmul(
                                pu[:, 0:n], u_[:, k, jj * 128:(jj + 1) * 128], XNh[0][:, k, t0:t0 + n],
                                start=(k == 0), stop=(k == 7)) for k in range(8)], reads=[u_.r, XNh[0].r], writes=[pu.r])
                            sg_ = sgt[cnt % 2]
                            ac = act[cnt % 3]
                            cnt += 1
                            P.op("act", lambda e, sg_=sg_, pg=pg, n=n: e.activation(sg_[:, 0:n], pg[:, 0:n], AF.Silu),
                                 reads=[pg.r], writes=[sg_.r])
                            P.op("dve", lambda e, ac=ac, sg_=sg_, pu=pu, n=n: e.tensor_tensor(
                                ac[:, 0:n], pu[:, 0:n], sg_[:, 0:n], op=ALU.mult), reads=[pu.r, sg_.r], writes=[ac.r])
                            P.dma("act", ACTT[j, :, t0:t0 + n], ac[:, 0:n], reads=[ac.r], writes=[dres["ACTT"]])
                P.flush()

        def ffn2_phase(l, WD):
            with contextlib.ExitStack() as es:
                at = [sb(es, "fat%d" % i, [128, NHC, 512], BF16) for i in range(2)]
                ht = [sb(es, "fht%d" % i, [128, 8, 512], F32) for i in range(2)]
                ho = sb(es, "fho", [128, 8, 512], F32)
                tiles = TILES if l == 0 else TILES[:8]
                for ti, (t0, n, s) in enumerate(tiles):
                    a_ = at[ti % 2]
                    h_ = ht[ti % 2]
                    P.dma("sp", a_[:, :, 0:n], ACTT[:, :, t0:t0 + n].rearrange("j p t -> p j t"),
                          reads=[dres["ACTT"]], writes=[a_.r])
                    P.dma("sp", h_[:, :, 0:n], hT[:, :, t0:t0 + n].rearrange("k p t -> p k t"),
                          reads=[dres["hT"]], writes=[h_.r])
                    for mo in range(8):
                        pp = nps()
                        P.group("pe", [lambda e, j=j, pp=pp, mo=mo, a_=a_, n=n: e.matmul(
                            pp[:, 0:n], WD[:, j, mo * 128:(mo + 1) * 128], a_[:, j, 0:n],
                            start=(j == 0), stop=(j == NHC - 1)) for j in range(NHC)], reads=[WD.r, a_.r], writes=[pp.r])
                        P.op("dve", lambda e, pp=pp, mo=mo, n=n, s=s, h_=h_: e.scalar_tensor_tensor(
                            ho[:, mo, 0:n], pp[:, 0:n], mod(l, 5, mo, s), h_[:, mo, 0:n], op0=ALU.mult, op1=ALU.add),
                            reads=[pp.r, mods.r, h_.r], writes=[ho.r])
                    P.dma("act", hT[:, :, t0:t0 + n].rearrange("k p t -> p k t"), ho[:, :, 0:n],
                          reads=[ho.r], writes=[dres["hT"]])
                P.flush()

        def out_phase():
            with contextlib.ExitStack() as es:
                hb = [sb(es, "oh%d" % i, [128, 8, 512], F32) for i in range(2)]
                sqs = [sb(es, "osq%d" % i, [128, 8, 512], BF16) for i in range(2)]
                rss = [sb(es, "ors%d" % i, [128, 512], F32) for i in range(2)]
                yys = [sb(es, "oyy%d" % i, [128, 8, 512], F32) for i in range(2)]
                ot = [sb(es, "oot%d" % i, [128, D], F32) for i in range(2)]
                cnt = 0
                def stage_a(ti):
                    t0, n, s = TILES[ti]
                    ht, sq, rs = hb[ti % 2], sqs[ti % 2], rss[ti % 2]
                    P.dma("sp", ht[:], hT[:, :, t0:t0 + n].rearrange("k p t -> p k t"), reads=[dres["hT"]], writes=[ht.r])
                    P.op("act", lambda e: e.activation(sq[:], ht[:], AF.Square), reads=[ht.r], writes=[sq.r])
                    pp = nps()
                    P.group("pe", [lambda e, k=k: e.matmul(pp[:], onesb[:], sq[:, k, :], start=(k == 0), stop=(k == 7))
                                   for k in range(8)], reads=[onesb.r, sq.r], writes=[pp.r])
                    P.op("act", lambda e: e.activation(rs[:], pp[:], AF.Sqrt, bias=epsD[:, 0:1]),
                         reads=[pp.r, epsD.r], writes=[rs.r])
                    P.op("dve", lambda e: e.reciprocal(rs[:], rs[:]), reads=[rs.r], writes=[rs.r])

                def stage_b(ti):
                    t0, n, s = TILES[ti]
                    ht, rs, yy = hb[ti % 2], rss[ti % 2], yys[ti % 2]
                    for kc in range(8):
                        P.op("dve", lambda e, kc=kc: e.scalar_tensor_tensor(
                            yy[:, kc, :], ht[:, kc, :], gfin[:, kc:kc + 1], rs[:], op0=ALU.mult, op1=ALU.mult),
                            reads=[ht.r, gfin.r, rs.r], writes=[yy.r])
                    for sub in range(4):
                        o_ = ot[(ti * 4 + sub) % 2]
                        for half in range(2):
                            pt = nps()
                            for kk in range(4):
                                kc = half * 4 + kk
                                P.op("pe", lambda e, pt=pt, kk=kk, kc=kc, sub=sub: e.transpose(
                                    pt[:, kk * 128:(kk + 1) * 128], yy[:, kc, sub * 128:(sub + 1) * 128], ident[:]),
                                    reads=[yy.r, ident.r], writes=[pt.r])
                            if half == 0:
                                P.op("dve", lambda e, o_=o_, pt=pt: e.tensor_copy(o_[:, 0:512], pt[:]),
                                     reads=[pt.r], writes=[o_.r])
                            else:
                                P.op("act", lambda e, o_=o_, pt=pt: e.activation(o_[:, 512:1024], pt[:], AF.Identity),
                                     reads=[pt.r], writes=[o_.r])
                        P.dma("pool", out_ap[t0 + sub * 128:t0 + (sub + 1) * 128, :], o_[:],
                              reads=[o_.r], writes=[r_out], is_output=True)

                stage_a(0)
                for ti in range(8):
                    if ti + 1 < 8:
                        stage_a(ti + 1)
                    stage_b(ti)
                P.flush(final=True)

        def with_xn(fns):
            with contextlib.ExitStack() as xs:
                XNh[0] = sb(xs, "XN", [128, 8, TOK], BF16)
                for f in fns:
                    f()
                XNh[0] = None

        for l in range(depth):
            last = (l == DEPTH - 1)
            retention_tables(l)
            with_xn([lambda: norm_phase(l, 0, TILES, dump=("XNd" in debug_outs)),
                     lambda: proj_phase(l)])
            retention_phase(l)
            with contextlib.ExitStack() as ms:
                WB, WO, mload = merge_weights(ms, l)
                na_phase(l, pre=mload)
                merge_phase(l, WB, WO)
            with contextlib.ExitStack() as fs:
                WD = sb(fs, "fWD", [128, NHC, D], BF16)

                def wdload():
                    P.dma("pool", WD[:], w_fd[l].rearrange("(j p) n -> p j n", p=128), writes=[WD.r])
                with_xn([lambda: norm_phase(l, 1, TILES[:8] if last else TILES),
                         lambda: ffn1_phase(l, pre=wdload)])
                ffn2_phase(l, WD)
        out_phase()
    return nc


def _const_tables():
    f32 = np.float32
    ident = np.eye(128, dtype=f32)
    amask = np.zeros((16, 1024), f32)
    for i in range(16):
        amask[i, i * 64:(i + 1) * 64] = 1.0
    bmask = np.full((16, 8, 512), NEGM, f32)
    for qt in range(8):
        R = qt * 8
        for i in range(16):
            kr = R - 4 + i
            for rho in range(8):
                r = R + rho
                r0 = min(max(r - 4, 0), 56)
                if 0 <= kr <= 63 and r0 <= kr < r0 + 8:
                    bmask[i, qt, rho * 64:(rho + 1) * 64] = 0.0
    t = np.arange(SEQ)
    row = (t // 64).astype(f32)
    col = (t % 64).astype(f32)
    inv = (10000.0 ** (-np.arange(16, dtype=f32) / 16)).astype(f32)
    C = np.zeros((64, SEQ), f32)
    S = np.zeros((64, SEQ), f32)
    for d in range(64):
        pos = row if d < 32 else col
        i = d % 16
        ang = (pos * inv[i]).astype(f32)
        C[d] = np.cos(ang)
        sgn = -1.0 if (d % 32) < 16 else 1.0
        S[d] = sgn * np.sin(ang)
    ropeC = np.concatenate([C, C], 0)
    ropeS = np.concatenate([S, S], 0)
    j = np.arange(128)[:, None].astype(f32)
    i = np.arange(128)[None, :].astype(f32)
    EF = np.maximum(i - j, 0)
    MF = (i >= j).astype(f32)
    EB = np.maximum(j - i, 0)
    MB = (j > i).astype(f32)
    ZEf = np.repeat(127.0 - j, 64, 1)
    ZEb = np.repeat(j, 64, 1)
    XE = np.zeros((128, 128), f32)
    XE[0:64] = i + 1.0
    XE[64:128] = 128.0 - i
    rconst = np.concatenate([EF, MF, EB, MB, ZEf, ZEb, XE], 1).astype(f32)

    def rc(n):
        tt = np.arange(n)
        out = np.zeros((4, n), f32)
        for gi, w in enumerate((2, 4, 8, 16)):
            lo = np.clip(tt - w // 2, 0, n)
            hi = np.clip(tt - w // 2 + w, 0, n)
            out[gi] = 1.0 / (hi - lo).astype(f32)
        return out
    return dict(ident=ident, amask=amask, bmask=bmask, ropeC=ropeC, ropeS=ropeS, rconst=rconst,
                prc=rc(SEQ), prcc=rc(CTX))


def _tl_table(rpb):
    L = rpb.shape[0]
    a = np.arange(2)[:, None, None, None]
    kc = np.arange(64)[None, :, None, None]
    u = np.arange(22)[None, None, :, None]
    qc = np.arange(64)[None, None, None, :]
    dr = a + 10 - u + 0 * kc + 0 * qc
    wstart = np.clip(qc - 8, 0, 48)
    ok = (np.abs(dr) <= 7) & (kc >= wstart) & (kc < wstart + 16)
    dri = np.clip(dr + 7, 0, 14)
    dci = np.clip(kc - qc + 15 + 0 * dr, 0, 30)
    out = np.empty((L, 2, 64, 8, 22, 64), np.float32)
    for l in range(L):
        for h in range(8):
            g = rpb[l, h][dri, dci]
            out[l, :, :, h] = np.where(ok, g, np.float32(NEGM))
    return out.reshape(L, 128, 8 * 22 * 64)


_CONSTS = None


def make_in_maps(inp):
    global _CONSTS
    if _CONSTS is None:
        _CONSTS = _const_tables()
    f32 = np.float32
    g = {k: np.asarray(v, dtype=f32) for k, v in inp.items()}
    L = DEPTH

    def col(v):
        return np.ascontiguousarray(v.reshape(L, -1, 128).transpose(2, 0, 1))
    fin = np.broadcast_to(g["final_norm_g"][None], (L, D))
    vecs = np.concatenate([col(g["norm1_g"]), col(g["norm2_g"]), col(g["ret_gn_g"]), col(g["pool_scale"]),
                           col(np.ascontiguousarray(fin))], axis=2)
    logits = np.ascontiguousarray(np.broadcast_to(
        np.concatenate([g["ret_logit_f"], g["ret_logit_b"]], 1)[None], (128, L, 8)))
    shared = dict(w_ada=g["w_ada"], b_ada=g["b_ada"].reshape(L, 1, 6 * D), vecs=np.ascontiguousarray(vecs),
                  logits=logits, w_in=g["w_in"], pool_w=g["pool_w"], w_branch_a=g["w_branch_a"],
                  w_branch_b=g["w_branch_b"], w_branch_c=g["w_branch_c"], w_out=g["w_out"],
                  w_ffn_gate=g["w_ffn_gate"], w_ffn_up=g["w_ffn_up"], w_ffn_down=g["w_ffn_down"],
                  tl=_tl_table(g["na_rpb"]))
    shared.update(_CONSTS)
    maps = []
    cc = g["c_ctx"].reshape(128, 8)
    for b in range(g["x"].shape[0]):
        m = dict(shared)
        m["x"] = np.ascontiguousarray(g["x"][b])
        m["ctx"] = np.ascontiguousarray(g["ctx"][b])
        m["cvec"] = np.ascontiguousarray(np.stack([g["c"][b].reshape(128, 8), cc], axis=2))
        maps.append(m)
    return maps


_NC = None


def kernel(**inputs):
    global _NC
    maps = make_in_maps(inputs)
    if _NC is None:
        _NC = build_program()
    res = run_bass_kernel_spmd(_NC, maps, core_ids=list(range(len(maps))))
    return np.stack([np.asarray(r["out"], dtype=np.float32) for r in res.results], axis=0)
```
